# Optimizing a Trainium2 kernel written in Bass

```python
import jax
import jax.numpy as jnp
from jax import lax
import numpy as np

D_MODEL = 1024
BATCH = 16
SEQ = 256
DEPTH = 4
DEC_BATCH = 2
DEC_SEQ = 1024
PAST_LEN = 512

GRID_W = 64
N_AB = (DEPTH + 1) // 2
N_C = DEPTH // 2
D_FF = 4 * D_MODEL
EPS = 1e-6
ROPE_BASE = 10000.0
Q_BLOCK = 128
H_A = 4
DK_A = D_MODEL // 16
DV_A = D_MODEL // 8
GK_RANK = 16
GATE_NORM = 16.0
GLA_CHUNK = 64
H_B = 8
KV_B = 2
G_B = H_B // KV_B
D_B = 64
WINDOW = 128
WIN_BLOCK = 128
AB_SPLITS = (H_A * DK_A, H_A * DK_A, H_A * DV_A, H_A * DV_A, 2 * GK_RANK, H_B * D_B, KV_B * D_B, KV_B * D_B)
AB_IN = 2 * H_A * DK_A + 2 * H_A * DV_A + 2 * GK_RANK + H_B * D_B + 2 * KV_B * D_B
MIX_AB = H_A * DV_A + H_B * D_B
H_C = 16
NOPE_C = 64
ROPE_C = 32
V_C = 64
Q_LORA = 384
KV_LORA = 256
DOWN_C = Q_LORA + KV_LORA + ROPE_C

kernel_name = 'bidir_hybrid_gla_swa_mla_dit_step'


def rmsnorm(x, g):
    xf = x.astype(jnp.float32)
    y = xf * lax.rsqrt(jnp.mean(xf * xf, axis=-1, keepdims=True) + EPS)
    return (y * g.astype(jnp.float32)).astype(x.dtype)


def split_last(x, sizes):
    out, o = [], 0
    for s in sizes:
        out.append(x[..., o:o + s])
        o += s
    return out


def rope_2d(x, rows, cols):
    hd = x.shape[-1]
    nf = hd // 4
    inv = ROPE_BASE ** (-jnp.arange(nf, dtype=jnp.float32) / nf)
    ang = jnp.stack([rows[:, None] * inv, cols[:, None] * inv], axis=1)
    ang = ang.reshape((ang.shape[0],) + (1,) * (x.ndim - 3) + (2, nf))
    cos, sin = jnp.cos(ang), jnp.sin(ang)
    xr = x.astype(jnp.float32).reshape(x.shape[:-1] + (2, 2, nf))
    x1, x2 = xr[..., 0, :], xr[..., 1, :]
    out = jnp.stack([x1 * cos - x2 * sin, x1 * sin + x2 * cos], axis=-2)
    return out.reshape(x.shape).astype(x.dtype)


def adaln(cvec, w_mod, b_mod):
    m = jax.nn.silu(cvec) @ w_mod + b_mod
    m = m.reshape(cvec.shape[0], 1, 6, D_MODEL)
    return [m[:, :, i] for i in range(6)]


def pre(x, g, shift, scale):
    return rmsnorm(x, g) * (1 + scale) + shift


def post(x, out, g, gate):
    return x + gate * rmsnorm(out, g)


def sq_relu_mlp(h, w1, w2):
    return jnp.square(jax.nn.relu(h @ w1)) @ w2


def gla_chunked(q, k, v, g, s0):
    B, L, H, _ = q.shape
    dv = v.shape[-1]
    nc = L // GLA_CHUNK

    def chunks(a):
        return a.astype(jnp.float32).reshape(B, nc, GLA_CHUNK, H, a.shape[-1]).transpose(1, 0, 3, 2, 4)

    causal = jnp.tril(jnp.ones((GLA_CHUNK, GLA_CHUNK), dtype=bool))

    def step(s, inp):
        qc, kc, vc, gc = inp
        b = jnp.cumsum(gc, axis=2)
        o_inter = jnp.einsum('bhtd,bhde->bhte', qc * jnp.exp(b), s)
        diff = b[:, :, :, None, :] - b[:, :, None, :, :]
        decay = jnp.exp(jnp.where(causal[:, :, None], diff, -jnp.inf))
        att = jnp.einsum('bhtsd,bhsd->bhts', qc[:, :, :, None, :] * decay, kc)
        o_intra = jnp.einsum('bhts,bhse->bhte', att, vc)
        b_last = b[:, :, -1:, :]
        s_new = jnp.exp(b_last[:, :, 0, :])[..., None] * s + jnp.einsum('bhsd,bhse->bhde', kc * jnp.exp(b_last - b), vc)
        return s_new, o_inter + o_intra

    s_fin, o = lax.scan(step, s0.astype(jnp.float32), (chunks(q), chunks(k), chunks(v), chunks(g)))
    o = o.transpose(1, 0, 3, 2, 4).reshape(B, L, H, dv)
    return o, s_fin


def dense_attn(q, k, v, scale, sink=None):
    B, Lq, KV, G, d = q.shape
    nb = Lq // Q_BLOCK
    qb = q.reshape(B, nb, Q_BLOCK, KV, G, d).transpose(1, 0, 2, 3, 4, 5)

    def block(qblk):
        s = jnp.einsum('bqkgd,bskd->bkgqs', qblk, k).astype(jnp.float32) * scale
        if sink is not None:
            sk = jnp.broadcast_to(sink.astype(jnp.float32).reshape(1, KV, G, 1, 1), s.shape[:-1] + (1,))
            p = jax.nn.softmax(jnp.concatenate([sk, s], axis=-1), axis=-1)[..., 1:]
        else:
            p = jax.nn.softmax(s, axis=-1)
        return jnp.einsum('bkgqs,bskd->bqkgd', p.astype(v.dtype), v)

    o = lax.map(block, qb)
    return o.transpose(1, 0, 2, 3, 4, 5).reshape(B, Lq, KV, G, v.shape[-1])


def banded_attn(q, k, v, k_ctx, v_ctx, scale, sink):
    B, L, KV, G, d = q.shape
    W = WIN_BLOCK
    nb = L // W
    Lc = k_ctx.shape[1]
    qb = q.reshape(B, nb, W, KV, G, d)

    def band(a):
        ap = jnp.pad(a, ((0, 0), (W, W), (0, 0), (0, 0))).reshape(B, nb + 2, W, KV, a.shape[-1])
        return jnp.concatenate([ap[:, :-2], ap[:, 1:-1], ap[:, 2:]], axis=2)

    kb, vb = band(k), band(v)
    qi = jnp.arange(L).reshape(nb, W)
    ki = jnp.arange(nb)[:, None] * W - W + jnp.arange(3 * W)[None, :]
    mask = (ki[:, None, :] >= 0) & (ki[:, None, :] < L) & (jnp.abs(qi[:, :, None] - ki[:, None, :]) <= WINDOW)
    s_loc = jnp.einsum('bnqkgd,bnskd->bnkgqs', qb, kb).astype(jnp.float32) * scale
    s_loc = jnp.where(mask[None, :, None, None], s_loc, -jnp.inf)
    s_ctx = jnp.einsum('bnqkgd,bckd->bnkgqc', qb, k_ctx).astype(jnp.float32) * scale
    s_sink = jnp.broadcast_to(sink.astype(jnp.float32).reshape(1, 1, KV, G, 1, 1), s_ctx.shape[:-1] + (1,))
    p = jax.nn.softmax(jnp.concatenate([s_sink, s_ctx, s_loc], axis=-1), axis=-1)
    p_ctx = p[..., 1:1 + Lc].astype(v.dtype)
    p_loc = p[..., 1 + Lc:].astype(v.dtype)
    o = jnp.einsum('bnkgqc,bckd->bnqkgd', p_ctx, v_ctx) + jnp.einsum('bnkgqs,bnskd->bnqkgd', p_loc, vb)
    return o.reshape(B, L, KV, G, v.shape[-1])


def mixer_ab(h, w_in, w_gk_f, b_gk_f, w_gk_b, b_gk_b, g_gla, sink, w_out, ctx=None, pos=None):
    B, L, _ = h.shape
    q_a, k_a, v_a, g_a, gk_lo, q_b, k_b, v_b = split_last(h @ w_in, AB_SPLITS)
    q_a = q_a.reshape(B, L, H_A, DK_A) * (DK_A ** -0.5)
    k_a = k_a.reshape(B, L, H_A, DK_A)
    v_a = v_a.reshape(B, L, H_A, DV_A)

    def log_decay(lo, w, b):
        return (jax.nn.log_sigmoid((lo @ w + b).astype(jnp.float32)) / GATE_NORM).reshape(B, L, H_A, DK_A)

    ld_f = log_decay(gk_lo[..., :GK_RANK], w_gk_f, b_gk_f)
    ld_b = log_decay(gk_lo[..., GK_RANK:], w_gk_b, b_gk_b)
    if ctx is None:
        s0_f = jnp.zeros((B, H_A, DK_A, DV_A), jnp.float32)
        s0_b = jnp.zeros((B, H_A, DK_A, DV_A), jnp.float32)
    else:
        s0_f, s0_b, k_ctx, v_ctx = ctx
    rev = lambda a: jnp.flip(a, axis=1)
    o_fwd, s_fwd = gla_chunked(q_a, k_a, v_a, ld_f, s0_f)
    o_bwd, s_bwd = gla_chunked(rev(q_a), rev(k_a), rev(v_a), rev(ld_b), s0_b)
    o_gla = (o_fwd + rev(o_bwd)).astype(h.dtype)
    o_gla = rmsnorm(o_gla, g_gla) * jax.nn.silu(g_a.reshape(B, L, H_A, DV_A))

    q_b = q_b.reshape(B, L, H_B, D_B)
    k_b = k_b.reshape(B, L, KV_B, D_B)
    v_b = v_b.reshape(B, L, KV_B, D_B)
    scale = D_B ** -0.5
    sink_g = sink.reshape(KV_B, G_B)
    if ctx is None:
        o_swa = dense_attn(q_b.reshape(B, L, KV_B, G_B, D_B), k_b, v_b, scale, sink_g)
        new = (s_fwd.astype(h.dtype), s_bwd.astype(h.dtype), k_b, v_b)
    else:
        rows, cols = pos
        q_b = rope_2d(q_b, rows, cols)
        k_b = rope_2d(k_b, rows, cols)
        o_swa = banded_attn(q_b.reshape(B, L, KV_B, G_B, D_B), k_b, v_b, k_ctx, v_ctx, scale, sink_g)
        new = None
    out = jnp.concatenate([o_gla.reshape(B, L, H_A * DV_A), o_swa.reshape(B, L, H_B * D_B)], axis=-1) @ w_out
    return out, new


def mla_expand(c_kv, k_rope, w_ukv):
    B, L, _ = c_kv.shape
    kv = (c_kv @ w_ukv).reshape(B, L, H_C, NOPE_C + V_C)
    k = jnp.concatenate([kv[..., :NOPE_C], jnp.broadcast_to(k_rope[:, :, None, :], (B, L, H_C, ROPE_C))], axis=-1)
    return k, kv[..., NOPE_C:]


def mixer_c(h, w_down, g_q, g_kv, w_uq, w_ukv, w_o, ctx=None, pos=None):
    B, L, _ = h.shape
    c_q, c_kv, k_rope = split_last(h @ w_down, (Q_LORA, KV_LORA, ROPE_C))
    q = (rmsnorm(c_q, g_q) @ w_uq).reshape(B, L, H_C, NOPE_C + ROPE_C)
    c_kv = rmsnorm(c_kv, g_kv)
    if ctx is None:
        k, v = mla_expand(c_kv, k_rope, w_ukv)
        new = (c_kv, k_rope)
    else:
        rows, cols = pos
        q = jnp.concatenate([q[..., :NOPE_C], rope_2d(q[..., NOPE_C:], rows, cols)], axis=-1)
        k, v = mla_expand(c_kv, rope_2d(k_rope, rows, cols), w_ukv)
        ckv_ctx, kr_ctx = ctx
        k_c, v_c = mla_expand(ckv_ctx, kr_ctx, w_ukv)
        k = jnp.concatenate([k_c, k], axis=1)
        v = jnp.concatenate([v_c, v], axis=1)
        new = None
    o = dense_attn(q[:, :, :, None, :], k, v, (NOPE_C + ROPE_C) ** -0.5)
    return o.reshape(B, L, H_C * V_C) @ w_o, new


def setup_inputs(seed: int = 0) -> dict:
    key = jax.random.key(seed)
    ks = jax.random.split(key, 32)

    def nrm(k, shape, scale):
        return jax.random.normal(k, shape, jnp.float32) * scale

    return {
        'x_prompt': nrm(ks[0], (BATCH, SEQ, D_MODEL), 1.0),
        'x_sample': nrm(ks[1], (DEC_BATCH, DEC_SEQ, D_MODEL), 1.0),
        'state_gla_fwd': nrm(ks[2], (DEC_BATCH, N_AB, H_A, DK_A, DV_A), 0.5),
        'state_gla_bwd': nrm(ks[3], (DEC_BATCH, N_AB, H_A, DK_A, DV_A), 0.5),
        'cache_swa_k': nrm(ks[4], (DEC_BATCH, N_AB, PAST_LEN, KV_B, D_B), 1.0),
        'cache_swa_v': nrm(ks[5], (DEC_BATCH, N_AB, PAST_LEN, KV_B, D_B), 1.0),
        'cache_mla_ckv': nrm(ks[6], (DEC_BATCH, N_C, PAST_LEN, KV_LORA), 1.0),
        'cache_mla_kr': nrm(ks[7], (DEC_BATCH, N_C, PAST_LEN, ROPE_C), 1.0),
        'c': nrm(ks[8], (DEC_BATCH, D_MODEL), 1.0),
        'c_ctx': nrm(ks[9], (D_MODEL,), 1.0),
        'w_mod': nrm(ks[10], (DEPTH, D_MODEL, 6 * D_MODEL), 0.5 * D_MODEL ** -0.5),
        'b_mod': nrm(ks[11], (DEPTH, 6 * D_MODEL), 0.02),
        'g_norm': 1.0 + nrm(ks[12], (DEPTH, 4, D_MODEL), 0.05),
        'w_ff1': nrm(ks[13], (DEPTH, D_MODEL, D_FF), D_MODEL ** -0.5),
        'w_ff2': nrm(ks[14], (DEPTH, D_FF, D_MODEL), D_FF ** -0.5),
        'w_in_ab': nrm(ks[15], (N_AB, D_MODEL, AB_IN), D_MODEL ** -0.5),
        'w_gk_f': nrm(ks[16], (N_AB, GK_RANK, H_A * DK_A), GK_RANK ** -0.5),
        'b_gk_f': nrm(ks[17], (N_AB, H_A * DK_A), 0.1),
        'w_gk_b': nrm(ks[18], (N_AB, GK_RANK, H_A * DK_A), GK_RANK ** -0.5),
        'b_gk_b': nrm(ks[19], (N_AB, H_A * DK_A), 0.1),
        'g_gla': 1.0 + nrm(ks[20], (N_AB, DV_A), 0.05),
        'swa_sink': nrm(ks[21], (N_AB, H_B), 0.5),
        'w_out_ab': nrm(ks[22], (N_AB, MIX_AB, D_MODEL), MIX_AB ** -0.5),
        'w_mla_down': nrm(ks[23], (N_C, D_MODEL, DOWN_C), D_MODEL ** -0.5),
        'g_mla_q': 1.0 + nrm(ks[24], (N_C, Q_LORA), 0.05),
        'g_mla_kv': 1.0 + nrm(ks[25], (N_C, KV_LORA), 0.05),
        'w_mla_uq': nrm(ks[26], (N_C, Q_LORA, H_C * (NOPE_C + ROPE_C)), Q_LORA ** -0.5),
        'w_mla_ukv': nrm(ks[27], (N_C, KV_LORA, H_C * (NOPE_C + V_C)), KV_LORA ** -0.5),
        'w_mla_o': nrm(ks[28], (N_C, H_C * V_C, D_MODEL), (H_C * V_C) ** -0.5),
    }


def reference(x_prompt, x_sample, state_gla_fwd, state_gla_bwd, cache_swa_k, cache_swa_v, cache_mla_ckv,
              cache_mla_kr, c, c_ctx, w_mod, b_mod, g_norm, w_ff1, w_ff2, w_in_ab, w_gk_f, b_gk_f, w_gk_b,
              b_gk_b, g_gla, swa_sink, w_out_ab, w_mla_down, g_mla_q, g_mla_kv, w_mla_uq, w_mla_ukv, w_mla_o):
    n_rows = x_sample.shape[1] // GRID_W
    rows = jnp.repeat(jnp.arange(n_rows, dtype=jnp.float32), GRID_W)
    cols = jnp.tile(jnp.arange(GRID_W, dtype=jnp.float32), n_rows)
    pos = (rows, cols)

    xp, xs = x_prompt, x_sample
    st_f, st_b, sk, sv, ckv, ckr = [], [], [], [], [], []
    for l in range(DEPTH):
        mp = adaln(c_ctx[None, :], w_mod[l], b_mod[l])
        ms = adaln(c, w_mod[l], b_mod[l])
        g = g_norm[l]
        hp = pre(xp, g[0], mp[0], mp[1])
        hs = pre(xs, g[0], ms[0], ms[1])
        i = l // 2
        if l % 2 == 0:
            wts = (w_in_ab[i], w_gk_f[i], b_gk_f[i], w_gk_b[i], b_gk_b[i], g_gla[i], swa_sink[i], w_out_ab[i])
            op, (s_f, s_b, k_c, v_c) = mixer_ab(hp, *wts)
            os_, _ = mixer_ab(hs, *wts, ctx=(state_gla_fwd[:, i], state_gla_bwd[:, i], cache_swa_k[:, i], cache_swa_v[:, i]), pos=pos)
            st_f.append(s_f)
            st_b.append(s_b)
            sk.append(k_c)
            sv.append(v_c)
        else:
            wts = (w_mla_down[i], g_mla_q[i], g_mla_kv[i], w_mla_uq[i], w_mla_ukv[i], w_mla_o[i])
            op, (c_kv, k_r) = mixer_c(hp, *wts)
            os_, _ = mixer_c(hs, *wts, ctx=(cache_mla_ckv[:, i], cache_mla_kr[:, i]), pos=pos)
            ckv.append(c_kv)
            ckr.append(k_r)
        xp = post(xp, op, g[1], mp[2])
        xs = post(xs, os_, g[1], ms[2])
        xp = post(xp, sq_relu_mlp(pre(xp, g[2], mp[3], mp[4]), w_ff1[l], w_ff2[l]), g[3], mp[5])
        xs = post(xs, sq_relu_mlp(pre(xs, g[2], ms[3], ms[4]), w_ff1[l], w_ff2[l]), g[3], ms[5])

    new_state_gla_fwd = jnp.stack(st_f, axis=1)
    new_state_gla_bwd = jnp.stack(st_b, axis=1)
    new_cache_swa_k = jnp.stack(sk, axis=1)
    new_cache_swa_v = jnp.stack(sv, axis=1)
    new_cache_mla_ckv = jnp.stack(ckv, axis=1)
    new_cache_mla_kr = jnp.stack(ckr, axis=1)
    return (xp, xs, new_state_gla_fwd, new_state_gla_bwd, new_cache_swa_k, new_cache_swa_v, new_cache_mla_ckv, new_cache_mla_kr)
```

```python
from contextlib import ExitStack
import numpy as np
import concourse.bass as bass
import concourse.mybir as mybir
from concourse.bass_utils import run_bass_kernel_spmd

F32 = mybir.dt.float32
BF16 = mybir.dt.bfloat16
AF = mybir.ActivationFunctionType
ALU = mybir.AluOpType

EPS = 1e-6
NCORES = 8


class Eng:
    def __init__(self, name, sem):
        self.name = name
        self.sem = sem
        self.cnt = 0
        self.seen = {}
        self.q = []
        self.pending = []


class DSem:
    def __init__(self, sem, serial=False):
        self.sem = sem
        self.cnt = 0
        self.serial = serial


class T:
    __slots__ = ("ap", "w", "r", "name", "scoped")

    def __init__(self, ap, name=""):
        self.ap = ap
        self.w = None
        self.r = []
        self.name = name
        self.scoped = False

    def __getitem__(self, k):
        return self.ap[k]


class FW:
    def __init__(self, nc):
        self.nc = nc
        self.es = ExitStack()
        self.engs = {}
        for n in ("pe", "act", "dve", "pool", "sp"):
            self.engs[n] = Eng(n, self.es.enter_context(nc.semaphore("s_" + n)))
        self.nd = 0
        self.nt = 0
        self.ninstr = 0
        self.spool = [self.dsem(serial=True) for _ in range(12)]
        self.spi = 0

    def dsem(self, serial=False):
        self.nd += 1
        return DSem(self.es.enter_context(self.nc.semaphore("d%d" % self.nd)), serial)

    def tile(self, shape, dt, name=None, stack=None):
        self.nt += 1
        es = stack if stack is not None else self.es
        t = es.enter_context(self.nc.sbuf_tensor("sb_" + (name or ("t%d" % self.nt)), list(shape), dt))
        return T(t, name or "")

    def ptile(self, shape, dt, name=None):
        self.nt += 1
        t = self.es.enter_context(self.nc.psum_tensor(name or ("p%d" % self.nt), list(shape), dt))
        return T(t, name or "")

    def _need(self, eng, deps):
        best = {}
        for so, v, raw in deps:
            if isinstance(so, DSem) and not so.serial:
                v = so.cnt
            k = id(so)
            if v > best.get(k, (None, 0))[1]:
                best[k] = (so, v)
        for k, (so, v) in best.items():
            if so is eng and eng.name == "pe":
                continue
            if eng.seen.get(k, 0) >= v:
                continue
            eng.seen[k] = v
            eng.q.append(lambda e, sem=so.sem, v=v: e.wait_ge(sem, v))

    @staticmethod
    def _deps(w, r):
        deps = []
        for t in r:
            if t.w is not None:
                deps.append((t.w[0], t.w[1], True))
        for t in w:
            if t.w is not None:
                deps.append((t.w[0], t.w[1], False))
            deps.extend((so, v, False) for so, v in t.r)
        return deps

    @staticmethod
    def _compact(t):
        m = {}
        for so, v in t.r:
            if isinstance(so, DSem) and not so.serial:
                v = so.cnt
            if v > m.get(id(so), (None, 0))[1]:
                m[id(so)] = (so, v)
        t.r = list(m.values())

    def _pool_pending(self, eng, w, r):
        if eng.pending and any(t.scoped for t in list(w) + list(r)):
            for so, v in eng.pending:
                k = id(so)
                if eng.seen.get(k, 0) >= v:
                    continue
                eng.seen[k] = v
                eng.q.append(lambda e, sem=so.sem, v=v: e.wait_ge(sem, v))
            eng.pending = []

    def op(self, en, fn, w=(), r=(), signal=True):
        eng = self.engs[en]
        self._pool_pending(eng, w, r)
        self._need(eng, self._deps(w, r))
        self.ninstr += 1
        if signal:
            eng.cnt += 1
            c = eng.cnt
            eng.q.append(lambda e, fn=fn, sem=eng.sem: fn(e).then_inc(sem, 1))
        else:
            c = eng.cnt + 1
            eng.q.append(lambda e, fn=fn: fn(e))
        for t in w:
            t.w = (eng, c)
            t.r = []
        for t in r:
            t.r.append((eng, c))
            if len(t.r) > 16:
                self._compact(t)

    def dma(self, en, ds, out, in_, w=(), r=()):
        eng = self.engs[en]
        self._pool_pending(eng, w, r)
        if ds is None:
            ds = self.spool[self.spi % len(self.spool)]
            self.spi += 1
            if ds.cnt and eng.seen.get(id(ds), 0) < ds.cnt:
                eng.seen[id(ds)] = ds.cnt
                eng.q.append(lambda e, sem=ds.sem, v=ds.cnt: e.wait_ge(sem, v))
        self._need(eng, self._deps(w, r))
        self.ninstr += 1
        ds.cnt += 16
        c = ds.cnt
        eng.q.append(lambda e, out=out, in_=in_, sem=ds.sem: e.dma_start(out=out, in_=in_).then_inc(sem, 16))
        for t in w:
            t.w = (ds, c)
            t.r = []
        for t in r:
            t.r.append((ds, c))
            if len(t.r) > 16:
                self._compact(t)

    def barrier(self, dsems=()):
        for eng in self.engs.values():
            waits = []
            for other in self.engs.values():
                if other is eng or other.cnt == 0:
                    continue
                waits.append((other, other.cnt))
            for ds in dsems:
                if ds.cnt:
                    waits.append((ds, ds.cnt))
            if eng.name == "pool":
                eng.pending = waits
                continue
            for so, v in waits:
                k = id(so)
                if eng.seen.get(k, 0) >= v:
                    continue
                eng.seen[k] = v
                eng.q.append(lambda e, sem=so.sem, v=v: e.wait_ge(sem, v))

    def flush(self, final=False, final_dsems=()):
        nc = self.nc
        if final:
            sp = self.engs["sp"]
            for ds in final_dsems:
                if ds.cnt:
                    sp.q.append(lambda e, sem=ds.sem, v=ds.cnt: e.wait_ge(sem, v))
            for n in ("pe", "act", "dve", "pool"):
                eg = self.engs[n]
                if eg.cnt:
                    sp.q.append(lambda e, sem=eg.sem, v=eg.cnt: e.wait_ge(sem, v))
        qs = {n: self.engs[n].q for n in self.engs}
        for n in self.engs:
            self.engs[n].q = []
        with nc.Block() as block:
            @block.tensor
            def _(e):
                for f in qs["pe"]:
                    f(e)

            @block.scalar
            def _(e):
                for f in qs["act"]:
                    f(e)

            @block.vector
            def _(e):
                for f in qs["dve"]:
                    f(e)

            @block.gpsimd
            def _(e):
                for f in qs["pool"]:
                    f(e)

            @block.sync
            def _(e):
                for f in qs["sp"]:
                    f(e)


def make_consts():
    s = np.arange(128)[:, None]
    t = np.arange(128)[None, :]
    same = (s // 64) == (t // 64)
    c = np.zeros((10, 128, 128), np.float32)
    c[0] = np.eye(128)
    k = -1.0 / 16.0
    c[1] = k * (same & (s <= t))
    c[2] = k * (same & (s >= t))
    c[3] = k * (same & (s > t))
    c[4] = k * (same & (s < t))
    c[5] = (same & (s <= t))
    c[6] = (same & (s >= t))
    c[7] = (s <= t)
    c[8] = (s >= t)
    pm = np.zeros((128, 128), np.float32)
    for p in range(128):
        q = p % 64
        b = (q % 32) // 16
        partner = p + 16 if b == 0 else p - 16
        pm[partner, p] = 1.0
    c[9] = pm
    L = 1024
    rows = np.repeat(np.arange(L // 64, dtype=np.float32), 64)
    cols = np.tile(np.arange(64, dtype=np.float32), L // 64)
    pos = [rows, cols]

    def tables(hd, nrows, base):
        nf = hd // 4
        inv = (10000.0 ** (-np.arange(nf, dtype=np.float32) / nf)).astype(np.float32)
        cs = np.zeros((2, nrows, L), np.float32)
        for p in range(nrows - base):
            q = p % hd
            a = q // (2 * nf)
            b = (q % (2 * nf)) // nf
            f = q % nf
            ang = (pos[a] * inv[f]).astype(np.float32)
            cs[0, base + p] = np.cos(ang)
            cs[1, base + p] = -np.sin(ang) if b == 0 else np.sin(ang)
        return cs

    cs64 = tables(64, 128, 0)
    cs32 = tables(32, 96, 64)
    pm32 = np.zeros((96, 96), np.float32)
    for q in range(32):
        b = (q % 16) // 8
        partner = q + 8 if b == 0 else q - 8
        pm32[64 + partner, 64 + q] = 1.0
    return c, cs64, cs32, pm32


def build_program(n_layers=4, do_mixer=True, do_mlp=True, groups_sel=(0, 1)):
    nc = bass.Bass("TRN2", target_bir_lowering=False)

    def din(name, shape):
        return nc.dram_tensor(name, list(shape), F32, kind="ExternalInput").ap()

    def dout(name, shape):
        return nc.dram_tensor(name, list(shape), F32, kind="ExternalOutput").ap()

    xin = din("xin", [1536, 1024])
    cvec = din("cvec", [2, 1024])
    st_f = din("st_f", [2, 4, 64, 128])
    st_b = din("st_b", [2, 4, 64, 128])
    cswa_k = din("cswa_k", [2, 512, 128])
    cswa_v = din("cswa_v", [2, 512, 128])
    cm_ckv = din("cm_ckv", [2, 512, 256])
    cm_kr = din("cm_kr", [2, 512, 32])
    w_mod = din("w_mod", [4, 1024, 6144])
    b_mod = din("b_mod", [4, 6144])
    g_norm = din("g_norm", [4, 4, 1024])
    w_ff1 = din("w_ff1", [4, 1024, 4096])
    w_ff2 = din("w_ff2", [4, 4096, 1024])
    w_in_ab = din("w_in_ab", [2, 1024, 2336])
    w_gk_f = din("w_gk_f", [2, 16, 256])
    b_gk_f = din("b_gk_f", [2, 256])
    w_gk_b = din("w_gk_b", [2, 16, 256])
    b_gk_b = din("b_gk_b", [2, 256])
    g_gla = din("g_gla", [2, 128])
    swa_sink = din("swa_sink", [2, 8])
    w_out_ab = din("w_out_ab", [2, 1024, 1024])
    w_mla_down = din("w_mla_down", [2, 1024, 672])
    g_mla_q = din("g_mla_q", [2, 384])
    g_mla_kv = din("g_mla_kv", [2, 256])
    w_mla_uq = din("w_mla_uq", [2, 384, 1536])
    w_mla_ukv = din("w_mla_ukv", [2, 256, 2048])
    w_mla_o = din("w_mla_o", [2, 1024, 1024])
    consts = din("consts", [10, 128, 128])
    cs64_d = din("cs64", [2, 128, 1024])
    cs32_d = din("cs32", [2, 96, 1024])
    pm32_d = din("pm32", [96, 96])

    y_out = dout("y", [1536, 1024])
    o_stf = dout("o_stf", [2, 2, 4, 64, 128])
    o_stb = dout("o_stb", [2, 2, 4, 64, 128])
    o_swk = dout("o_swk", [2, 2, 256, 128])
    o_swv = dout("o_swv", [2, 2, 256, 128])
    o_ckv = dout("o_ckv", [2, 2, 256, 256])
    o_kr = dout("o_kr", [2, 2, 256, 32])

    fw = FW(nc)
    d_in = None
    d_out = None
    uniq = {"i": 0}

    open_phases = []
    phase_stacks = []
    import os as _os
    DBG = int(_os.environ.get("KDBG", "0"))

    class Bail(Exception):
        pass

    def CK(k):
        if DBG == k:
            raise Bail()

    class Phase:
        def __init__(self):
            self.es = ExitStack()
            open_phases.append(self)

        def tile(self, shape, dt, name):
            uniq["i"] += 1
            t = fw.tile(shape, dt, "%s_%d" % (name, uniq["i"]), stack=self.es)
            t.scoped = True
            return t

        def close(self):
            fw.barrier(fw.spool)
            self.es.close()
            open_phases.remove(self)

    PB = [fw.ptile([128, 512], F32, "pb%d" % i) for i in range(8)]
    rot = {"i": 0, "a": 0, "wide": True}

    def PG():
        t = PB[rot["i"] % (8 if rot["wide"] else 4)]
        rot["i"] += 1
        return t

    def ACCPAIR():
        k = rot["a"] % 2
        rot["a"] += 1
        return PB[4 + 2 * k], PB[5 + 2 * k]

    def MM(out, lhsT, rhs, start, stop, w, r):
        fw.op("pe", lambda e: e.matmul(out, lhsT=lhsT, rhs=rhs, start=start, stop=stop), w=w, r=r, signal=stop)

    def TR(out, in_, idn, w, r):
        fw.op("pe", lambda e: e.transpose(out, in_, idn), w=w, r=r)

    def ACT(out, in_, func, w, r, scale=1.0, bias=0.0):
        fw.op("act", lambda e: e.activation(out=out, in_=in_, func=func, bias=bias, scale=scale), w=w, r=r)

    def TT(out, in0, in1, op, w, r, en="dve"):
        fw.op(en, lambda e: e.tensor_tensor(out=out, in0=in0, in1=in1, op=op), w=w, r=r)

    def STT(out, in0, scalar, in1, op0, op1, w, r, en="dve"):
        fw.op(en, lambda e: e.scalar_tensor_tensor(out=out, in0=in0, scalar=scalar, in1=in1, op0=op0, op1=op1), w=w, r=r)

    def TS(out, in0, s1, op0, w, r, en="dve"):
        fw.op(en, lambda e: e.tensor_scalar(out=out, in0=in0, scalar1=s1, scalar2=None, op0=op0), w=w, r=r)

    def CP(out, in_, w, r, en="dve"):
        fw.op(en, lambda e: e.tensor_copy(out=out, in_=in_), w=w, r=r)

    def RCP(out, in_, w, r):
        fw.op("dve", lambda e: e.reciprocal(out=out, in_=in_), w=w, r=r)

    def MSET(ap, val, w, en="dve"):
        fw.op(en, lambda e: e.memset(ap, val), w=w, r=())

    def SPLIT(src, hi, lo, hiT, loT_, srcT):
        CP(hi, src, [hiT], [srcT])
        TT(lo, src, hi, ALU.subtract, [loT_], [srcT, hiT])

    def bc(ap, shape, axis):
        return ap.unsqueeze(axis).to_broadcast(list(shape))

    evac_rr = {"i": 0}
    fin = {"f": None}

    def defer(fn):
        run_deferred()
        fin["f"] = fn

    def run_deferred():
        if fin["f"] is not None:
            f_ = fin["f"]
            fin["f"] = None
            f_()

    def EVAC(out, in_, w, r):
        bi = PB.index(r[0]) if r and r[0] in PB else evac_rr["i"]
        evac_rr["i"] += 1
        if bi % 2:
            ACT(out, in_, AF.Copy, w, r)
        else:
            CP(out, in_, w, r)

    xT = fw.tile([128, 8, 1536], F32, "xT")
    xTb = [T(xT.ap[:, :, b * 512:(b + 1) * 512], "xTb%d" % b) for b in range(3)]
    cst = fw.tile([128, 10, 128], F32, "cst")
    ident = cst.ap[:, 0, :]
    ones_bf = fw.tile([128, 3, 128], BF16, "ones_bf")
    ones_row = fw.tile([1, 128], F32, "ones_row")
    pm32 = fw.tile([96, 96], F32, "pm32")
    cstb = fw.tile([128, 10, 128], BF16, "cstb")
    pm32b = fw.tile([96, 96], BF16, "pm32b")
    ones_rb = fw.tile([1, 128], BF16, "ones_rb")
    wgkh = fw.tile([49, 2, 256], BF16, "wgkh")
    wgkl = fw.tile([49, 2, 256], BF16, "wgkl")
    sinkh = fw.tile([1, 16], BF16, "sinkh")
    sinkl = fw.tile([1, 16], BF16, "sinkl")
    gnT = fw.tile([128, 128], F32, "gnT")
    bmT = fw.tile([128, 192], F32, "bmT")
    smallT = fw.tile([128, 28], F32, "smallT")
    scT = fw.tile([128, 16], BF16, "scT")
    wgk = fw.tile([49, 2, 256], F32, "wgk")
    sink_row = fw.tile([1, 16], F32, "sink_row")
    esink = fw.tile([128, 16], F32, "esink")
    modv = [fw.tile([128, 2, 6, 8], F32, "modv%d" % i) for i in range(2)]
    mraw = fw.tile([128, 2, 48], F32, "mraw")
    rstd_t = [fw.tile([128, 512], F32, "rstd%d" % i) for i in range(2)]
    utmp = [fw.tile([128, 512], F32, "utmp%d" % i) for i in range(3)]
    rr = {"u": 0, "r": 0}

    def UT():
        rr["u"] += 1
        return utmp[rr["u"] % 3]

    NWB = 6
    WBE = 2048
    wbufs = [(fw.tile([128, WBE], BF16, "wb%d" % i), fw.dsem()) for i in range(NWB)]
    wrot = {"i": 0}
    for t_, _ in wbufs:
        MSET(t_[:], 0.0, [t_], en="pool")

    def wload(parts):
        t, ds = wbufs[wrot["i"] % NWB]
        wrot["i"] += 1
        for dstf, src in parts:
            fw.dma("pool", ds, dstf(t.ap), src, w=[t])
        return t, t.ap

    def v3(ap, P, a, b):
        assert a * b <= WBE
        return ap[0:P, 0:a * b].rearrange("p (a b) -> p a b", a=a)

    def wl3(src, P, a, b):
        t, ap = wload([(lambda q: v3(q, P, a, b), src)])
        return t, v3(ap, P, a, b)

    SU = Phase()
    stage = [SU.tile([128, 1024], F32, "stage") for _ in range(2)]
    fw.dma("sp", d_in, cst[:], consts.rearrange("k p n -> p k n"), w=[cst])
    fw.dma("sp", d_in, pm32[:], pm32_d, w=[pm32])
    MSET(ones_bf[:, 0, :], 1.0 / 1024.0, [ones_bf])
    MSET(ones_bf[:, 1, :], 1.0 / 128.0, [ones_bf])
    MSET(ones_bf[:, 2, :], 1.0, [ones_bf])
    MSET(ones_row[:], 1.0, [ones_row])
    MSET(ones_rb[:], 1.0, [ones_rb])
    CP(cstb[:], cst[:], [cstb], [cst])
    CP(pm32b[:], pm32[:], [pm32b], [pm32])
    st0 = stage[0]
    fw.dma("sp", d_in, st0[0:16, 0:128], cvec.rearrange("j (c p) -> (j c) p", p=128), w=[st0])
    fw.dma("sp", d_in, st0[16:18, 0:128], g_gla, w=[st0])
    fw.dma("sp", d_in, st0[18:24, 0:128], g_mla_q.rearrange("i (c p) -> (i c) p", p=128), w=[st0])
    fw.dma("sp", d_in, st0[24:28, 0:128], g_mla_kv.rearrange("i (c p) -> (i c) p", p=128), w=[st0])
    p = PG()
    TR(p[:, 0:28], st0[0:28, 0:128], ident[0:28, 0:28], [p], [st0, cst])
    CP(smallT[:], p[:, 0:28], [smallT], [p])
    ACT(scT[:], smallT[:, 0:16], AF.Silu, [scT], [smallT])
    st1 = stage[1]
    fw.dma("sp", d_in, st1[:, 0:128], g_norm.rearrange("l i (c p) -> (l i c) p", p=128), w=[st1])
    p = PG()
    TR(p[:, 0:128], st1[:, 0:128], ident, [p], [st1, cst])
    CP(gnT[:], p[:, 0:128], [gnT], [p])
    bmv = b_mod.rearrange("l (n p) -> (l n) p", p=128)
    fw.dma("sp", d_in, st0[:, 128:256], bmv[0:128, :], w=[st0])
    fw.dma("sp", d_in, st1[0:64, 128:256], bmv[128:192, :], w=[st1])
    p = PG()
    TR(p[:, 0:128], st0[:, 128:256], ident, [p], [st0, cst])
    TR(p[:, 128:192], st1[0:64, 128:256], ident[0:64, 0:64], [p], [st1, cst])
    CP(bmT[:], p[:, 0:192], [bmT], [p])
    fw.dma("sp", d_in, wgk[0:16, :, :], w_gk_f.rearrange("i k n -> k i n"), w=[wgk])
    fw.dma("sp", d_in, wgk[32:48, :, :], w_gk_b.rearrange("i k n -> k i n"), w=[wgk])
    fw.dma("sp", d_in, wgk[16:17, :, :], b_gk_f.rearrange("(o i) n -> o i n", o=1), w=[wgk])
    fw.dma("sp", d_in, wgk[48:49, :, :], b_gk_b.rearrange("(o i) n -> o i n", o=1), w=[wgk])
    fw.dma("sp", d_in, sink_row[:], swa_sink.rearrange("(o i) h -> o (i h)", o=1), w=[sink_row])
    SPLIT(sink_row[:], sinkh[:], sinkl[:], sinkh, sinkl, sink_row)
    for q0_, q1_ in ((0, 17), (32, 49)):
        SPLIT(wgk[q0_:q1_, :, :], wgkh[q0_:q1_, :, :], wgkl[q0_:q1_, :, :], wgkh, wgkl, wgk)
    p = PG()
    MM(p[:, 0:16], ones_rb[0:1, :], sinkh[0:1, :], True, False, [p], [ones_rb, sinkh])
    MM(p[:, 0:16], ones_rb[0:1, :], sinkl[0:1, :], False, True, [p], [ones_rb, sinkl])
    ACT(esink[:], p[:, 0:16], AF.Exp, [esink], [p])

    for t in range(12):
        stg = stage[t % 2]
        fw.dma("sp", d_in, stg[:], xin[t * 128:(t + 1) * 128, :], w=[stg])
        for half in range(2):
            p = PG()
            for cc in range(4):
                c = half * 4 + cc
                TR(p[:, cc * 128:(cc + 1) * 128], stg[:, c * 128:(c + 1) * 128], ident, [p], [stg, cst])
            EVAC(xT.ap[:, half * 4:half * 4 + 4, t * 128:(t + 1) * 128], p.ap[:, :].rearrange("p (c n) -> p c n", c=4),
                 [xTb[t // 4]], [p])
    SU.close()

    def modulation(l):
        mv = modv[l % 2]
        pm = PG()
        wv = w_mod[l].rearrange("(k p) n -> p k n", p=128)
        scv = scT.ap[:, :].rearrange("p (j k) -> p j k", j=2)
        for pc in range(24):
            wt, wvw = wl3(wv[:, :, pc * 256:(pc + 1) * 256], 128, 8, 256)
            for nn in range(2):
                n = pc * 2 + nn
                for k in range(8):
                    MM(pm[:, n * 2:(n + 1) * 2], wvw[:, k, nn * 128:(nn + 1) * 128], scv[:, :, k], k == 0, k == 7, [pm], [wt, scT])
        pmv = pm.ap[:, 0:96].rearrange("p (n j) -> p n j", j=2)
        for j in range(2):
            TT(mraw[:, j, :], pmv[:, :, j], bmT[:, l * 48:(l + 1) * 48], ALU.add, [mraw], [pm, bmT])
        g = lambda q: gnT[:, l * 32 + q * 8:l * 32 + q * 8 + 8]
        for j in range(2):
            STT(mv[:, j, 0, :], mraw[:, j, 8:16], 1.0, g(0), ALU.add, ALU.mult, [mv], [mraw, gnT])
            CP(mv[:, j, 1, :], mraw[:, j, 0:8], [mv], [mraw])
            TT(mv[:, j, 2, :], mraw[:, j, 16:24], g(1), ALU.mult, [mv], [mraw, gnT])
            STT(mv[:, j, 3, :], mraw[:, j, 32:40], 1.0, g(2), ALU.add, ALU.mult, [mv], [mraw, gnT])
            CP(mv[:, j, 4, :], mraw[:, j, 24:32], [mv], [mraw])
            TT(mv[:, j, 5, :], mraw[:, j, 40:48], g(3), ALU.mult, [mv], [mraw, gnT])
        return mv

    def rstd_of(sq_aps, sq_T, ones_ap, N):
        ss = PG()
        n = len(sq_aps)
        for c, a in enumerate(sq_aps):
            MM(ss[:, 0:N], ones_ap, a, c == 0, c == n - 1, [ss], [ones_bf] + sq_T)
        rr["r"] += 1
        rs = rstd_t[rr["r"] % 2]
        ACT(rs[:, 0:N], ss[:, 0:N], AF.Ln, [rs], [ss], bias=EPS)
        ACT(rs[:, 0:N], rs[:, 0:N], AF.Exp, [rs], [rs], scale=-0.5)
        return rs

    def pre(b, mv, which, dst, sq):
        j = 0 if b == 0 else 1
        A = mv.ap[:, j, 3 * which + 0, :]
        Sh = mv.ap[:, j, 3 * which + 1, :]
        xb = xTb[b]
        ACT(sq[:], xb[:], AF.Square, [sq], [xb])
        rs = rstd_of([sq[:, c, :] for c in range(8)], [sq], ones_bf[:, 0, :], 512)
        for c in range(8):
            u = UT()
            TT(u[:], xb[:, c, :], rs[:, :], ALU.mult, [u], [xb, rs])
            ACT(dst[:, c, :], u[:], AF.Identity, [dst], [u, mv], scale=A[:, c:c + 1], bias=Sh[:, c:c + 1])

    def post_evac(n, p, osb, sq):
        ACT(osb[:, n, :], p[:, :], AF.Copy, [osb], [p])
        TT(sq[:, n, :], osb[:, n, :], osb[:, n, :], ALU.mult, [sq], [osb])

    def post_finish(b, mv, which, osb, sq):
        j = 0 if b == 0 else 1
        G = mv.ap[:, j, 3 * which + 2, :]
        xb = xTb[b]
        rs = rstd_of([sq[:, c, :] for c in range(8)], [sq], ones_bf[:, 0, :], 512)
        for c in range(8):
            u = UT()
            STT(u[:], osb[:, c, :], G[:, c:c + 1], rs[:, :], ALU.mult, ALU.mult, [u], [osb, rs, mv])
            TT(xb[:, c, :], xb[:, c, :], u[:], ALU.add, [xb], [xb, u])

    def mlp(l, b, mv, hT, hid, osb, sq, next_pre=None):
        w1v = w_ff1[l].rearrange("(k p) n -> p k n", p=128)
        for jp in range(16):
            wt, wv = wl3(w1v[:, :, jp * 256:(jp + 1) * 256], 128, 8, 256)
            for jj in range(2):
                jn = jp * 2 + jj
                ph_ = PG()
                for k in range(8):
                    MM(ph_[:, :], wv[:, k, jj * 128:(jj + 1) * 128], hT[:, k, :], k == 0, k == 7, [ph_], [wt, hT])
                ACT(hid[:, jn, :], ph_[:, :], AF.Relu, [hid], [ph_])
                TT(hid[:, jn, :], hid[:, jn, :], hid[:, jn, :], ALU.mult, [hid], [hid])
        if next_pre is not None:
            next_pre()
        w2v = w_ff2[l].rearrange("(j p) n -> p j n", p=128)
        for n in range(8):
            po = PG()
            for jh in range(2):
                wt, wv = wl3(w2v[:, jh * 16:(jh + 1) * 16, n * 128:(n + 1) * 128], 128, 16, 128)
                for jj in range(16):
                    jn = jh * 16 + jj
                    MM(po[:, :], wv[:, jj, :], hid[:, jn, :], jn == 0, jn == 31, [po], [wt, hid])
            post_evac(n, po, osb, sq)
        post_finish(b, mv, 1, osb, sq)

    def even_mixer(l, i, mv, G):
        kind, nt, blocks = G["kind"], G["nt"], G["blocks"]
        ntok = nt * 128
        nb = len(blocks)
        tg0 = G["t0"]
        E = Phase()
        hT = [E.tile([128, 8, 512], BF16, "hT") for _ in range(nb)]
        sq = E.tile([128, 8, 512], BF16, "sq")
        loTh = E.tile([49, ntok], BF16, "loTh")
        loTl = E.tile([49, ntok], BF16, "loTl")
        lotmp = E.tile([48, 512], F32, "lotmp")
        for q0_ in (0, 32):
            MSET(loTh[q0_:q0_ + 17, :], 1.0, [loTh])
            MSET(loTl[q0_:q0_ + 17, :], 0.0, [loTl])
        mixT = E.tile([128, 4, ntok], BF16, "mixT")
        winv = w_in_ab[i].rearrange("(k p) n -> p k n", p=128)
        for bl, b in enumerate(blocks):
            pre(b, mv, 0, hT[bl], sq)
        wt, wa = wload([(lambda q: v3(q, 128, 8, 48)[:, :, 0:16], winv[:, :, 1536:1552]),
                        (lambda q: v3(q, 128, 8, 48)[:, :, 32:48], winv[:, :, 1552:1568])])
        wv = v3(wa, 128, 8, 48)
        for bl in range(nb):
            p = PG()
            for k in range(8):
                MM(p[0:48, :], wv[:, k, :], hT[bl][:, k, :], k == 0, k == 7, [p], [wt, hT[bl]])
            ACT(lotmp[0:48, :], p[0:48, :], AF.Copy, [lotmp], [p])
            for q0_ in (0, 32):
                SPLIT(lotmp[q0_:q0_ + 16, :], loTh[q0_:q0_ + 16, bl * 512:(bl + 1) * 512],
                      loTl[q0_:q0_ + 16, bl * 512:(bl + 1) * 512], loTh, loTl, lotmp)

        CK(1)
        S = Phase()
        qaT = S.tile([128, ntok], BF16, "qaT")
        kaT = S.tile([128, ntok], BF16, "kaT")
        sgT = S.tile([128, 2, ntok], BF16, "sgT")
        katok = S.tile([128, nt, 128], BF16, "katok")
        vatok = S.tile([128, nt, 256], BF16, "vatok")
        qeS = S.tile([128, nt, 2, 128], BF16, "qeS")
        attS = S.tile([128, nt, 2, 2, 128], BF16, "attS")
        Ust = S.tile([128, nt, 2, 2, 128], F32, "Ust")
        SbfF = S.tile([128, nt, 2, 128], BF16, "SbfF")
        SbfB = S.tile([128, nt, 2, 128], BF16, "SbfB")
        dsc = S.tile([128, nt, 2, 2], F32, "dsc")
        Sf = S.tile([128, 128], F32, "Sf")
        Sb = S.tile([128, 128], F32, "Sb")
        etmp = [S.tile([128, 256], F32, "etmp") for _ in range(2)]
        g2 = [S.tile([128, 2, 128], F32, "g2") for _ in range(2)]
        ggh = [S.tile([128, 2, 128], BF16, "ggh") for _ in range(2)]
        ggl = [S.tile([128, 2, 128], BF16, "ggl") for _ in range(2)]
        Epos = [S.tile([128, 2, 128], F32, "Epos") for _ in range(2)]
        Eneg = [S.tile([128, 2, 128], F32, "Eneg") for _ in range(2)]
        ED = [S.tile([128, 2, 128], F32, "ED") for _ in range(2)]
        ke = [S.tile([128, 2, 128], BF16, "ke") for _ in range(2)]
        kd = [S.tile([128, 2, 128], BF16, "kd") for _ in range(2)]
        osb2 = [S.tile([128, 256], F32, "osb2") for _ in range(2)]
        sq2 = [S.tile([128, 256], BF16, "sq2") for _ in range(2)]
        for ch in range(2):
            wt, wa = wload([(lambda q: v3(q, 128, 8, 256)[:, :, 0:128], winv[:, :, ch * 128:(ch + 1) * 128]),
                            (lambda q: v3(q, 128, 8, 256)[:, :, 128:256], winv[:, :, 256 + ch * 128:256 + (ch + 1) * 128])])
            wv = v3(wa, 128, 8, 256)
            for bl in range(nb):
                p = PG()
                for k in range(8):
                    MM(p[:, :], wv[:, k, 0:128], hT[bl][:, k, :], k == 0, k == 7, [p], [wt, hT[bl]])
                ACT(qaT[:, bl * 512:(bl + 1) * 512], p[:, :], AF.Copy, [qaT], [p], scale=0.125)
                p = PG()
                for k in range(8):
                    MM(p[:, :], wv[:, k, 128:256], hT[bl][:, k, :], k == 0, k == 7, [p], [wt, hT[bl]])
                CP(kaT[:, bl * 512:(bl + 1) * 512], p[:, :], [kaT], [p])
            CK(11)
            wt, wv = wl3(winv[:, :, 1024 + ch * 256:1024 + (ch + 1) * 256], 128, 8, 256)
            for hl in range(2):
                for bl in range(nb):
                    p = PG()
                    for k in range(8):
                        MM(p[:, :], wv[:, k, hl * 128:(hl + 1) * 128], hT[bl][:, k, :], k == 0, k == 7, [p], [wt, hT[bl]])
                    ACT(sgT[:, hl, bl * 512:(bl + 1) * 512], p[:, :], AF.Silu, [sgT], [p])
            CK(12)
            wt1, wv1 = wl3(winv[:, :, 256 + ch * 128:256 + (ch + 1) * 128], 128, 8, 128)
            wt2, wv2 = wl3(winv[:, :, 512 + ch * 256:512 + (ch + 1) * 256], 128, 8, 256)
            CK(13)
            for tl in range(nt):
                if tl == 1:
                    CK(14)
                bl, tt = tl // 4, tl % 4
                p = PG()
                for k in range(8):
                    MM(p[:, 0:128], hT[bl][:, k, tt * 128:(tt + 1) * 128], wv1[:, k, :], k == 0, k == 7, [p], [wt1, hT[bl]])
                CP(katok[:, tl, :], p[:, 0:128], [katok], [p])
                p = PG()
                for k in range(8):
                    MM(p[:, 0:256], hT[bl][:, k, tt * 128:(tt + 1) * 128], wv2[:, k, :], k == 0, k == 7, [p], [wt2, hT[bl]])
                ACT(vatok[:, tl, :], p[:, 0:256], AF.Copy, [vatok], [p])

            CK(2)
            for si, (s0, sn) in enumerate(G["seqs"]):
                tiles = list(range(s0, s0 + sn))
                if kind == "s":
                    fw.dma("sp", d_in, Sf[:], st_f[i, 2 * ch:2 * ch + 2].rearrange("h k v -> (h k) v"), w=[Sf])
                    fw.dma("sp", d_in, Sb[:], st_b[i, 2 * ch:2 * ch + 2].rearrange("h k v -> (h k) v"), w=[Sb])
                else:
                    MSET(Sf[:], 0.0, [Sf])
                    MSET(Sb[:], 0.0, [Sb])
                def stA1(tl):
                    tsl = slice(tl * 128, (tl + 1) * 128)
                    pb_ = tl % 2
                    gg = g2[pb_]
                    for d in range(2):
                        base = 0 if d == 0 else 32
                        bsl = slice(base, base + 17)
                        csl = slice(ch * 128, (ch + 1) * 128)
                        pz = PG()
                        zr = pz[:, 0:128]
                        MM(zr, loTh[bsl, tsl], wgkh[bsl, i, csl], True, False, [pz], [loTh, wgkh])
                        MM(zr, loTh[bsl, tsl], wgkl[bsl, i, csl], False, False, [pz], [loTh, wgkl])
                        MM(zr, loTl[bsl, tsl], wgkh[bsl, i, csl], False, True, [pz], [loTl, wgkh])
                        ACT(etmp[pb_][:, d * 128:(d + 1) * 128], pz[:, 0:128], AF.Exp, [etmp[pb_]], [pz], scale=-1.0)
                    ACT(gg.ap[:, :, :].rearrange("p a b -> p (a b)"), etmp[pb_][:], AF.Ln, [gg], [etmp[pb_]], bias=1.0)
                    SPLIT(gg[:, :, :], ggh[pb_][:, :, :], ggl[pb_][:, :, :], ggh[pb_], ggl[pb_], gg)

                def stA2(tl):
                    pb_ = tl % 2
                    for d in range(2):
                        pc = PG()
                        MM(pc[:, 0:128], ggh[pb_][:, d, :], cstb[:, 1 + d, :], True, False, [pc], [ggh[pb_], cstb])
                        MM(pc[:, 0:128], ggl[pb_][:, d, :], cstb[:, 1 + d, :], False, True, [pc], [ggl[pb_], cstb])
                        ACT(Epos[pb_][:, d, :], pc[:, 0:128], AF.Exp, [Epos[pb_]], [pc])
                        ACT(Eneg[pb_][:, d, :], pc[:, 0:128], AF.Exp, [Eneg[pb_]], [pc], scale=-1.0)
                    for d in range(2):
                        pc = PG()
                        MM(pc[:, 0:128], cstb[:, 3 + d, :], ggh[pb_][:, d, :], True, False, [pc], [ggh[pb_], cstb])
                        MM(pc[:, 0:128], cstb[:, 3 + d, :], ggl[pb_][:, d, :], False, True, [pc], [ggl[pb_], cstb])
                        ACT(ED[pb_][:, d, :], pc[:, 0:128], AF.Exp, [ED[pb_]], [pc])

                def stA3(tl):
                    tsl = slice(tl * 128, (tl + 1) * 128)
                    pb_ = tl % 2
                    TT(qeS[:, tl, :, :], bc(qaT[:, tsl], [128, 2, 128], 1), Epos[pb_][:], ALU.mult, [qeS], [qaT, Epos[pb_]])
                    TT(ke[pb_][:], bc(kaT[:, tsl], [128, 2, 128], 1), Eneg[pb_][:], ALU.mult, [ke[pb_]], [kaT, Eneg[pb_]])
                    TT(kd[pb_][:], bc(katok[:, tl, :], [128, 2, 128], 1), ED[pb_][:], ALU.mult, [kd[pb_]], [katok, ED[pb_]])
                    CP(dsc[:, tl, 0, :], Epos[pb_][:, 0, 63:128:64], [dsc], [Epos[pb_]])
                    CP(dsc[:, tl, 1, :], Epos[pb_][:, 1, 0:128:64], [dsc], [Epos[pb_]])

                def stA4(tl):
                    pb_ = tl % 2
                    for hl in range(2):
                        for d in range(2):
                            pa = PG()
                            MM(pa[:, 0:128], ke[pb_][hl * 64:(hl + 1) * 64, d, :],
                               qeS[hl * 64:(hl + 1) * 64, tl, d, :], True, True, [pa], [ke[pb_], qeS])
                            TT(attS[:, tl, hl, d, :], pa[:, 0:128], cst[:, 5 + d, :], ALU.mult, [attS], [pa, cst])

                def stA5(tl):
                    pb_ = tl % 2
                    for half in range(2):
                        for d in range(2):
                            pu = PG()
                            MM(pu[:, 0:256], kd[pb_][half * 64:(half + 1) * 64, d, :],
                               vatok[half * 64:(half + 1) * 64, tl, :], True, True, [pu], [kd[pb_], vatok])
                            EVAC(Ust[0:64, tl, d, half, :], pu[0:64, 0:128], [Ust], [pu])
                            EVAC(Ust[64:128, tl, d, half, :], pu[64:128, 128:256], [Ust], [pu])

                for t0_ in range(0, len(tiles), 2):
                    pair = tiles[t0_:t0_ + 2]
                    for st_ in (stA1, stA2, stA3, stA4, stA5):
                        for tl in pair:
                            st_(tl)
                CK(3)
                for tl in tiles:
                    for half in (0, 1):
                        CP(SbfF[:, tl, half, :], Sf[:], [SbfF], [Sf])
                        STT(Sf[:], Sf[:], dsc[:, tl, 0, half:half + 1], Ust[:, tl, 0, half, :], ALU.mult, ALU.add, [Sf], [Sf, dsc, Ust])
                for tl in reversed(tiles):
                    for half in (1, 0):
                        CP(SbfB[:, tl, half, :], Sb[:], [SbfB], [Sb])
                        STT(Sb[:], Sb[:], dsc[:, tl, 1, half:half + 1], Ust[:, tl, 1, half, :], ALU.mult, ALU.add, [Sb], [Sb, dsc, Ust])
                if kind == "p":
                    fw.dma("sp", d_out, o_stf[si, i, 2 * ch:2 * ch + 2].rearrange("h k v -> (h k) v"), Sf[:], r=[Sf])
                    fw.dma("sp", d_out, o_stb[si, i, 2 * ch:2 * ch + 2].rearrange("h k v -> (h k) v"), Sb[:], r=[Sb])
                CK(4)
                def stB1(tl):
                    pb_ = tl % 2
                    for hl in range(2):
                        hs = slice(hl * 64, (hl + 1) * 64)
                        po = PG()
                        reg = po[:, 0:128]
                        MM(reg, vatok[:, tl, hl * 128:(hl + 1) * 128], attS[:, tl, hl, 0, :], True, False, [po], [vatok, attS])
                        MM(reg, vatok[:, tl, hl * 128:(hl + 1) * 128], attS[:, tl, hl, 1, :], False, False, [po], [vatok, attS])
                        for half in range(2):
                            MM(po[:, half * 64:half * 64 + 64], SbfF[hs, tl, half, :],
                               qeS[hs, tl, 0, half * 64:(half + 1) * 64], False, False, [po], [SbfF, qeS])
                        for half in range(2):
                            MM(po[:, half * 64:half * 64 + 64], SbfB[hs, tl, half, :],
                               qeS[hs, tl, 1, half * 64:(half + 1) * 64], False, half == 1, [po], [SbfB, qeS])
                        EVAC(osb2[pb_][:, hl * 128:(hl + 1) * 128], po[:, 0:128], [osb2[pb_]], [po])
                    TT(sq2[pb_][:], osb2[pb_][:], osb2[pb_][:], ALU.mult, [sq2[pb_]], [osb2[pb_]])

                def stB2(tl):
                    tsl = slice(tl * 128, (tl + 1) * 128)
                    pb_ = tl % 2
                    rs = rstd_of([sq2[pb_][:, :]], [sq2[pb_]], ones_bf[:, 1, :], 256)
                    STT(osb2[pb_][:], osb2[pb_][:], smallT[:, 16 + i:17 + i], rs[:, 0:256], ALU.mult, ALU.mult,
                        [osb2[pb_]], [osb2[pb_], rs, smallT])
                    TT(mixT[:, 2 * ch:2 * ch + 2, tsl], osb2[pb_].ap[:, :].rearrange("p (a b) -> p a b", a=2), sgT[:, :, tsl],
                       ALU.mult, [mixT], [osb2[pb_], sgT])

                for t0_ in range(0, len(tiles), 2):
                    pair = tiles[t0_:t0_ + 2]
                    for st_ in (stB1, stB2):
                        for tl in pair:
                            st_(tl)
        S.close()

        CK(5)
        oswaT = E.tile([64, 8, ntok], BF16, "oswaT")
        W = Phase()
        qbT = W.tile([128, 4, ntok], BF16, "qbT")
        kbT = W.tile([128, ntok], BF16, "kbT")
        vbtok = W.tile([128, nt, 128], BF16, "vbtok")
        Et = [W.tile([128, 512], BF16, "Et") for _ in range(3)]
        dn = W.tile([64, 512], F32, "dn")
        if kind == "s":
            rt = W.tile([128, 2, 1024], F32, "rt")
            fw.dma("sp", d_in, rt[:], cs64_d.rearrange("k p n -> p k n"), w=[rt])
            xf = W.tile([128, 512], F32, "xf")
            xfh = W.tile([128, 512], BF16, "xfh")
            xfl = W.tile([128, 512], BF16, "xfl")
            t1 = W.tile([128, 512], F32, "t1")
            kc_st = W.tile([128, 4, 128], F32, "kc_st")
            vc_st = W.tile([128, 4, 128], F32, "vc_st")
            kcT = W.tile([128, 512], BF16, "kcT")
            vc = W.tile([128, 4, 128], BF16, "vc")
        else:
            kvst = [W.tile([128, 256], F32, "kvst") for _ in range(2)]

        def rope64(p, bl, dst, dstT):
            ACT(xf[:], p[:, :], AF.Copy, [xf], [p])
            SPLIT(xf[:], xfh[:], xfl[:], xfh, xfl, xf)
            pp = PG()
            MM(pp[:, :], cstb[:, 9, :], xfh[:], True, False, [pp], [cstb, xfh])
            MM(pp[:, :], cstb[:, 9, :], xfl[:], False, True, [pp], [cstb, xfl])
            TT(t1[:], xf[:], rt[:, 0, bl * 512:(bl + 1) * 512], ALU.mult, [t1], [xf, rt])
            TT(xf[:], pp[:, :], rt[:, 1, bl * 512:(bl + 1) * 512], ALU.mult, [xf], [pp, rt])
            TT(dst, t1[:], xf[:], ALU.add, [dstT], [t1, xf])

        for ap_ in range(2):
            def v5(q):
                return q[0:128, 0:2048].rearrange("p (k a g d) -> p k a g d", k=8, a=2, g=2)
            wt, wa = wload([(lambda q, g=g, al=al: v5(q)[:, :, al, g, :],
                             winv[:, :, 1568 + g * 256 + (ap_ * 2 + al) * 64:1568 + g * 256 + (ap_ * 2 + al + 1) * 64])
                            for g in range(2) for al in range(2)])
            wv = v5(wa)
            for al in range(2):
                a = ap_ * 2 + al
                for bl in range(nb):
                    p = PG()
                    for k in range(8):
                        MM(p[:, :], wv[:, k, al, :, :].rearrange("p g d -> p (g d)"), hT[bl][:, k, :], k == 0, k == 7, [p], [wt, hT[bl]])
                    if kind == "p":
                        EVAC(qbT[:, a, bl * 512:(bl + 1) * 512], p[:, :], [qbT], [p])
                    else:
                        rope64(p, bl, qbT[:, a, bl * 512:(bl + 1) * 512], qbT)
        wt, wv = wl3(winv[:, :, 2080:2208], 128, 8, 128)
        for bl in range(nb):
            p = PG()
            for k in range(8):
                MM(p[:, :], wv[:, k, :], hT[bl][:, k, :], k == 0, k == 7, [p], [wt, hT[bl]])
            if kind == "p":
                EVAC(kbT[:, bl * 512:(bl + 1) * 512], p[:, :], [kbT], [p])
            else:
                rope64(p, bl, kbT[:, bl * 512:(bl + 1) * 512], kbT)
        wt, wv = wl3(winv[:, :, 2080:2336], 128, 8, 256)
        for tl in range(nt):
            bl, tt = tl // 4, tl % 4
            p = PG()
            for k in range(8):
                MM(p[:, 0:256], hT[bl][:, k, tt * 128:(tt + 1) * 128], wv[:, k, :], k == 0, k == 7, [p], [wt, hT[bl]])
            if kind == "p":
                ks = kvst[tl % 2]
                ACT(ks[:], p[:, 0:256], AF.Copy, [ks], [p])
                CP(vbtok[:, tl, :], ks[:, 128:256], [vbtok], [ks])
            else:
                CP(vbtok[:, tl, :], p[:, 128:256], [vbtok], [p])
            if kind == "p":
                si, r0 = tl // 2, (tl % 2) * 128
                fw.dma("sp", d_out, o_swk[si, i, r0:r0 + 128, :], ks[:, 0:128], r=[ks])
                fw.dma("sp", d_out, o_swv[si, i, r0:r0 + 128, :], ks[:, 128:256], r=[ks])
        if kind == "s":
            fw.dma("sp", d_in, kc_st[:], cswa_k[i].rearrange("(t p) f -> p t f", p=128), w=[kc_st])
            fw.dma("sp", d_in, vc_st[:], cswa_v[i].rearrange("(t p) f -> p t f", p=128), w=[vc_st])
            p = PG()
            for t in range(4):
                TR(p[:, t * 128:(t + 1) * 128], kc_st[:, t, :], ident, [p], [kc_st, cst])
            CP(kcT[:], p[:, :], [kcT], [p])
            CP(vc[:], vc_st[:], [vc], [vc_st])
        CK(6)
        rot["wide"] = False
        SC = 0.125
        eti = 0
        if kind == "p":
            for si, (s0, sn) in enumerate(G["seqs"]):
                q0 = s0 * 128
                for kv in range(2):
                    ks_ = slice(kv * 64, (kv + 1) * 64)
                    for apair in range(2):
                        po, pd = ACCPAIR()

                        def score_p(kt):
                            tl = s0 + kt
                            ps = PG()
                            MM(ps[:, :], kbT[ks_, tl * 128:(tl + 1) * 128], qbT[ks_, 2 * apair:2 * apair + 2, q0:q0 + 256],
                               True, True, [ps], [kbT, qbT])
                            return ps
                        ps_next = score_p(0)
                        for kt in range(sn):
                            tl = s0 + kt
                            ps = ps_next
                            if kt + 1 < sn:
                                ps_next = score_p(kt + 1)
                            et = Et[eti % 3]
                            eti += 1
                            ACT(et[:], ps[:, :], AF.Exp, [et], [ps], scale=SC)
                            MM(po[0:64, :], vbtok[:, tl, kv * 64:(kv + 1) * 64], et[:], kt == 0, kt == sn - 1, [po], [vbtok, et])
                            MM(pd[0:64, :], ones_bf[:, 2, 0:64], et[:], kt == 0, kt == sn - 1, [pd], [ones_bf, et])
                            if kt == min(1, sn - 1):
                                run_deferred()

                        def fin_p(po=po, pd=pd, kv=kv, apair=apair, q0=q0):
                            for al in range(2):
                                col = i * 8 + kv * 4 + apair * 2 + al
                                TS(dn[0:64, al * 256:(al + 1) * 256], pd[0:64, al * 256:(al + 1) * 256], esink[0:64, col:col + 1],
                                   ALU.add, [dn], [pd, esink])
                            ACT(dn[:], dn[:], AF.Ln, [dn], [dn])
                            ACT(dn[:], dn[:], AF.Exp, [dn], [dn], scale=-1.0)
                            h0 = kv * 4 + apair * 2
                            TT(oswaT[0:64, h0:h0 + 2, q0:q0 + 256], po.ap[0:64, :].rearrange("p (a b) -> p a b", a=2),
                               dn.ap[0:64, :].rearrange("p (a b) -> p a b", a=2), ALU.mult, [oswaT], [po, dn])
                        defer(fin_p)
        else:
            for qt in range(nt):
                for kv in range(2):
                    ks_ = slice(kv * 64, (kv + 1) * 64)
                    keys = [("c", t) for t in range(4)]
                    if qt > 0:
                        keys.append(("l", qt - 1))
                    keys.append(("l", qt))
                    if qt < nt - 1:
                        keys.append(("l", qt + 1))
                    po, pd = ACCPAIR()

                    def score_s(ki):
                        kk, kt = keys[ki]
                        ps = PG()
                        if kk == "c":
                            MM(ps[:, :], kcT[ks_, kt * 128:(kt + 1) * 128], qbT[ks_, :, qt * 128:(qt + 1) * 128], True, True, [ps], [kcT, qbT])
                        else:
                            MM(ps[:, :], kbT[ks_, kt * 128:(kt + 1) * 128], qbT[ks_, :, qt * 128:(qt + 1) * 128], True, True, [ps], [kbT, qbT])
                        return ps
                    ps_next = score_s(0)
                    for ki, (kk, kt) in enumerate(keys):
                        ps = ps_next
                        if ki + 1 < len(keys):
                            ps_next = score_s(ki + 1)
                        et = Et[eti % 3]
                        eti += 1
                        ACT(et[:], ps[:, :], AF.Exp, [et], [ps], scale=SC)
                        if kk == "l" and kt != qt:
                            mk = cst[:, 8, :] if kt < qt else cst[:, 7, :]
                            etv = et.ap[:, :].rearrange("p (a b) -> p a b", a=4)
                            TT(etv, etv, bc(mk, [128, 4, 128], 1), ALU.mult, [et], [et, cst])
                        vsrc, vT = (vc[:, kt, kv * 64:(kv + 1) * 64], vc) if kk == "c" else (vbtok[:, kt, kv * 64:(kv + 1) * 64], vbtok)
                        last = ki == len(keys) - 1
                        MM(po[0:64, :], vsrc, et[:], ki == 0, last, [po], [vT, et])
                        MM(pd[0:64, :], ones_bf[:, 2, 0:64], et[:], ki == 0, last, [pd], [ones_bf, et])
                        if ki == 2:
                            run_deferred()

                    def fin_s(po=po, pd=pd, kv=kv, qt=qt):
                        c0 = i * 8 + kv * 4
                        dnv = dn.ap[0:64, :].rearrange("p (a b) -> p a b", a=4)
                        TT(dnv, pd.ap[0:64, :].rearrange("p (a b) -> p a b", a=4), bc(esink[0:64, c0:c0 + 4], [64, 4, 128], 2),
                           ALU.add, [dn], [pd, esink])
                        ACT(dn[:], dn[:], AF.Ln, [dn], [dn])
                        ACT(dn[:], dn[:], AF.Exp, [dn], [dn], scale=-1.0)
                        TT(oswaT[0:64, kv * 4:(kv + 1) * 4, qt * 128:(qt + 1) * 128],
                           po.ap[0:64, :].rearrange("p (a b) -> p a b", a=4), dnv, ALU.mult, [oswaT], [po, dn])
                    defer(fin_s)
        run_deferred()
        rot["wide"] = True
        CK(7)

        O = W
        osb = O.tile([128, 8, 512], F32, "osb")
        woa_v = w_out_ab[i][0:512, :].rearrange("(h p) n -> p h n", p=128)
        wob_v = w_out_ab[i][512:1024, :].rearrange("(h p) n -> p h n", p=64)
        for bl, b in enumerate(blocks):
            bs = slice(bl * 512, (bl + 1) * 512)
            for npc in range(4):
                wta, wva = wl3(woa_v[:, :, npc * 256:(npc + 1) * 256], 128, 4, 256)
                wtb, wvb = wl3(wob_v[:, :, npc * 256:(npc + 1) * 256], 64, 8, 256)
                for nn in range(2):
                    n = npc * 2 + nn
                    p = PG()
                    for h in range(4):
                        MM(p[:, :], wva[:, h, nn * 128:(nn + 1) * 128], mixT[:, h, bs], h == 0, False, [p], [wta, mixT])
                    for h in range(8):
                        MM(p[:, :], wvb[0:64, h, nn * 128:(nn + 1) * 128], oswaT[0:64, h, bs], False, h == 7, [p], [wtb, oswaT])
                    post_evac(n, p, osb, sq)
            post_finish(b, mv, 0, osb, sq)
        O.close()
        E.close()

    def odd_mixer(l, i, mv, G):
        kind, nt, blocks = G["kind"], G["nt"], G["blocks"]
        ntok = nt * 128
        nb = len(blocks)
        koff = 512 if kind == "s" else 0
        nkeys = ntok + koff
        E = Phase()
        cqn = E.tile([128, 3, ntok], BF16, "cqn")
        ckvT = E.tile([128, 2, nkeys], BF16, "ckvT")
        krT = E.tile([96, nkeys], BF16, "krT")
        oT = E.tile([64, 16, ntok], BF16, "oT")
        t1 = E.tile([96, 512], F32, "t1")
        kr32 = E.tile([96, 512], F32, "kr32")
        krh = E.tile([96, 512], BF16, "krh")
        krl = E.tile([96, 512], BF16, "krl")
        if kind == "s":
            rt32 = E.tile([96, 2, 1024], F32, "rt32")
            fw.dma("sp", d_in, rt32[:], cs32_d.rearrange("k p n -> p k n"), w=[rt32])
        wdv = w_mla_down[i].rearrange("(k p) n -> p k n", p=128)

        def rope32(p, bl, dst, dstT):
            ACT(kr32[64:96, :], p[64:96, :], AF.Copy, [kr32], [p])
            SPLIT(kr32[64:96, :], krh[64:96, :], krl[64:96, :], krh, krl, kr32)
            pp = PG()
            MM(pp[0:96, :], pm32b[64:96, 0:96], krh[64:96, :], True, False, [pp], [pm32b, krh])
            MM(pp[0:96, :], pm32b[64:96, 0:96], krl[64:96, :], False, True, [pp], [pm32b, krl])
            TT(t1[64:96, :], kr32[64:96, :], rt32[64:96, 0, bl * 512:(bl + 1) * 512], ALU.mult, [t1], [kr32, rt32])
            TT(kr32[64:96, :], pp[64:96, :], rt32[64:96, 1, bl * 512:(bl + 1) * 512], ALU.mult, [kr32], [pp, rt32])
            TT(dst, t1[64:96, :], kr32[64:96, :], ALU.add, [dstT], [t1, kr32])

        D = Phase()
        hT = [D.tile([128, 8, 512], BF16, "hT") for _ in range(nb)]
        sq = D.tile([128, 8, 512], BF16, "sq")
        for bl, b in enumerate(blocks):
            pre(b, mv, 0, hT[bl], sq)
        cq32 = D.tile([128, 3, 512], F32, "cq32")
        ckv32 = D.tile([128, 2, 512], F32, "ckv32")
        if kind == "p":
            ost = D.tile([128, 4, 256], F32, "ost")
            krst = D.tile([128, 4, 32], F32, "krst")
        else:
            ckv_st = D.tile([128, 4, 256], F32, "ckv_st")
            kr_st = D.tile([128, 4, 96], F32, "kr_st")
        for bl, b in enumerate(blocks):
            bs = slice(bl * 512, (bl + 1) * 512)
            for c in range(3):
                wt, wv = wl3(wdv[:, :, c * 128:(c + 1) * 128], 128, 8, 128)
                p = PG()
                for k in range(8):
                    MM(p[:, :], wv[:, k, :], hT[bl][:, k, :], k == 0, k == 7, [p], [wt, hT[bl]])
                ACT(cq32[:, c, :], p[:, :], AF.Copy, [cq32], [p])
                ACT(sq[:, c, :], p[:, :], AF.Square, [sq], [p], scale=float(1.0 / np.sqrt(3.0)))
            rs = rstd_of([sq[:, c, :] for c in range(3)], [sq], ones_bf[:, 1, :], 512)
            for c in range(3):
                STT(cqn[:, c, bs], cq32[:, c, :], smallT[:, 18 + i * 3 + c:19 + i * 3 + c], rs[:, :], ALU.mult, ALU.mult,
                    [cqn], [cq32, rs, smallT])
            for c in range(2):
                wt, wv = wl3(wdv[:, :, 384 + c * 128:384 + (c + 1) * 128], 128, 8, 128)
                p = PG()
                for k in range(8):
                    MM(p[:, :], wv[:, k, :], hT[bl][:, k, :], k == 0, k == 7, [p], [wt, hT[bl]])
                ACT(ckv32[:, c, :], p[:, :], AF.Copy, [ckv32], [p])
                ACT(sq[:, 4 + c, :], p[:, :], AF.Square, [sq], [p], scale=float(1.0 / np.sqrt(2.0)))
            rs = rstd_of([sq[:, 4 + c, :] for c in range(2)], [sq], ones_bf[:, 1, :], 512)
            for c in range(2):
                STT(ckv32[:, c, :], ckv32[:, c, :], smallT[:, 24 + i * 2 + c:25 + i * 2 + c], rs[:, :], ALU.mult, ALU.mult,
                    [ckv32], [ckv32, rs, smallT])
                CP(ckvT[:, c, koff + bl * 512:koff + (bl + 1) * 512], ckv32[:, c, :], [ckvT], [ckv32])
            if kind == "p":
                for c in range(2):
                    p = PG()
                    for tt in range(4):
                        TR(p[:, tt * 128:(tt + 1) * 128], ckv32[:, c, tt * 128:(tt + 1) * 128], ident, [p], [ckv32, cst])
                    EVAC(ost[:, :, c * 128:(c + 1) * 128], p.ap[:, :].rearrange("p (a b) -> p a b", a=4), [ost], [p])
                for tt in range(4):
                    si, r0 = tt // 2, (tt % 2) * 128
                    fw.dma("sp", d_out, o_ckv[si, i, r0:r0 + 128, :], ost[:, tt, :], r=[ost])
            wt, wa = wload([(lambda q: v3(q, 128, 8, 96)[:, :, 64:96], wdv[:, :, 640:672])])
            wv = v3(wa, 128, 8, 96)
            p = PG()
            for k in range(8):
                MM(p[0:96, :], wv[:, k, :], hT[bl][:, k, :], k == 0, k == 7, [p], [wt, hT[bl]])
            if kind == "p":
                ACT(kr32[64:96, :], p[64:96, :], AF.Copy, [kr32], [p])
                CP(krT[64:96, bs], kr32[64:96, :], [krT], [kr32])
                pp = PG()
                for tt in range(4):
                    TR(pp[:, tt * 32:(tt + 1) * 32], kr32[64:96, tt * 128:(tt + 1) * 128], ident[64:96, 64:96], [pp], [kr32, cst])
                CP(krst[:], pp.ap[:, 0:128].rearrange("p (a b) -> p a b", a=4), [krst], [pp])
                for tt in range(4):
                    si, r0 = tt // 2, (tt % 2) * 128
                    fw.dma("sp", d_out, o_kr[si, i, r0:r0 + 128, :], krst[:, tt, :], r=[krst])
            else:
                rope32(p, bl, krT[64:96, koff + bl * 512:koff + (bl + 1) * 512], krT)
        if kind == "s":
            fw.dma("sp", d_in, ckv_st[:], cm_ckv[i].rearrange("(t p) f -> p t f", p=128), w=[ckv_st])
            MSET(kr_st[:], 0.0, [kr_st])
            fw.dma("sp", d_in, kr_st[:, :, 64:96], cm_kr[i].rearrange("(t p) f -> p t f", p=128), w=[kr_st])
            for c in range(2):
                p = PG()
                for t in range(4):
                    TR(p[:, t * 128:(t + 1) * 128], ckv_st[:, t, c * 128:(c + 1) * 128], ident, [p], [ckv_st, cst])
                EVAC(ckvT[:, c, 0:512], p[:, :], [ckvT], [p])
            p = PG()
            for t in range(4):
                TR(p[0:96, t * 128:(t + 1) * 128], kr_st[:, t, :], ident, [p], [kr_st, cst])
            CP(krT[64:96, 0:512], p[64:96, :], [krT], [p])
        D.close()

        wuq_v = w_mla_uq[i].rearrange("(k p) n -> p k n", p=128)
        wukv_v = w_mla_ukv[i].rearrange("(k p) (h t d) -> p k h t d", p=128, t=2, d=64)
        SCL = float(96.0 ** -0.5)
        nkb = nkeys // 512
        nkt = nkeys // 128
        H = Phase()
        rot["wide"] = False
        QT = H.tile([96, 4, ntok], BF16, "QT")
        KT = H.tile([96, 4, nkeys], BF16, "KT")
        Vg = H.tile([128, nkt, 256], BF16, "Vg")
        Et = [H.tile([128, 512], BF16, "Et") for _ in range(3)]
        dn = H.tile([64, 512], F32, "dn")
        for hg in range(4):
            wt, wv = wl3(wuq_v[:, :, hg * 384:(hg + 1) * 384], 128, 3, 384)
            for hl in range(4):
                for bl in range(nb):
                    bs = slice(bl * 512, (bl + 1) * 512)
                    p = PG()
                    for k in range(3):
                        MM(p[0:96, :], wv[:, k, hl * 96:(hl + 1) * 96], cqn[:, k, bs], k == 0, k == 2, [p], [wt, cqn])
                    if kind == "p":
                        EVAC(QT[0:64, hl, bs], p[0:64, :], [QT], [p])
                        EVAC(QT[64:96, hl, bs], p[64:96, :], [QT], [p])
                    else:
                        ACT(QT[0:64, hl, bs], p[0:64, :], AF.Copy, [QT], [p])
                        rope32(p, bl, QT[64:96, hl, bs], QT)

            def vk(q):
                return q[0:128, 0:512].rearrange("p (k h d) -> p k h d", k=2, h=4)
            wtk, wak = wload([(lambda q, k=k: vk(q)[:, k, :, :], wukv_v[:, k, hg * 4:(hg + 1) * 4, 0, :]) for k in range(2)])
            wvk = vk(wak)
            wtv, wav = wload([(lambda q, k=k: vk(q)[:, k, :, :], wukv_v[:, k, hg * 4:(hg + 1) * 4, 1, :]) for k in range(2)])
            wvv = vk(wav)
            for hl in range(4):
                for kb in range(nkb):
                    p = PG()
                    for k in range(2):
                        MM(p[0:64, :], wvk[:, k, hl, :], ckvT[:, k, kb * 512:(kb + 1) * 512], k == 0, k == 1, [p], [wtk, ckvT])
                    EVAC(KT[0:64, hl, kb * 512:(kb + 1) * 512], p[0:64, :], [KT], [p])
                CP(KT[64:96, hl, :], krT[64:96, :], [KT], [krT])
            for kt in range(nkt):
                p = PG()
                for k in range(2):
                    MM(p[:, 0:256], ckvT[:, k, kt * 128:(kt + 1) * 128], wvv[:, k, :, :].rearrange("p h d -> p (h d)"),
                       k == 0, k == 1, [p], [wtv, ckvT])
                EVAC(Vg[:, kt, :], p[:, 0:256], [Vg], [p])
            eti = 0
            for hl in range(4):
                h = hg * 4 + hl
                if kind == "s":
                    jobs = [(qb * 512, 512, list(range(nkt))) for qb in range(nb)]
                else:
                    jobs = [(s0 * 128, sn * 128, list(range(s0, s0 + sn))) for (s0, sn) in G["seqs"]]
                for (q0, qn, kts) in jobs:
                    po, pd = ACCPAIR()

                    def score_m(ki):
                        kt = kts[ki]
                        ps = PG()
                        MM(ps[:, 0:qn], KT[0:96, hl, kt * 128:(kt + 1) * 128], QT[0:96, hl, q0:q0 + qn], True, True, [ps], [KT, QT])
                        return ps
                    ps_next = score_m(0)
                    for ki, kt in enumerate(kts):
                        ps = ps_next
                        if ki + 1 < len(kts):
                            ps_next = score_m(ki + 1)
                        et = Et[eti % 3]
                        eti += 1
                        ACT(et[:, 0:qn], ps[:, 0:qn], AF.Exp, [et], [ps], scale=SCL)
                        last = ki == len(kts) - 1
                        MM(po[0:64, 0:qn], Vg[:, kt, hl * 64:(hl + 1) * 64], et[:, 0:qn], ki == 0, last, [po], [Vg, et])
                        MM(pd[0:64, 0:qn], ones_bf[:, 2, 0:64], et[:, 0:qn], ki == 0, last, [pd], [ones_bf, et])
                        if ki == min(2, len(kts) - 1):
                            run_deferred()

                    def fin_m(po=po, pd=pd, h=h, q0=q0, qn=qn):
                        ACT(dn[0:64, 0:qn], pd[0:64, 0:qn], AF.Ln, [dn], [pd])
                        ACT(dn[0:64, 0:qn], dn[0:64, 0:qn], AF.Exp, [dn], [dn], scale=-1.0)
                        TT(oT[0:64, h, q0:q0 + qn], po[0:64, 0:qn], dn[0:64, 0:qn], ALU.mult, [oT], [po, dn])
                    defer(fin_m)
        run_deferred()
        rot["wide"] = True
        H.close()

        O = Phase()
        osb = O.tile([128, 8, 512], F32, "osb")
        sq = O.tile([128, 8, 512], BF16, "sqo")
        wo_v = w_mla_o[i].rearrange("(h p) n -> p h n", p=64)
        for bl, b in enumerate(blocks):
            bs = slice(bl * 512, (bl + 1) * 512)
            for n in range(8):
                wt, wv = wl3(wo_v[:, :, n * 128:(n + 1) * 128], 64, 16, 128)
                p = PG()
                for h in range(16):
                    MM(p[:, :], wv[0:64, h, :], oT[0:64, h, bs], h == 0, h == 15, [p], [wt, oT])
                post_evac(n, p, osb, sq)
            post_finish(b, mv, 0, osb, sq)
        O.close()
        E.close()

    GROUPS = [dict(t0=0, nt=4, blocks=[0], seqs=[(0, 2), (2, 2)], kind="p"),
              dict(t0=4, nt=8, blocks=[1, 2], seqs=[(0, 8)], kind="s")]

    for l in range(n_layers):
        mv = modulation(l)
        i = l // 2
        if do_mixer:
            for gi in groups_sel:
                try:
                    if l % 2 == 0:
                        even_mixer(l, i, mv, GROUPS[gi])
                    else:
                        odd_mixer(l, i, mv, GROUPS[gi])
                except Bail:
                    for ph_ in reversed(list(open_phases)):
                        ph_.close()
        if do_mlp:
            M = Phase()
            hid = M.tile([128, 32, 512], BF16, "hid")
            hTm = [M.tile([128, 8, 512], BF16, "hTm") for _ in range(2)]
            osb = M.tile([128, 8, 512], F32, "osbm")
            sqm = M.tile([128, 8, 512], BF16, "sqm")
            sqp = M.tile([128, 8, 512], BF16, "sqp")
            pre(0, mv, 1, hTm[0], sqp)
            for b in range(3):
                nxt = (lambda b=b: pre(b + 1, mv, 1, hTm[(b + 1) % 2], sqp)) if b < 2 else None
                mlp(l, b, mv, hTm[b % 2], hid, osb, sqm, nxt)
            M.close()

    Y = Phase()
    stage = [Y.tile([128, 1024], F32, "ystage") for _ in range(2)]
    for t in range(12):
        stg = stage[t % 2]
        for half in range(2):
            p = PG()
            for cc in range(4):
                c = half * 4 + cc
                TR(p[:, cc * 128:(cc + 1) * 128], xT.ap[:, c, t * 128:(t + 1) * 128], ident, [p], [xTb[t // 4], cst])
            EVAC(stg[:, half * 512:(half + 1) * 512], p[:, :], [stg], [p])
        fw.dma("sp", d_out, y_out[t * 128:(t + 1) * 128, :], stg[:], r=[stg])
    fw.barrier(fw.spool)
    fw.flush(final=True, final_dsems=fw.spool)
    Y.es.close()
    fw.es.close()
    print("program: %d instrs; engine counts:" % fw.ninstr, {n: e.cnt for n, e in fw.engs.items()})
    return nc


_CACHE = {}


def _run(inputs, n_layers=4, do_mixer=True, do_mlp=True, trace=False, groups_sel=(0, 1)):
    key = (n_layers, do_mixer, do_mlp, tuple(groups_sel))
    if key not in _CACHE:
        _CACHE[key] = build_program(n_layers, do_mixer, do_mlp, tuple(groups_sel))
    nc = _CACHE[key]
    f = lambda a: np.ascontiguousarray(np.asarray(a, dtype=np.float32))
    consts, cs64, cs32, pm32 = make_consts()
    xp = f(inputs["x_prompt"])
    xs = f(inputs["x_sample"])
    shared = {k: f(inputs[k]) for k in ("w_mod", "b_mod", "g_norm", "w_ff1", "w_ff2", "w_in_ab", "w_gk_f", "b_gk_f",
                                        "w_gk_b", "b_gk_b", "g_gla", "swa_sink", "w_out_ab", "w_mla_down", "g_mla_q",
                                        "g_mla_kv", "w_mla_uq", "w_mla_ukv", "w_mla_o")}
    shared.update(consts=consts, cs64=cs64, cs32=cs32, pm32=pm32)
    c = f(inputs["c"])
    c_ctx = f(inputs["c_ctx"])
    in_maps = []
    for core in range(NCORES):
        sb = core % 2
        m = dict(shared)
        m["xin"] = np.ascontiguousarray(np.concatenate([xp[2 * core], xp[2 * core + 1], xs[sb]], axis=0))
        m["cvec"] = np.ascontiguousarray(np.stack([c_ctx, c[sb]], axis=0))
        m["st_f"] = f(inputs["state_gla_fwd"][sb])
        m["st_b"] = f(inputs["state_gla_bwd"][sb])
        m["cswa_k"] = f(inputs["cache_swa_k"][sb]).reshape(2, 512, 128)
        m["cswa_v"] = f(inputs["cache_swa_v"][sb]).reshape(2, 512, 128)
        m["cm_ckv"] = f(inputs["cache_mla_ckv"][sb])
        m["cm_kr"] = f(inputs["cache_mla_kr"][sb])
        in_maps.append(m)
    res = run_bass_kernel_spmd(nc, in_maps, core_ids=list(range(NCORES)), trace=trace)
    R = res.results
    y_p = np.stack([R[cidx]["y"][s * 256:(s + 1) * 256] for cidx in range(NCORES) for s in range(2)], axis=0)
    y_s = np.stack([R[b]["y"][512:1536] for b in range(2)], axis=0)

    def gather(name, shape):
        return np.concatenate([R[cidx][name] for cidx in range(NCORES)], axis=0).reshape(shape)
    outs = (y_p.astype(np.float32), y_s.astype(np.float32),
            gather("o_stf", (16, 2, 4, 64, 128)), gather("o_stb", (16, 2, 4, 64, 128)),
            gather("o_swk", (16, 2, 256, 2, 64)), gather("o_swv", (16, 2, 256, 2, 64)),
            gather("o_ckv", (16, 2, 256, 256)), gather("o_kr", (16, 2, 256, 32)))
    return outs, res


def kernel(**inputs):
    outs, _ = _run(inputs)
    return outs
```

```python
from contextlib import ExitStack
import numpy as np
import concourse.bass as bass
import concourse.mybir as mybir
from concourse.bass_utils import run_bass_kernel_spmd

F32 = mybir.dt.float32
BF16 = mybir.dt.bfloat16
AF = mybir.ActivationFunctionType
ALU = mybir.AluOpType

EPS = 1e-6
NCORES = 8


class Eng:
    def __init__(self, name, sem):
        self.name = name
        self.sem = sem
        self.cnt = 0
        self.seen = {}
        self.q = []
        self.pending = []


class DSem:
    def __init__(self, sem, serial=False):
        self.sem = sem
        self.cnt = 0
        self.serial = serial


class T:
    __slots__ = ("ap", "w", "r", "name", "scoped")

    def __init__(self, ap, name=""):
        self.ap = ap
        self.w = None
        self.r = []
        self.name = name
        self.scoped = False

    def __getitem__(self, k):
        return self.ap[k]


class FW:
    def __init__(self, nc):
        self.nc = nc
        self.es = ExitStack()
        self.engs = {}
        for n in ("pe", "act", "dve", "pool", "sp"):
            self.engs[n] = Eng(n, self.es.enter_context(nc.semaphore("s_" + n)))
        self.nd = 0
        self.nt = 0
        self.ninstr = 0
        self.spool = [self.dsem(serial=True) for _ in range(12)]
        self.spi = 0

    def dsem(self, serial=False):
        self.nd += 1
        return DSem(self.es.enter_context(self.nc.semaphore("d%d" % self.nd)), serial)

    def tile(self, shape, dt, name=None, stack=None):
        self.nt += 1
        es = stack if stack is not None else self.es
        t = es.enter_context(self.nc.sbuf_tensor("sb_" + (name or ("t%d" % self.nt)), list(shape), dt))
        return T(t, name or "")

    def ptile(self, shape, dt, name=None):
        self.nt += 1
        t = self.es.enter_context(self.nc.psum_tensor(name or ("p%d" % self.nt), list(shape), dt))
        return T(t, name or "")

    def _need(self, eng, deps):
        best = {}
        for so, v, raw in deps:
            if isinstance(so, DSem) and not so.serial:
                v = so.cnt
            k = id(so)
            if v > best.get(k, (None, 0))[1]:
                best[k] = (so, v)
        for k, (so, v) in best.items():
            if so is eng and eng.name == "pe":
                continue
            if eng.seen.get(k, 0) >= v:
                continue
            eng.seen[k] = v
            eng.q.append(lambda e, sem=so.sem, v=v: e.wait_ge(sem, v))

    @staticmethod
    def _deps(w, r):
        deps = []
        for t in r:
            if t.w is not None:
                deps.append((t.w[0], t.w[1], True))
        for t in w:
            if t.w is not None:
                deps.append((t.w[0], t.w[1], False))
            deps.extend((so, v, False) for so, v in t.r)
        return deps

    @staticmethod
    def _compact(t):
        m = {}
        for so, v in t.r:
            if isinstance(so, DSem) and not so.serial:
                v = so.cnt
            if v > m.get(id(so), (None, 0))[1]:
                m[id(so)] = (so, v)
        t.r = list(m.values())

    def _pool_pending(self, eng, w, r):
        if eng.pending and any(t.scoped for t in list(w) + list(r)):
            for so, v in eng.pending:
                k = id(so)
                if eng.seen.get(k, 0) >= v:
                    continue
                eng.seen[k] = v
                eng.q.append(lambda e, sem=so.sem, v=v: e.wait_ge(sem, v))
            eng.pending = []

    def op(self, en, fn, w=(), r=(), signal=True):
        eng = self.engs[en]
        self._pool_pending(eng, w, r)
        self._need(eng, self._deps(w, r))
        self.ninstr += 1
        if signal:
            eng.cnt += 1
            c = eng.cnt
            eng.q.append(lambda e, fn=fn, sem=eng.sem: fn(e).then_inc(sem, 1))
        else:
            c = eng.cnt + 1
            eng.q.append(lambda e, fn=fn: fn(e))
        for t in w:
            t.w = (eng, c)
            t.r = []
        for t in r:
            t.r.append((eng, c))
            if len(t.r) > 16:
                self._compact(t)

    def dma(self, en, ds, out, in_, w=(), r=()):
        eng = self.engs[en]
        self._pool_pending(eng, w, r)
        if ds is None:
            ds = self.spool[self.spi % len(self.spool)]
            self.spi += 1
            if ds.cnt and eng.seen.get(id(ds), 0) < ds.cnt:
                eng.seen[id(ds)] = ds.cnt
                eng.q.append(lambda e, sem=ds.sem, v=ds.cnt: e.wait_ge(sem, v))
        self._need(eng, self._deps(w, r))
        self.ninstr += 1
        ds.cnt += 16
        c = ds.cnt
        eng.q.append(lambda e, out=out, in_=in_, sem=ds.sem: e.dma_start(out=out, in_=in_).then_inc(sem, 16))
        for t in w:
            t.w = (ds, c)
            t.r = []
        for t in r:
            t.r.append((ds, c))
            if len(t.r) > 16:
                self._compact(t)

    def barrier(self, dsems=()):
        for eng in self.engs.values():
            waits = []
            for other in self.engs.values():
                if other is eng or other.cnt == 0:
                    continue
                waits.append((other, other.cnt))
            for ds in dsems:
                if ds.cnt:
                    waits.append((ds, ds.cnt))
            if eng.name == "pool":
                eng.pending = waits
                continue
            for so, v in waits:
                k = id(so)
                if eng.seen.get(k, 0) >= v:
                    continue
                eng.seen[k] = v
                eng.q.append(lambda e, sem=so.sem, v=v: e.wait_ge(sem, v))

    def flush(self, final=False, final_dsems=()):
        nc = self.nc
        if final:
            sp = self.engs["sp"]
            for ds in final_dsems:
                if ds.cnt:
                    sp.q.append(lambda e, sem=ds.sem, v=ds.cnt: e.wait_ge(sem, v))
            for n in ("pe", "act", "dve", "pool"):
                eg = self.engs[n]
                if eg.cnt:
                    sp.q.append(lambda e, sem=eg.sem, v=eg.cnt: e.wait_ge(sem, v))
        qs = {n: self.engs[n].q for n in self.engs}
        for n in self.engs:
            self.engs[n].q = []
        with nc.Block() as block:
            @block.tensor
            def _(e):
                for f in qs["pe"]:
                    f(e)

            @block.scalar
            def _(e):
                for f in qs["act"]:
                    f(e)

            @block.vector
            def _(e):
                for f in qs["dve"]:
                    f(e)

            @block.gpsimd
            def _(e):
                for f in qs["pool"]:
                    f(e)

            @block.sync
            def _(e):
                for f in qs["sp"]:
                    f(e)


def make_consts():
    s = np.arange(128)[:, None]
    t = np.arange(128)[None, :]
    same = (s // 64) == (t // 64)
    c = np.zeros((10, 128, 128), np.float32)
    c[0] = np.eye(128)
    k = -1.0 / 16.0
    c[1] = k * (same & (s <= t))
    c[2] = k * (same & (s >= t))
    c[3] = k * (same & (s > t))
    c[4] = k * (same & (s < t))
    c[5] = (same & (s <= t))
    c[6] = (same & (s >= t))
    c[7] = (s <= t)
    c[8] = (s >= t)
    pm = np.zeros((128, 128), np.float32)
    for p in range(128):
        q = p % 64
        b = (q % 32) // 16
        partner = p + 16 if b == 0 else p - 16
        pm[partner, p] = 1.0
    c[9] = pm
    L = 1024
    rows = np.repeat(np.arange(L // 64, dtype=np.float32), 64)
    cols = np.tile(np.arange(64, dtype=np.float32), L // 64)
    pos = [rows, cols]

    def tables(hd, nrows, base):
        nf = hd // 4
        inv = (10000.0 ** (-np.arange(nf, dtype=np.float32) / nf)).astype(np.float32)
        cs = np.zeros((2, nrows, L), np.float32)
        for p in range(nrows - base):
            q = p % hd
            a = q // (2 * nf)
            b = (q % (2 * nf)) // nf
            f = q % nf
            ang = (pos[a] * inv[f]).astype(np.float32)
            cs[0, base + p] = np.cos(ang)
            cs[1, base + p] = -np.sin(ang) if b == 0 else np.sin(ang)
        return cs

    cs64 = tables(64, 128, 0)
    cs32 = tables(32, 96, 64)
    pm32 = np.zeros((96, 96), np.float32)
    for q in range(32):
        b = (q % 16) // 8
        partner = q + 8 if b == 0 else q - 8
        pm32[64 + partner, 64 + q] = 1.0
    return c, cs64, cs32, pm32


def build_program(n_layers=4, do_mixer=True, do_mlp=True, groups_sel=(0, 1)):
    nc = bass.Bass("TRN2", target_bir_lowering=False)

    def din(name, shape):
        return nc.dram_tensor(name, list(shape), F32, kind="ExternalInput").ap()

    def dout(name, shape):
        return nc.dram_tensor(name, list(shape), F32, kind="ExternalOutput").ap()

    xin = din("xin", [1536, 1024])
    cvec = din("cvec", [2, 1024])
    st_f = din("st_f", [2, 4, 64, 128])
    st_b = din("st_b", [2, 4, 64, 128])
    cswa_k = din("cswa_k", [2, 512, 128])
    cswa_v = din("cswa_v", [2, 512, 128])
    cm_ckv = din("cm_ckv", [2, 512, 256])
    cm_kr = din("cm_kr", [2, 512, 32])
    w_mod = din("w_mod", [4, 1024, 6144])
    b_mod = din("b_mod", [4, 6144])
    g_norm = din("g_norm", [4, 4, 1024])
    w_ff1 = din("w_ff1", [4, 1024, 4096])
    w_ff2 = din("w_ff2", [4, 4096, 1024])
    w_in_ab = din("w_in_ab", [2, 1024, 2336])
    w_gk_f = din("w_gk_f", [2, 16, 256])
    b_gk_f = din("b_gk_f", [2, 256])
    w_gk_b = din("w_gk_b", [2, 16, 256])
    b_gk_b = din("b_gk_b", [2, 256])
    g_gla = din("g_gla", [2, 128])
    swa_sink = din("swa_sink", [2, 8])
    w_out_ab = din("w_out_ab", [2, 1024, 1024])
    w_mla_down = din("w_mla_down", [2, 1024, 672])
    g_mla_q = din("g_mla_q", [2, 384])
    g_mla_kv = din("g_mla_kv", [2, 256])
    w_mla_uq = din("w_mla_uq", [2, 384, 1536])
    w_mla_ukv = din("w_mla_ukv", [2, 256, 2048])
    w_mla_o = din("w_mla_o", [2, 1024, 1024])
    consts = din("consts", [10, 128, 128])
    cs64_d = din("cs64", [2, 128, 1024])
    cs32_d = din("cs32", [2, 96, 1024])
    pm32_d = din("pm32", [96, 96])

    y_out = dout("y", [1536, 1024])
    o_stf = dout("o_stf", [2, 2, 4, 64, 128])
    o_stb = dout("o_stb", [2, 2, 4, 64, 128])
    o_swk = dout("o_swk", [2, 2, 256, 128])
    o_swv = dout("o_swv", [2, 2, 256, 128])
    o_ckv = dout("o_ckv", [2, 2, 256, 256])
    o_kr = dout("o_kr", [2, 2, 256, 32])

    fw = FW(nc)
    d_in = None
    d_out = None
    uniq = {"i": 0}

    open_phases = []
    phase_stacks = []
    import os as _os
    DBG = int(_os.environ.get("KDBG", "0"))

    class Bail(Exception):
        pass

    def CK(k):
        if DBG == k:
            raise Bail()

    class Phase:
        def __init__(self):
            self.es = ExitStack()
            open_phases.append(self)

        def tile(self, shape, dt, name):
            uniq["i"] += 1
            t = fw.tile(shape, dt, "%s_%d" % (name, uniq["i"]), stack=self.es)
            t.scoped = True
            return t

        def close(self):
            fw.barrier(fw.spool)
            self.es.close()
            open_phases.remove(self)

    PB = [fw.ptile([128, 512], F32, "pb%d" % i) for i in range(8)]
    rot = {"i": 0, "a": 0, "wide": True}

    def PG():
        t = PB[rot["i"] % (8 if rot["wide"] else 4)]
        rot["i"] += 1
        return t

    def ACCPAIR():
        k = rot["a"] % 2
        rot["a"] += 1
        return PB[4 + 2 * k], PB[5 + 2 * k]

    def MM(out, lhsT, rhs, start, stop, w, r):
        fw.op("pe", lambda e: e.matmul(out, lhsT=lhsT, rhs=rhs, start=start, stop=stop), w=w, r=r, signal=stop)

    def TR(out, in_, idn, w, r):
        fw.op("pe", lambda e: e.transpose(out, in_, idn), w=w, r=r)

    def ACT(out, in_, func, w, r, scale=1.0, bias=0.0):
        fw.op("act", lambda e: e.activation(out=out, in_=in_, func=func, bias=bias, scale=scale), w=w, r=r)

    def TT(out, in0, in1, op, w, r, en="dve"):
        fw.op(en, lambda e: e.tensor_tensor(out=out, in0=in0, in1=in1, op=op), w=w, r=r)

    def STT(out, in0, scalar, in1, op0, op1, w, r, en="dve"):
        fw.op(en, lambda e: e.scalar_tensor_tensor(out=out, in0=in0, scalar=scalar, in1=in1, op0=op0, op1=op1), w=w, r=r)

    def TS(out, in0, s1, op0, w, r, en="dve"):
        fw.op(en, lambda e: e.tensor_scalar(out=out, in0=in0, scalar1=s1, scalar2=None, op0=op0), w=w, r=r)

    def CP(out, in_, w, r, en="dve"):
        fw.op(en, lambda e: e.tensor_copy(out=out, in_=in_), w=w, r=r)

    def RCP(out, in_, w, r):
        fw.op("dve", lambda e: e.reciprocal(out=out, in_=in_), w=w, r=r)

    def MSET(ap, val, w, en="dve"):
        fw.op(en, lambda e: e.memset(ap, val), w=w, r=())

    def SPLIT(src, hi, lo, hiT, loT_, srcT):
        CP(hi, src, [hiT], [srcT])
        TT(lo, src, hi, ALU.subtract, [loT_], [srcT, hiT])

    def bc(ap, shape, axis):
        return ap.unsqueeze(axis).to_broadcast(list(shape))

    evac_rr = {"i": 0}
    fin = {"f": None}

    def defer(fn):
        run_deferred()
        fin["f"] = fn

    def run_deferred():
        if fin["f"] is not None:
            f_ = fin["f"]
            fin["f"] = None
            f_()

    def EVAC(out, in_, w, r):
        bi = PB.index(r[0]) if r and r[0] in PB else evac_rr["i"]
        evac_rr["i"] += 1
        if bi % 2:
            ACT(out, in_, AF.Copy, w, r)
        else:
            CP(out, in_, w, r)

    xT = fw.tile([128, 8, 1536], F32, "xT")
    xTb = [T(xT.ap[:, :, b * 512:(b + 1) * 512], "xTb%d" % b) for b in range(3)]
    cst = fw.tile([128, 10, 128], F32, "cst")
    ident = cst.ap[:, 0, :]
    ones_bf = fw.tile([128, 3, 128], BF16, "ones_bf")
    ones_row = fw.tile([1, 128], F32, "ones_row")
    pm32 = fw.tile([96, 96], F32, "pm32")
    cstb = fw.tile([128, 10, 128], BF16, "cstb")
    pm32b = fw.tile([96, 96], BF16, "pm32b")
    ones_rb = fw.tile([1, 128], BF16, "ones_rb")
    wgkh = fw.tile([49, 2, 256], BF16, "wgkh")
    wgkl = fw.tile([49, 2, 256], BF16, "wgkl")
    sinkh = fw.tile([1, 16], BF16, "sinkh")
    sinkl = fw.tile([1, 16], BF16, "sinkl")
    gnT = fw.tile([128, 128], F32, "gnT")
    bmT = fw.tile([128, 192], F32, "bmT")
    smallT = fw.tile([128, 28], F32, "smallT")
    scT = fw.tile([128, 16], BF16, "scT")
    wgk = fw.tile([49, 2, 256], F32, "wgk")
    sink_row = fw.tile([1, 16], F32, "sink_row")
    esink = fw.tile([128, 16], F32, "esink")
    modv = [fw.tile([128, 2, 6, 8], F32, "modv%d" % i) for i in range(2)]
    mraw = fw.tile([128, 2, 48], F32, "mraw")
    rstd_t = [fw.tile([128, 512], F32, "rstd%d" % i) for i in range(2)]
    utmp = [fw.tile([128, 512], F32, "utmp%d" % i) for i in range(3)]
    rr = {"u": 0, "r": 0}

    def UT():
        rr["u"] += 1
        return utmp[rr["u"] % 3]

    NWB = 6
    WBE = 2048
    wbufs = [(fw.tile([128, WBE], BF16, "wb%d" % i), fw.dsem()) for i in range(NWB)]
    wrot = {"i": 0}
    for t_, _ in wbufs:
        MSET(t_[:], 0.0, [t_], en="pool")

    def wload(parts):
        t, ds = wbufs[wrot["i"] % NWB]
        wrot["i"] += 1
        for dstf, src in parts:
            fw.dma("pool", ds, dstf(t.ap), src, w=[t])
        return t, t.ap

    def v3(ap, P, a, b):
        assert a * b <= WBE
        return ap[0:P, 0:a * b].rearrange("p (a b) -> p a b", a=a)

    def wl3(src, P, a, b):
        t, ap = wload([(lambda q: v3(q, P, a, b), src)])
        return t, v3(ap, P, a, b)

    SU = Phase()
    stage = [SU.tile([128, 1024], F32, "stage") for _ in range(2)]
    fw.dma("sp", d_in, cst[:], consts.rearrange("k p n -> p k n"), w=[cst])
    fw.dma("sp", d_in, pm32[:], pm32_d, w=[pm32])
    MSET(ones_bf[:, 0, :], 1.0 / 1024.0, [ones_bf])
    MSET(ones_bf[:, 1, :], 1.0 / 128.0, [ones_bf])
    MSET(ones_bf[:, 2, :], 1.0, [ones_bf])
    MSET(ones_row[:], 1.0, [ones_row])
    MSET(ones_rb[:], 1.0, [ones_rb])
    CP(cstb[:], cst[:], [cstb], [cst])
    CP(pm32b[:], pm32[:], [pm32b], [pm32])
    st0 = stage[0]
    fw.dma("sp", d_in, st0[0:16, 0:128], cvec.rearrange("j (c p) -> (j c) p", p=128), w=[st0])
    fw.dma("sp", d_in, st0[16:18, 0:128], g_gla, w=[st0])
    fw.dma("sp", d_in, st0[18:24, 0:128], g_mla_q.rearrange("i (c p) -> (i c) p", p=128), w=[st0])
    fw.dma("sp", d_in, st0[24:28, 0:128], g_mla_kv.rearrange("i (c p) -> (i c) p", p=128), w=[st0])
    p = PG()
    TR(p[:, 0:28], st0[0:28, 0:128], ident[0:28, 0:28], [p], [st0, cst])
    CP(smallT[:], p[:, 0:28], [smallT], [p])
    ACT(scT[:], smallT[:, 0:16], AF.Silu, [scT], [smallT])
    st1 = stage[1]
    fw.dma("sp", d_in, st1[:, 0:128], g_norm.rearrange("l i (c p) -> (l i c) p", p=128), w=[st1])
    p = PG()
    TR(p[:, 0:128], st1[:, 0:128], ident, [p], [st1, cst])
    CP(gnT[:], p[:, 0:128], [gnT], [p])
    bmv = b_mod.rearrange("l (n p) -> (l n) p", p=128)
    fw.dma("sp", d_in, st0[:, 128:256], bmv[0:128, :], w=[st0])
    fw.dma("sp", d_in, st1[0:64, 128:256], bmv[128:192, :], w=[st1])
    p = PG()
    TR(p[:, 0:128], st0[:, 128:256], ident, [p], [st0, cst])
    TR(p[:, 128:192], st1[0:64, 128:256], ident[0:64, 0:64], [p], [st1, cst])
    CP(bmT[:], p[:, 0:192], [bmT], [p])
    fw.dma("sp", d_in, wgk[0:16, :, :], w_gk_f.rearrange("i k n -> k i n"), w=[wgk])
    fw.dma("sp", d_in, wgk[32:48, :, :], w_gk_b.rearrange("i k n -> k i n"), w=[wgk])
    fw.dma("sp", d_in, wgk[16:17, :, :], b_gk_f.rearrange("(o i) n -> o i n", o=1), w=[wgk])
    fw.dma("sp", d_in, wgk[48:49, :, :], b_gk_b.rearrange("(o i) n -> o i n", o=1), w=[wgk])
    fw.dma("sp", d_in, sink_row[:], swa_sink.rearrange("(o i) h -> o (i h)", o=1), w=[sink_row])
    SPLIT(sink_row[:], sinkh[:], sinkl[:], sinkh, sinkl, sink_row)
    for q0_, q1_ in ((0, 17), (32, 49)):
        SPLIT(wgk[q0_:q1_, :, :], wgkh[q0_:q1_, :, :], wgkl[q0_:q1_, :, :], wgkh, wgkl, wgk)
    p = PG()
    MM(p[:, 0:16], ones_rb[0:1, :], sinkh[0:1, :], True, False, [p], [ones_rb, sinkh])
    MM(p[:, 0:16], ones_rb[0:1, :], sinkl[0:1, :], False, True, [p], [ones_rb, sinkl])
    ACT(esink[:], p[:, 0:16], AF.Exp, [esink], [p])

    for t in range(12):
        stg = stage[t % 2]
        fw.dma("sp", d_in, stg[:], xin[t * 128:(t + 1) * 128, :], w=[stg])
        for half in range(2):
            p = PG()
            for cc in range(4):
                c = half * 4 + cc
                TR(p[:, cc * 128:(cc + 1) * 128], stg[:, c * 128:(c + 1) * 128], ident, [p], [stg, cst])
            EVAC(xT.ap[:, half * 4:half * 4 + 4, t * 128:(t + 1) * 128], p.ap[:, :].rearrange("p (c n) -> p c n", c=4),
                 [xTb[t // 4]], [p])
    SU.close()

    def modulation(l):
        mv = modv[l % 2]
        pm = PG()
        wv = w_mod[l].rearrange("(k p) n -> p k n", p=128)
        scv = scT.ap[:, :].rearrange("p (j k) -> p j k", j=2)
        for pc in range(24):
            wt, wvw = wl3(wv[:, :, pc * 256:(pc + 1) * 256], 128, 8, 256)
            for nn in range(2):
                n = pc * 2 + nn
                for k in range(8):
                    MM(pm[:, n * 2:(n + 1) * 2], wvw[:, k, nn * 128:(nn + 1) * 128], scv[:, :, k], k == 0, k == 7, [pm], [wt, scT])
        pmv = pm.ap[:, 0:96].rearrange("p (n j) -> p n j", j=2)
        for j in range(2):
            TT(mraw[:, j, :], pmv[:, :, j], bmT[:, l * 48:(l + 1) * 48], ALU.add, [mraw], [pm, bmT])
        g = lambda q: gnT[:, l * 32 + q * 8:l * 32 + q * 8 + 8]
        for j in range(2):
            STT(mv[:, j, 0, :], mraw[:, j, 8:16], 1.0, g(0), ALU.add, ALU.mult, [mv], [mraw, gnT])
            CP(mv[:, j, 1, :], mraw[:, j, 0:8], [mv], [mraw])
            TT(mv[:, j, 2, :], mraw[:, j, 16:24], g(1), ALU.mult, [mv], [mraw, gnT])
            STT(mv[:, j, 3, :], mraw[:, j, 32:40], 1.0, g(2), ALU.add, ALU.mult, [mv], [mraw, gnT])
            CP(mv[:, j, 4, :], mraw[:, j, 24:32], [mv], [mraw])
            TT(mv[:, j, 5, :], mraw[:, j, 40:48], g(3), ALU.mult, [mv], [mraw, gnT])
        return mv

    def rstd_of(sq_aps, sq_T, ones_ap, N):
        ss = PG()
        n = len(sq_aps)
        for c, a in enumerate(sq_aps):
            MM(ss[:, 0:N], ones_ap, a, c == 0, c == n - 1, [ss], [ones_bf] + sq_T)
        rr["r"] += 1
        rs = rstd_t[rr["r"] % 2]
        ACT(rs[:, 0:N], ss[:, 0:N], AF.Ln, [rs], [ss], bias=EPS)
        ACT(rs[:, 0:N], rs[:, 0:N], AF.Exp, [rs], [rs], scale=-0.5)
        return rs

    def pre(b, mv, which, dst, sq):
        j = 0 if b == 0 else 1
        A = mv.ap[:, j, 3 * which + 0, :]
        Sh = mv.ap[:, j, 3 * which + 1, :]
        xb = xTb[b]
        ACT(sq[:, 0:4, :], xb[:, 0:4, :], AF.Square, [sq], [xb])
        TT(sq[:, 4:8, :], xb[:, 4:8, :], xb[:, 4:8, :], ALU.mult, [sq], [xb])
        rs = rstd_of([sq[:, c, :] for c in range(8)], [sq], ones_bf[:, 0, :], 512)
        for c in range(8):
            u = UT()
            TT(u[:], xb[:, c, :], rs[:, :], ALU.mult, [u], [xb, rs])
            ACT(dst[:, c, :], u[:], AF.Identity, [dst], [u, mv], scale=A[:, c:c + 1], bias=Sh[:, c:c + 1])

    def post_evac(n, p, osb, sq):
        ACT(osb[:, n, :], p[:, :], AF.Copy, [osb], [p])
        TT(sq[:, n, :], osb[:, n, :], osb[:, n, :], ALU.mult, [sq], [osb])

    def post_finish(b, mv, which, osb, sq):
        j = 0 if b == 0 else 1
        G = mv.ap[:, j, 3 * which + 2, :]
        xb = xTb[b]
        rs = rstd_of([sq[:, c, :] for c in range(8)], [sq], ones_bf[:, 0, :], 512)
        for c in range(8):
            u = UT()
            STT(u[:], osb[:, c, :], G[:, c:c + 1], rs[:, :], ALU.mult, ALU.mult, [u], [osb, rs, mv])
            TT(xb[:, c, :], xb[:, c, :], u[:], ALU.add, [xb], [xb, u])

    def mlp(l, b, mv, hT, hid, osb, sq, next_pre=None):
        w1v = w_ff1[l].rearrange("(k p) n -> p k n", p=128)
        for jp in range(16):
            wt, wv = wl3(w1v[:, :, jp * 256:(jp + 1) * 256], 128, 8, 256)
            for jj in range(2):
                jn = jp * 2 + jj
                ph_ = PG()
                for k in range(8):
                    MM(ph_[:, :], wv[:, k, jj * 128:(jj + 1) * 128], hT[:, k, :], k == 0, k == 7, [ph_], [wt, hT])
                ACT(hid[:, jn, :], ph_[:, :], AF.Relu, [hid], [ph_])
                TT(hid[:, jn, :], hid[:, jn, :], hid[:, jn, :], ALU.mult, [hid], [hid])
        if next_pre is not None:
            next_pre()
        w2v = w_ff2[l].rearrange("(j p) n -> p j n", p=128)
        for n in range(8):
            po = PG()
            for jh in range(2):
                wt, wv = wl3(w2v[:, jh * 16:(jh + 1) * 16, n * 128:(n + 1) * 128], 128, 16, 128)
                for jj in range(16):
                    jn = jh * 16 + jj
                    MM(po[:, :], wv[:, jj, :], hid[:, jn, :], jn == 0, jn == 31, [po], [wt, hid])
            post_evac(n, po, osb, sq)
        post_finish(b, mv, 1, osb, sq)

    def even_mixer(l, i, mv, G):
        kind, nt, blocks = G["kind"], G["nt"], G["blocks"]
        ntok = nt * 128
        nb = len(blocks)
        tg0 = G["t0"]
        E = Phase()
        hT = [E.tile([128, 8, 512], BF16, "hT") for _ in range(nb)]
        sq = E.tile([128, 8, 512], BF16, "sq")
        loTh = E.tile([49, ntok], BF16, "loTh")
        loTl = E.tile([49, ntok], BF16, "loTl")
        lotmp = E.tile([48, 512], F32, "lotmp")
        for q0_ in (0, 32):
            MSET(loTh[q0_:q0_ + 17, :], 1.0, [loTh])
            MSET(loTl[q0_:q0_ + 17, :], 0.0, [loTl])
        mixT = E.tile([128, 4, ntok], BF16, "mixT")
        winv = w_in_ab[i].rearrange("(k p) n -> p k n", p=128)
        for bl, b in enumerate(blocks):
            pre(b, mv, 0, hT[bl], sq)
        wt, wa = wload([(lambda q: v3(q, 128, 8, 48)[:, :, 0:16], winv[:, :, 1536:1552]),
                        (lambda q: v3(q, 128, 8, 48)[:, :, 32:48], winv[:, :, 1552:1568])])
        wv = v3(wa, 128, 8, 48)
        for bl in range(nb):
            p = PG()
            for k in range(8):
                MM(p[0:48, :], wv[:, k, :], hT[bl][:, k, :], k == 0, k == 7, [p], [wt, hT[bl]])
            ACT(lotmp[0:48, :], p[0:48, :], AF.Copy, [lotmp], [p])
            for q0_ in (0, 32):
                SPLIT(lotmp[q0_:q0_ + 16, :], loTh[q0_:q0_ + 16, bl * 512:(bl + 1) * 512],
                      loTl[q0_:q0_ + 16, bl * 512:(bl + 1) * 512], loTh, loTl, lotmp)

        CK(1)
        S = Phase()
        qaT = S.tile([128, ntok], BF16, "qaT")
        kaT = S.tile([128, ntok], BF16, "kaT")
        sgT = S.tile([128, 2, ntok], BF16, "sgT")
        katok = S.tile([128, nt, 128], BF16, "katok")
        vatok = S.tile([128, nt, 256], BF16, "vatok")
        qeS = S.tile([128, nt, 2, 128], BF16, "qeS")
        attS = S.tile([128, nt, 2, 2, 128], BF16, "attS")
        Ust = S.tile([128, nt, 2, 2, 128], F32, "Ust")
        SbfF = S.tile([128, nt, 2, 128], BF16, "SbfF")
        SbfB = S.tile([128, nt, 2, 128], BF16, "SbfB")
        dsc = S.tile([128, nt, 2, 2], F32, "dsc")
        Sf = S.tile([128, 128], F32, "Sf")
        Sb = S.tile([128, 128], F32, "Sb")
        etmp = [S.tile([128, 256], F32, "etmp") for _ in range(2)]
        g2 = [S.tile([128, 2, 128], F32, "g2") for _ in range(2)]
        ggh = [S.tile([128, 2, 128], BF16, "ggh") for _ in range(2)]
        ggl = [S.tile([128, 2, 128], BF16, "ggl") for _ in range(2)]
        Epos = [S.tile([128, 2, 128], F32, "Epos") for _ in range(2)]
        Eneg = [S.tile([128, 2, 128], F32, "Eneg") for _ in range(2)]
        ED = [S.tile([128, 2, 128], F32, "ED") for _ in range(2)]
        ke = [S.tile([128, 2, 128], BF16, "ke") for _ in range(2)]
        kd = [S.tile([128, 2, 128], BF16, "kd") for _ in range(2)]
        osb2 = [S.tile([128, 256], F32, "osb2") for _ in range(2)]
        sq2 = [S.tile([128, 256], BF16, "sq2") for _ in range(2)]
        for ch in range(2):
            wt, wa = wload([(lambda q: v3(q, 128, 8, 256)[:, :, 0:128], winv[:, :, ch * 128:(ch + 1) * 128]),
                            (lambda q: v3(q, 128, 8, 256)[:, :, 128:256], winv[:, :, 256 + ch * 128:256 + (ch + 1) * 128])])
            wv = v3(wa, 128, 8, 256)
            for bl in range(nb):
                p = PG()
                for k in range(8):
                    MM(p[:, :], wv[:, k, 0:128], hT[bl][:, k, :], k == 0, k == 7, [p], [wt, hT[bl]])
                ACT(qaT[:, bl * 512:(bl + 1) * 512], p[:, :], AF.Copy, [qaT], [p], scale=0.125)
                p = PG()
                for k in range(8):
                    MM(p[:, :], wv[:, k, 128:256], hT[bl][:, k, :], k == 0, k == 7, [p], [wt, hT[bl]])
                CP(kaT[:, bl * 512:(bl + 1) * 512], p[:, :], [kaT], [p])
            CK(11)
            wt, wv = wl3(winv[:, :, 1024 + ch * 256:1024 + (ch + 1) * 256], 128, 8, 256)
            for hl in range(2):
                for bl in range(nb):
                    p = PG()
                    for k in range(8):
                        MM(p[:, :], wv[:, k, hl * 128:(hl + 1) * 128], hT[bl][:, k, :], k == 0, k == 7, [p], [wt, hT[bl]])
                    ACT(sgT[:, hl, bl * 512:(bl + 1) * 512], p[:, :], AF.Silu, [sgT], [p])
            CK(12)
            wt1, wv1 = wl3(winv[:, :, 256 + ch * 128:256 + (ch + 1) * 128], 128, 8, 128)
            wt2, wv2 = wl3(winv[:, :, 512 + ch * 256:512 + (ch + 1) * 256], 128, 8, 256)
            CK(13)
            for tl in range(nt):
                if tl == 1:
                    CK(14)
                bl, tt = tl // 4, tl % 4
                p = PG()
                for k in range(8):
                    MM(p[:, 0:128], hT[bl][:, k, tt * 128:(tt + 1) * 128], wv1[:, k, :], k == 0, k == 7, [p], [wt1, hT[bl]])
                CP(katok[:, tl, :], p[:, 0:128], [katok], [p])
                p = PG()
                for k in range(8):
                    MM(p[:, 0:256], hT[bl][:, k, tt * 128:(tt + 1) * 128], wv2[:, k, :], k == 0, k == 7, [p], [wt2, hT[bl]])
                ACT(vatok[:, tl, :], p[:, 0:256], AF.Copy, [vatok], [p])

            CK(2)
            for si, (s0, sn) in enumerate(G["seqs"]):
                tiles = list(range(s0, s0 + sn))
                if kind == "s":
                    fw.dma("sp", d_in, Sf[:], st_f[i, 2 * ch:2 * ch + 2].rearrange("h k v -> (h k) v"), w=[Sf])
                    fw.dma("sp", d_in, Sb[:], st_b[i, 2 * ch:2 * ch + 2].rearrange("h k v -> (h k) v"), w=[Sb])
                else:
                    MSET(Sf[:], 0.0, [Sf])
                    MSET(Sb[:], 0.0, [Sb])
                def stA1(tl):
                    tsl = slice(tl * 128, (tl + 1) * 128)
                    pb_ = tl % 2
                    gg = g2[pb_]
                    for d in range(2):
                        base = 0 if d == 0 else 32
                        bsl = slice(base, base + 17)
                        csl = slice(ch * 128, (ch + 1) * 128)
                        pz = PG()
                        zr = pz[:, 0:128]
                        MM(zr, loTh[bsl, tsl], wgkh[bsl, i, csl], True, False, [pz], [loTh, wgkh])
                        MM(zr, loTh[bsl, tsl], wgkl[bsl, i, csl], False, False, [pz], [loTh, wgkl])
                        MM(zr, loTl[bsl, tsl], wgkh[bsl, i, csl], False, True, [pz], [loTl, wgkh])
                        ACT(etmp[pb_][:, d * 128:(d + 1) * 128], pz[:, 0:128], AF.Exp, [etmp[pb_]], [pz], scale=-1.0)
                    ACT(gg.ap[:, :, :].rearrange("p a b -> p (a b)"), etmp[pb_][:], AF.Ln, [gg], [etmp[pb_]], bias=1.0)
                    SPLIT(gg[:, :, :], ggh[pb_][:, :, :], ggl[pb_][:, :, :], ggh[pb_], ggl[pb_], gg)

                def stA2(tl):
                    pb_ = tl % 2
                    for d in range(2):
                        pc = PG()
                        MM(pc[:, 0:128], ggh[pb_][:, d, :], cstb[:, 1 + d, :], True, False, [pc], [ggh[pb_], cstb])
                        MM(pc[:, 0:128], ggl[pb_][:, d, :], cstb[:, 1 + d, :], False, True, [pc], [ggl[pb_], cstb])
                        ACT(Epos[pb_][:, d, :], pc[:, 0:128], AF.Exp, [Epos[pb_]], [pc])
                        ACT(Eneg[pb_][:, d, :], pc[:, 0:128], AF.Exp, [Eneg[pb_]], [pc], scale=-1.0)
                    for d in range(2):
                        pc = PG()
                        MM(pc[:, 0:128], cstb[:, 3 + d, :], ggh[pb_][:, d, :], True, False, [pc], [ggh[pb_], cstb])
                        MM(pc[:, 0:128], cstb[:, 3 + d, :], ggl[pb_][:, d, :], False, True, [pc], [ggl[pb_], cstb])
                        ACT(ED[pb_][:, d, :], pc[:, 0:128], AF.Exp, [ED[pb_]], [pc])

                def stA3(tl):
                    tsl = slice(tl * 128, (tl + 1) * 128)
                    pb_ = tl % 2
                    TT(qeS[:, tl, :, :], bc(qaT[:, tsl], [128, 2, 128], 1), Epos[pb_][:], ALU.mult, [qeS], [qaT, Epos[pb_]])
                    TT(ke[pb_][:], bc(kaT[:, tsl], [128, 2, 128], 1), Eneg[pb_][:], ALU.mult, [ke[pb_]], [kaT, Eneg[pb_]])
                    TT(kd[pb_][:], bc(katok[:, tl, :], [128, 2, 128], 1), ED[pb_][:], ALU.mult, [kd[pb_]], [katok, ED[pb_]])
                    CP(dsc[:, tl, 0, :], Epos[pb_][:, 0, 63:128:64], [dsc], [Epos[pb_]])
                    CP(dsc[:, tl, 1, :], Epos[pb_][:, 1, 0:128:64], [dsc], [Epos[pb_]])

                def stA4(tl):
                    pb_ = tl % 2
                    for hl in range(2):
                        for d in range(2):
                            pa = PG()
                            MM(pa[:, 0:128], ke[pb_][hl * 64:(hl + 1) * 64, d, :],
                               qeS[hl * 64:(hl + 1) * 64, tl, d, :], True, True, [pa], [ke[pb_], qeS])
                            TT(attS[:, tl, hl, d, :], pa[:, 0:128], cst[:, 5 + d, :], ALU.mult, [attS], [pa, cst])

                def stA5(tl):
                    pb_ = tl % 2
                    for half in range(2):
                        for d in range(2):
                            pu = PG()
                            MM(pu[:, 0:256], kd[pb_][half * 64:(half + 1) * 64, d, :],
                               vatok[half * 64:(half + 1) * 64, tl, :], True, True, [pu], [kd[pb_], vatok])
                            EVAC(Ust[0:64, tl, d, half, :], pu[0:64, 0:128], [Ust], [pu])
                            EVAC(Ust[64:128, tl, d, half, :], pu[64:128, 128:256], [Ust], [pu])

                for t0_ in range(0, len(tiles), 2):
                    pair = tiles[t0_:t0_ + 2]
                    for st_ in (stA1, stA2, stA3, stA4, stA5):
                        for tl in pair:
                            st_(tl)
                CK(3)
                for tl in tiles:
                    for half in (0, 1):
                        CP(SbfF[:, tl, half, :], Sf[:], [SbfF], [Sf])
                        STT(Sf[:], Sf[:], dsc[:, tl, 0, half:half + 1], Ust[:, tl, 0, half, :], ALU.mult, ALU.add, [Sf], [Sf, dsc, Ust])
                for tl in reversed(tiles):
                    for half in (1, 0):
                        CP(SbfB[:, tl, half, :], Sb[:], [SbfB], [Sb])
                        STT(Sb[:], Sb[:], dsc[:, tl, 1, half:half + 1], Ust[:, tl, 1, half, :], ALU.mult, ALU.add, [Sb], [Sb, dsc, Ust])
                if kind == "p":
                    fw.dma("sp", d_out, o_stf[si, i, 2 * ch:2 * ch + 2].rearrange("h k v -> (h k) v"), Sf[:], r=[Sf])
                    fw.dma("sp", d_out, o_stb[si, i, 2 * ch:2 * ch + 2].rearrange("h k v -> (h k) v"), Sb[:], r=[Sb])
                CK(4)
                def stB1(tl):
                    pb_ = tl % 2
                    for hl in range(2):
                        hs = slice(hl * 64, (hl + 1) * 64)
                        po = PG()
                        reg = po[:, 0:128]
                        MM(reg, vatok[:, tl, hl * 128:(hl + 1) * 128], attS[:, tl, hl, 0, :], True, False, [po], [vatok, attS])
                        MM(reg, vatok[:, tl, hl * 128:(hl + 1) * 128], attS[:, tl, hl, 1, :], False, False, [po], [vatok, attS])
                        for half in range(2):
                            MM(po[:, half * 64:half * 64 + 64], SbfF[hs, tl, half, :],
                               qeS[hs, tl, 0, half * 64:(half + 1) * 64], False, False, [po], [SbfF, qeS])
                        for half in range(2):
                            MM(po[:, half * 64:half * 64 + 64], SbfB[hs, tl, half, :],
                               qeS[hs, tl, 1, half * 64:(half + 1) * 64], False, half == 1, [po], [SbfB, qeS])
                        EVAC(osb2[pb_][:, hl * 128:(hl + 1) * 128], po[:, 0:128], [osb2[pb_]], [po])
                    TT(sq2[pb_][:], osb2[pb_][:], osb2[pb_][:], ALU.mult, [sq2[pb_]], [osb2[pb_]])

                def stB2(tl):
                    tsl = slice(tl * 128, (tl + 1) * 128)
                    pb_ = tl % 2
                    rs = rstd_of([sq2[pb_][:, :]], [sq2[pb_]], ones_bf[:, 1, :], 256)
                    STT(osb2[pb_][:], osb2[pb_][:], smallT[:, 16 + i:17 + i], rs[:, 0:256], ALU.mult, ALU.mult,
                        [osb2[pb_]], [osb2[pb_], rs, smallT])
                    TT(mixT[:, 2 * ch:2 * ch + 2, tsl], osb2[pb_].ap[:, :].rearrange("p (a b) -> p a b", a=2), sgT[:, :, tsl],
                       ALU.mult, [mixT], [osb2[pb_], sgT])

                for t0_ in range(0, len(tiles), 2):
                    pair = tiles[t0_:t0_ + 2]
                    for st_ in (stB1, stB2):
                        for tl in pair:
                            st_(tl)
        S.close()

        CK(5)
        oswaT = E.tile([64, 8, ntok], BF16, "oswaT")
        W = Phase()
        qbT = W.tile([128, 4, ntok], BF16, "qbT")
        kbT = W.tile([128, ntok], BF16, "kbT")
        vbtok = W.tile([128, nt, 128], BF16, "vbtok")
        Et = [W.tile([128, 512], BF16, "Et") for _ in range(3)]
        dn = W.tile([64, 512], F32, "dn")
        if kind == "s":
            rt = W.tile([128, 2, 1024], F32, "rt")
            fw.dma("sp", d_in, rt[:], cs64_d.rearrange("k p n -> p k n"), w=[rt])
            xf = W.tile([128, 512], F32, "xf")
            xfh = W.tile([128, 512], BF16, "xfh")
            xfl = W.tile([128, 512], BF16, "xfl")
            t1 = W.tile([128, 512], F32, "t1")
            kc_st = W.tile([128, 4, 128], F32, "kc_st")
            vc_st = W.tile([128, 4, 128], F32, "vc_st")
            kcT = W.tile([128, 512], BF16, "kcT")
            vc = W.tile([128, 4, 128], BF16, "vc")
        else:
            kvst = [W.tile([128, 256], F32, "kvst") for _ in range(2)]

        def rope64(p, bl, dst, dstT):
            ACT(xf[:], p[:, :], AF.Copy, [xf], [p])
            SPLIT(xf[:], xfh[:], xfl[:], xfh, xfl, xf)
            pp = PG()
            MM(pp[:, :], cstb[:, 9, :], xfh[:], True, False, [pp], [cstb, xfh])
            MM(pp[:, :], cstb[:, 9, :], xfl[:], False, True, [pp], [cstb, xfl])
            TT(t1[:], xf[:], rt[:, 0, bl * 512:(bl + 1) * 512], ALU.mult, [t1], [xf, rt])
            TT(xf[:], pp[:, :], rt[:, 1, bl * 512:(bl + 1) * 512], ALU.mult, [xf], [pp, rt])
            TT(dst, t1[:], xf[:], ALU.add, [dstT], [t1, xf])

        for ap_ in range(2):
            def v5(q):
                return q[0:128, 0:2048].rearrange("p (k a g d) -> p k a g d", k=8, a=2, g=2)
            wt, wa = wload([(lambda q, g=g, al=al: v5(q)[:, :, al, g, :],
                             winv[:, :, 1568 + g * 256 + (ap_ * 2 + al) * 64:1568 + g * 256 + (ap_ * 2 + al + 1) * 64])
                            for g in range(2) for al in range(2)])
            wv = v5(wa)
            for al in range(2):
                a = ap_ * 2 + al
                for bl in range(nb):
                    p = PG()
                    for k in range(8):
                        MM(p[:, :], wv[:, k, al, :, :].rearrange("p g d -> p (g d)"), hT[bl][:, k, :], k == 0, k == 7, [p], [wt, hT[bl]])
                    if kind == "p":
                        EVAC(qbT[:, a, bl * 512:(bl + 1) * 512], p[:, :], [qbT], [p])
                    else:
                        rope64(p, bl, qbT[:, a, bl * 512:(bl + 1) * 512], qbT)
        wt, wv = wl3(winv[:, :, 2080:2208], 128, 8, 128)
        for bl in range(nb):
            p = PG()
            for k in range(8):
                MM(p[:, :], wv[:, k, :], hT[bl][:, k, :], k == 0, k == 7, [p], [wt, hT[bl]])
            if kind == "p":
                EVAC(kbT[:, bl * 512:(bl + 1) * 512], p[:, :], [kbT], [p])
            else:
                rope64(p, bl, kbT[:, bl * 512:(bl + 1) * 512], kbT)
        wt, wv = wl3(winv[:, :, 2080:2336], 128, 8, 256)
        for tl in range(nt):
            bl, tt = tl // 4, tl % 4
            p = PG()
            for k in range(8):
                MM(p[:, 0:256], hT[bl][:, k, tt * 128:(tt + 1) * 128], wv[:, k, :], k == 0, k == 7, [p], [wt, hT[bl]])
            if kind == "p":
                ks = kvst[tl % 2]
                ACT(ks[:], p[:, 0:256], AF.Copy, [ks], [p])
                CP(vbtok[:, tl, :], ks[:, 128:256], [vbtok], [ks])
            else:
                CP(vbtok[:, tl, :], p[:, 128:256], [vbtok], [p])
            if kind == "p":
                si, r0 = tl // 2, (tl % 2) * 128
                fw.dma("sp", d_out, o_swk[si, i, r0:r0 + 128, :], ks[:, 0:128], r=[ks])
                fw.dma("sp", d_out, o_swv[si, i, r0:r0 + 128, :], ks[:, 128:256], r=[ks])
        if kind == "s":
            fw.dma("sp", d_in, kc_st[:], cswa_k[i].rearrange("(t p) f -> p t f", p=128), w=[kc_st])
            fw.dma("sp", d_in, vc_st[:], cswa_v[i].rearrange("(t p) f -> p t f", p=128), w=[vc_st])
            p = PG()
            for t in range(4):
                TR(p[:, t * 128:(t + 1) * 128], kc_st[:, t, :], ident, [p], [kc_st, cst])
            CP(kcT[:], p[:, :], [kcT], [p])
            CP(vc[:], vc_st[:], [vc], [vc_st])
        CK(6)
        rot["wide"] = False
        SC = 0.125
        eti = 0
        if kind == "p":
            for si, (s0, sn) in enumerate(G["seqs"]):
                q0 = s0 * 128
                for kv in range(2):
                    ks_ = slice(kv * 64, (kv + 1) * 64)
                    for apair in range(2):
                        po, pd = ACCPAIR()

                        def score_p(kt):
                            tl = s0 + kt
                            ps = PG()
                            MM(ps[:, :], kbT[ks_, tl * 128:(tl + 1) * 128], qbT[ks_, 2 * apair:2 * apair + 2, q0:q0 + 256],
                               True, True, [ps], [kbT, qbT])
                            return ps
                        ps_next = score_p(0)
                        for kt in range(sn):
                            tl = s0 + kt
                            ps = ps_next
                            if kt + 1 < sn:
                                ps_next = score_p(kt + 1)
                            et = Et[eti % 3]
                            eti += 1
                            ACT(et[:], ps[:, :], AF.Exp, [et], [ps], scale=SC)
                            MM(po[0:64, :], vbtok[:, tl, kv * 64:(kv + 1) * 64], et[:], kt == 0, kt == sn - 1, [po], [vbtok, et])
                            MM(pd[0:64, :], ones_bf[:, 2, 0:64], et[:], kt == 0, kt == sn - 1, [pd], [ones_bf, et])
                            if kt == min(1, sn - 1):
                                run_deferred()

                        def fin_p(po=po, pd=pd, kv=kv, apair=apair, q0=q0):
                            for al in range(2):
                                col = i * 8 + kv * 4 + apair * 2 + al
                                TS(dn[0:64, al * 256:(al + 1) * 256], pd[0:64, al * 256:(al + 1) * 256], esink[0:64, col:col + 1],
                                   ALU.add, [dn], [pd, esink])
                            ACT(dn[:], dn[:], AF.Ln, [dn], [dn])
                            ACT(dn[:], dn[:], AF.Exp, [dn], [dn], scale=-1.0)
                            h0 = kv * 4 + apair * 2
                            TT(oswaT[0:64, h0:h0 + 2, q0:q0 + 256], po.ap[0:64, :].rearrange("p (a b) -> p a b", a=2),
                               dn.ap[0:64, :].rearrange("p (a b) -> p a b", a=2), ALU.mult, [oswaT], [po, dn])
                        defer(fin_p)
        else:
            for qt in range(nt):
                for kv in range(2):
                    ks_ = slice(kv * 64, (kv + 1) * 64)
                    keys = [("c", t) for t in range(4)]
                    if qt > 0:
                        keys.append(("l", qt - 1))
                    keys.append(("l", qt))
                    if qt < nt - 1:
                        keys.append(("l", qt + 1))
                    po, pd = ACCPAIR()

                    def score_s(ki):
                        kk, kt = keys[ki]
                        ps = PG()
                        if kk == "c":
                            MM(ps[:, :], kcT[ks_, kt * 128:(kt + 1) * 128], qbT[ks_, :, qt * 128:(qt + 1) * 128], True, True, [ps], [kcT, qbT])
                        else:
                            MM(ps[:, :], kbT[ks_, kt * 128:(kt + 1) * 128], qbT[ks_, :, qt * 128:(qt + 1) * 128], True, True, [ps], [kbT, qbT])
                        return ps
                    ps_next = score_s(0)
                    for ki, (kk, kt) in enumerate(keys):
                        ps = ps_next
                        if ki + 1 < len(keys):
                            ps_next = score_s(ki + 1)
                        et = Et[eti % 3]
                        eti += 1
                        ACT(et[:], ps[:, :], AF.Exp, [et], [ps], scale=SC)
                        if kk == "l" and kt != qt:
                            mk = cst[:, 8, :] if kt < qt else cst[:, 7, :]
                            etv = et.ap[:, :].rearrange("p (a b) -> p a b", a=4)
                            TT(etv, etv, bc(mk, [128, 4, 128], 1), ALU.mult, [et], [et, cst])
                        vsrc, vT = (vc[:, kt, kv * 64:(kv + 1) * 64], vc) if kk == "c" else (vbtok[:, kt, kv * 64:(kv + 1) * 64], vbtok)
                        last = ki == len(keys) - 1
                        MM(po[0:64, :], vsrc, et[:], ki == 0, last, [po], [vT, et])
                        MM(pd[0:64, :], ones_bf[:, 2, 0:64], et[:], ki == 0, last, [pd], [ones_bf, et])
                        if ki == 2:
                            run_deferred()

                    def fin_s(po=po, pd=pd, kv=kv, qt=qt):
                        c0 = i * 8 + kv * 4
                        dnv = dn.ap[0:64, :].rearrange("p (a b) -> p a b", a=4)
                        TT(dnv, pd.ap[0:64, :].rearrange("p (a b) -> p a b", a=4), bc(esink[0:64, c0:c0 + 4], [64, 4, 128], 2),
                           ALU.add, [dn], [pd, esink])
                        ACT(dn[:], dn[:], AF.Ln, [dn], [dn])
                        ACT(dn[:], dn[:], AF.Exp, [dn], [dn], scale=-1.0)
                        TT(oswaT[0:64, kv * 4:(kv + 1) * 4, qt * 128:(qt + 1) * 128],
                           po.ap[0:64, :].rearrange("p (a b) -> p a b", a=4), dnv, ALU.mult, [oswaT], [po, dn])
                    defer(fin_s)
        run_deferred()
        rot["wide"] = True
        CK(7)

        O = W
        osb = O.tile([128, 8, 512], F32, "osb")
        woa_v = w_out_ab[i][0:512, :].rearrange("(h p) n -> p h n", p=128)
        wob_v = w_out_ab[i][512:1024, :].rearrange("(h p) n -> p h n", p=64)
        for bl, b in enumerate(blocks):
            bs = slice(bl * 512, (bl + 1) * 512)
            for npc in range(4):
                wta, wva = wl3(woa_v[:, :, npc * 256:(npc + 1) * 256], 128, 4, 256)
                wtb, wvb = wl3(wob_v[:, :, npc * 256:(npc + 1) * 256], 64, 8, 256)
                for nn in range(2):
                    n = npc * 2 + nn
                    p = PG()
                    for h in range(4):
                        MM(p[:, :], wva[:, h, nn * 128:(nn + 1) * 128], mixT[:, h, bs], h == 0, False, [p], [wta, mixT])
                    for h in range(8):
                        MM(p[:, :], wvb[0:64, h, nn * 128:(nn + 1) * 128], oswaT[0:64, h, bs], False, h == 7, [p], [wtb, oswaT])
                    post_evac(n, p, osb, sq)
            post_finish(b, mv, 0, osb, sq)
        O.close()
        E.close()

    def odd_mixer(l, i, mv, G):
        kind, nt, blocks = G["kind"], G["nt"], G["blocks"]
        ntok = nt * 128
        nb = len(blocks)
        koff = 512 if kind == "s" else 0
        nkeys = ntok + koff
        E = Phase()
        cqn = E.tile([128, 3, ntok], BF16, "cqn")
        ckvT = E.tile([128, 2, nkeys], BF16, "ckvT")
        krT = E.tile([96, nkeys], BF16, "krT")
        oT = E.tile([64, 16, ntok], BF16, "oT")
        t1 = E.tile([96, 512], F32, "t1")
        kr32 = E.tile([96, 512], F32, "kr32")
        krh = E.tile([96, 512], BF16, "krh")
        krl = E.tile([96, 512], BF16, "krl")
        if kind == "s":
            rt32 = E.tile([96, 2, 1024], F32, "rt32")
            fw.dma("sp", d_in, rt32[:], cs32_d.rearrange("k p n -> p k n"), w=[rt32])
        wdv = w_mla_down[i].rearrange("(k p) n -> p k n", p=128)

        def rope32(p, bl, dst, dstT):
            ACT(kr32[64:96, :], p[64:96, :], AF.Copy, [kr32], [p])
            SPLIT(kr32[64:96, :], krh[64:96, :], krl[64:96, :], krh, krl, kr32)
            pp = PG()
            MM(pp[0:96, :], pm32b[64:96, 0:96], krh[64:96, :], True, False, [pp], [pm32b, krh])
            MM(pp[0:96, :], pm32b[64:96, 0:96], krl[64:96, :], False, True, [pp], [pm32b, krl])
            TT(t1[64:96, :], kr32[64:96, :], rt32[64:96, 0, bl * 512:(bl + 1) * 512], ALU.mult, [t1], [kr32, rt32])
            TT(kr32[64:96, :], pp[64:96, :], rt32[64:96, 1, bl * 512:(bl + 1) * 512], ALU.mult, [kr32], [pp, rt32])
            TT(dst, t1[64:96, :], kr32[64:96, :], ALU.add, [dstT], [t1, kr32])

        D = Phase()
        hT = [D.tile([128, 8, 512], BF16, "hT") for _ in range(nb)]
        sq = D.tile([128, 8, 512], BF16, "sq")
        for bl, b in enumerate(blocks):
            pre(b, mv, 0, hT[bl], sq)
        cq32 = D.tile([128, 3, 512], F32, "cq32")
        ckv32 = D.tile([128, 2, 512], F32, "ckv32")
        if kind == "p":
            ost = D.tile([128, 4, 256], F32, "ost")
            krst = D.tile([128, 4, 32], F32, "krst")
        else:
            ckv_st = D.tile([128, 4, 256], F32, "ckv_st")
            kr_st = D.tile([128, 4, 96], F32, "kr_st")
        for bl, b in enumerate(blocks):
            bs = slice(bl * 512, (bl + 1) * 512)
            for c in range(3):
                wt, wv = wl3(wdv[:, :, c * 128:(c + 1) * 128], 128, 8, 128)
                p = PG()
                for k in range(8):
                    MM(p[:, :], wv[:, k, :], hT[bl][:, k, :], k == 0, k == 7, [p], [wt, hT[bl]])
                ACT(cq32[:, c, :], p[:, :], AF.Copy, [cq32], [p])
                ACT(sq[:, c, :], p[:, :], AF.Square, [sq], [p], scale=float(1.0 / np.sqrt(3.0)))
            rs = rstd_of([sq[:, c, :] for c in range(3)], [sq], ones_bf[:, 1, :], 512)
            for c in range(3):
                STT(cqn[:, c, bs], cq32[:, c, :], smallT[:, 18 + i * 3 + c:19 + i * 3 + c], rs[:, :], ALU.mult, ALU.mult,
                    [cqn], [cq32, rs, smallT])
            for c in range(2):
                wt, wv = wl3(wdv[:, :, 384 + c * 128:384 + (c + 1) * 128], 128, 8, 128)
                p = PG()
                for k in range(8):
                    MM(p[:, :], wv[:, k, :], hT[bl][:, k, :], k == 0, k == 7, [p], [wt, hT[bl]])
                ACT(ckv32[:, c, :], p[:, :], AF.Copy, [ckv32], [p])
                ACT(sq[:, 4 + c, :], p[:, :], AF.Square, [sq], [p], scale=float(1.0 / np.sqrt(2.0)))
            rs = rstd_of([sq[:, 4 + c, :] for c in range(2)], [sq], ones_bf[:, 1, :], 512)
            for c in range(2):
                STT(ckv32[:, c, :], ckv32[:, c, :], smallT[:, 24 + i * 2 + c:25 + i * 2 + c], rs[:, :], ALU.mult, ALU.mult,
                    [ckv32], [ckv32, rs, smallT])
                CP(ckvT[:, c, koff + bl * 512:koff + (bl + 1) * 512], ckv32[:, c, :], [ckvT], [ckv32])
            if kind == "p":
                for c in range(2):
                    p = PG()
                    for tt in range(4):
                        TR(p[:, tt * 128:(tt + 1) * 128], ckv32[:, c, tt * 128:(tt + 1) * 128], ident, [p], [ckv32, cst])
                    EVAC(ost[:, :, c * 128:(c + 1) * 128], p.ap[:, :].rearrange("p (a b) -> p a b", a=4), [ost], [p])
                for tt in range(4):
                    si, r0 = tt // 2, (tt % 2) * 128
                    fw.dma("sp", d_out, o_ckv[si, i, r0:r0 + 128, :], ost[:, tt, :], r=[ost])
            wt, wa = wload([(lambda q: v3(q, 128, 8, 96)[:, :, 64:96], wdv[:, :, 640:672])])
            wv = v3(wa, 128, 8, 96)
            p = PG()
            for k in range(8):
                MM(p[0:96, :], wv[:, k, :], hT[bl][:, k, :], k == 0, k == 7, [p], [wt, hT[bl]])
            if kind == "p":
                ACT(kr32[64:96, :], p[64:96, :], AF.Copy, [kr32], [p])
                CP(krT[64:96, bs], kr32[64:96, :], [krT], [kr32])
                pp = PG()
                for tt in range(4):
                    TR(pp[:, tt * 32:(tt + 1) * 32], kr32[64:96, tt * 128:(tt + 1) * 128], ident[64:96, 64:96], [pp], [kr32, cst])
                CP(krst[:], pp.ap[:, 0:128].rearrange("p (a b) -> p a b", a=4), [krst], [pp])
                for tt in range(4):
                    si, r0 = tt // 2, (tt % 2) * 128
                    fw.dma("sp", d_out, o_kr[si, i, r0:r0 + 128, :], krst[:, tt, :], r=[krst])
            else:
                rope32(p, bl, krT[64:96, koff + bl * 512:koff + (bl + 1) * 512], krT)
        if kind == "s":
            fw.dma("sp", d_in, ckv_st[:], cm_ckv[i].rearrange("(t p) f -> p t f", p=128), w=[ckv_st])
            MSET(kr_st[:], 0.0, [kr_st])
            fw.dma("sp", d_in, kr_st[:, :, 64:96], cm_kr[i].rearrange("(t p) f -> p t f", p=128), w=[kr_st])
            for c in range(2):
                p = PG()
                for t in range(4):
                    TR(p[:, t * 128:(t + 1) * 128], ckv_st[:, t, c * 128:(c + 1) * 128], ident, [p], [ckv_st, cst])
                EVAC(ckvT[:, c, 0:512], p[:, :], [ckvT], [p])
            p = PG()
            for t in range(4):
                TR(p[0:96, t * 128:(t + 1) * 128], kr_st[:, t, :], ident, [p], [kr_st, cst])
            CP(krT[64:96, 0:512], p[64:96, :], [krT], [p])
        D.close()

        wuq_v = w_mla_uq[i].rearrange("(k p) n -> p k n", p=128)
        wukv_v = w_mla_ukv[i].rearrange("(k p) (h t d) -> p k h t d", p=128, t=2, d=64)
        SCL = float(96.0 ** -0.5)
        nkb = nkeys // 512
        nkt = nkeys // 128
        H = Phase()
        rot["wide"] = False
        QT = H.tile([96, 4, ntok], BF16, "QT")
        KT = H.tile([96, 4, nkeys], BF16, "KT")
        Vg = H.tile([128, nkt, 256], BF16, "Vg")
        Et = [H.tile([128, 512], BF16, "Et") for _ in range(3)]
        dn = H.tile([64, 512], F32, "dn")
        for hg in range(4):
            wt, wv = wl3(wuq_v[:, :, hg * 384:(hg + 1) * 384], 128, 3, 384)
            for hl in range(4):
                for bl in range(nb):
                    bs = slice(bl * 512, (bl + 1) * 512)
                    p = PG()
                    for k in range(3):
                        MM(p[0:96, :], wv[:, k, hl * 96:(hl + 1) * 96], cqn[:, k, bs], k == 0, k == 2, [p], [wt, cqn])
                    if kind == "p":
                        EVAC(QT[0:64, hl, bs], p[0:64, :], [QT], [p])
                        EVAC(QT[64:96, hl, bs], p[64:96, :], [QT], [p])
                    else:
                        ACT(QT[0:64, hl, bs], p[0:64, :], AF.Copy, [QT], [p])
                        rope32(p, bl, QT[64:96, hl, bs], QT)

            def vk(q):
                return q[0:128, 0:512].rearrange("p (k h d) -> p k h d", k=2, h=4)
            wtk, wak = wload([(lambda q, k=k: vk(q)[:, k, :, :], wukv_v[:, k, hg * 4:(hg + 1) * 4, 0, :]) for k in range(2)])
            wvk = vk(wak)
            wtv, wav = wload([(lambda q, k=k: vk(q)[:, k, :, :], wukv_v[:, k, hg * 4:(hg + 1) * 4, 1, :]) for k in range(2)])
            wvv = vk(wav)
            for hl in range(4):
                for kb in range(nkb):
                    p = PG()
                    for k in range(2):
                        MM(p[0:64, :], wvk[:, k, hl, :], ckvT[:, k, kb * 512:(kb + 1) * 512], k == 0, k == 1, [p], [wtk, ckvT])
                    EVAC(KT[0:64, hl, kb * 512:(kb + 1) * 512], p[0:64, :], [KT], [p])
                CP(KT[64:96, hl, :], krT[64:96, :], [KT], [krT])
            for kt in range(nkt):
                p = PG()
                for k in range(2):
                    MM(p[:, 0:256], ckvT[:, k, kt * 128:(kt + 1) * 128], wvv[:, k, :, :].rearrange("p h d -> p (h d)"),
                       k == 0, k == 1, [p], [wtv, ckvT])
                EVAC(Vg[:, kt, :], p[:, 0:256], [Vg], [p])
            eti = 0
            for hl in range(4):
                h = hg * 4 + hl
                if kind == "s":
                    jobs = [(qb * 512, 512, list(range(nkt))) for qb in range(nb)]
                else:
                    jobs = [(s0 * 128, sn * 128, list(range(s0, s0 + sn))) for (s0, sn) in G["seqs"]]
                for (q0, qn, kts) in jobs:
                    po, pd = ACCPAIR()

                    def score_m(ki):
                        kt = kts[ki]
                        ps = PG()
                        MM(ps[:, 0:qn], KT[0:96, hl, kt * 128:(kt + 1) * 128], QT[0:96, hl, q0:q0 + qn], True, True, [ps], [KT, QT])
                        return ps
                    ps_next = score_m(0)
                    for ki, kt in enumerate(kts):
                        ps = ps_next
                        if ki + 1 < len(kts):
                            ps_next = score_m(ki + 1)
                        et = Et[eti % 3]
                        eti += 1
                        ACT(et[:, 0:qn], ps[:, 0:qn], AF.Exp, [et], [ps], scale=SCL)
                        last = ki == len(kts) - 1
                        MM(po[0:64, 0:qn], Vg[:, kt, hl * 64:(hl + 1) * 64], et[:, 0:qn], ki == 0, last, [po], [Vg, et])
                        MM(pd[0:64, 0:qn], ones_bf[:, 2, 0:64], et[:, 0:qn], ki == 0, last, [pd], [ones_bf, et])
                        if ki == min(2, len(kts) - 1):
                            run_deferred()

                    def fin_m(po=po, pd=pd, h=h, q0=q0, qn=qn):
                        ACT(dn[0:64, 0:qn], pd[0:64, 0:qn], AF.Ln, [dn], [pd])
                        ACT(dn[0:64, 0:qn], dn[0:64, 0:qn], AF.Exp, [dn], [dn], scale=-1.0)
                        TT(oT[0:64, h, q0:q0 + qn], po[0:64, 0:qn], dn[0:64, 0:qn], ALU.mult, [oT], [po, dn])
                    defer(fin_m)
        run_deferred()
        rot["wide"] = True
        H.close()

        O = Phase()
        osb = O.tile([128, 8, 512], F32, "osb")
        sq = O.tile([128, 8, 512], BF16, "sqo")
        wo_v = w_mla_o[i].rearrange("(h p) n -> p h n", p=64)
        for bl, b in enumerate(blocks):
            bs = slice(bl * 512, (bl + 1) * 512)
            for n in range(8):
                wt, wv = wl3(wo_v[:, :, n * 128:(n + 1) * 128], 64, 16, 128)
                p = PG()
                for h in range(16):
                    MM(p[:, :], wv[0:64, h, :], oT[0:64, h, bs], h == 0, h == 15, [p], [wt, oT])
                post_evac(n, p, osb, sq)
            post_finish(b, mv, 0, osb, sq)
        O.close()
        E.close()

    GROUPS = [dict(t0=0, nt=4, blocks=[0], seqs=[(0, 2), (2, 2)], kind="p"),
              dict(t0=4, nt=8, blocks=[1, 2], seqs=[(0, 8)], kind="s")]

    for l in range(n_layers):
        mv = modulation(l)
        i = l // 2
        if do_mixer:
            for gi in groups_sel:
                try:
                    if l % 2 == 0:
                        even_mixer(l, i, mv, GROUPS[gi])
                    else:
                        odd_mixer(l, i, mv, GROUPS[gi])
                except Bail:
                    for ph_ in reversed(list(open_phases)):
                        ph_.close()
        if do_mlp:
            M = Phase()
            hid = M.tile([128, 32, 512], BF16, "hid")
            hTm = [M.tile([128, 8, 512], BF16, "hTm") for _ in range(2)]
            osb = M.tile([128, 8, 512], F32, "osbm")
            sqm = M.tile([128, 8, 512], BF16, "sqm")
            sqp = M.tile([128, 8, 512], BF16, "sqp")
            pre(0, mv, 1, hTm[0], sqp)
            for b in range(3):
                nxt = (lambda b=b: pre(b + 1, mv, 1, hTm[(b + 1) % 2], sqp)) if b < 2 else None
                mlp(l, b, mv, hTm[b % 2], hid, osb, sqm, nxt)
            M.close()

    Y = Phase()
    stage = [Y.tile([128, 1024], F32, "ystage") for _ in range(2)]
    for t in range(12):
        stg = stage[t % 2]
        for half in range(2):
            p = PG()
            for cc in range(4):
                c = half * 4 + cc
                TR(p[:, cc * 128:(cc + 1) * 128], xT.ap[:, c, t * 128:(t + 1) * 128], ident, [p], [xTb[t // 4], cst])
            EVAC(stg[:, half * 512:(half + 1) * 512], p[:, :], [stg], [p])
        fw.dma("sp", d_out, y_out[t * 128:(t + 1) * 128, :], stg[:], r=[stg])
    fw.barrier(fw.spool)
    fw.flush(final=True, final_dsems=fw.spool)
    Y.es.close()
    fw.es.close()
    print("program: %d instrs; engine counts:" % fw.ninstr, {n: e.cnt for n, e in fw.engs.items()})
    return nc


_CACHE = {}


def _run(inputs, n_layers=4, do_mixer=True, do_mlp=True, trace=False, groups_sel=(0, 1)):
    key = (n_layers, do_mixer, do_mlp, tuple(groups_sel))
    if key not in _CACHE:
        _CACHE[key] = build_program(n_layers, do_mixer, do_mlp, tuple(groups_sel))
    nc = _CACHE[key]
    f = lambda a: np.ascontiguousarray(np.asarray(a, dtype=np.float32))
    consts, cs64, cs32, pm32 = make_consts()
    xp = f(inputs["x_prompt"])
    xs = f(inputs["x_sample"])
    shared = {k: f(inputs[k]) for k in ("w_mod", "b_mod", "g_norm", "w_ff1", "w_ff2", "w_in_ab", "w_gk_f", "b_gk_f",
                                        "w_gk_b", "b_gk_b", "g_gla", "swa_sink", "w_out_ab", "w_mla_down", "g_mla_q",
                                        "g_mla_kv", "w_mla_uq", "w_mla_ukv", "w_mla_o")}
    shared.update(consts=consts, cs64=cs64, cs32=cs32, pm32=pm32)
    c = f(inputs["c"])
    c_ctx = f(inputs["c_ctx"])
    in_maps = []
    for core in range(NCORES):
        sb = core % 2
        m = dict(shared)
        m["xin"] = np.ascontiguousarray(np.concatenate([xp[2 * core], xp[2 * core + 1], xs[sb]], axis=0))
        m["cvec"] = np.ascontiguousarray(np.stack([c_ctx, c[sb]], axis=0))
        m["st_f"] = f(inputs["state_gla_fwd"][sb])
        m["st_b"] = f(inputs["state_gla_bwd"][sb])
        m["cswa_k"] = f(inputs["cache_swa_k"][sb]).reshape(2, 512, 128)
        m["cswa_v"] = f(inputs["cache_swa_v"][sb]).reshape(2, 512, 128)
        m["cm_ckv"] = f(inputs["cache_mla_ckv"][sb])
        m["cm_kr"] = f(inputs["cache_mla_kr"][sb])
        in_maps.append(m)
    res = run_bass_kernel_spmd(nc, in_maps, core_ids=list(range(NCORES)), trace=trace)
    R = res.results
    y_p = np.stack([R[cidx]["y"][s * 256:(s + 1) * 256] for cidx in range(NCORES) for s in range(2)], axis=0)
    y_s = np.stack([R[b]["y"][512:1536] for b in range(2)], axis=0)

    def gather(name, shape):
        return np.concatenate([R[cidx][name] for cidx in range(NCORES)], axis=0).reshape(shape)
    outs = (y_p.astype(np.float32), y_s.astype(np.float32),
            gather("o_stf", (16, 2, 4, 64, 128)), gather("o_stb", (16, 2, 4, 64, 128)),
            gather("o_swk", (16, 2, 256, 2, 64)), gather("o_swv", (16, 2, 256, 2, 64)),
            gather("o_ckv", (16, 2, 256, 256)), gather("o_kr", (16, 2, 256, 32)))
    return outs, res


def kernel(**inputs):
    outs, _ = _run(inputs)
    return outs
```

```python
from contextlib import ExitStack
import numpy as np
import concourse.bass as bass
import concourse.mybir as mybir
from concourse.bass_utils import run_bass_kernel_spmd

F32 = mybir.dt.float32
BF16 = mybir.dt.bfloat16
AF = mybir.ActivationFunctionType
ALU = mybir.AluOpType

EPS = 1e-6
NCORES = 8


class Eng:
    def __init__(self, name, sem):
        self.name = name
        self.sem = sem
        self.cnt = 0
        self.seen = {}
        self.q = []
        self.pending = []


class DSem:
    def __init__(self, sem, serial=False):
        self.sem = sem
        self.cnt = 0
        self.serial = serial


class T:
    __slots__ = ("ap", "w", "r", "name", "scoped")

    def __init__(self, ap, name=""):
        self.ap = ap
        self.w = None
        self.r = []
        self.name = name
        self.scoped = False

    def __getitem__(self, k):
        return self.ap[k]


class FW:
    def __init__(self, nc):
        self.nc = nc
        self.es = ExitStack()
        self.engs = {}
        for n in ("pe", "act", "dve", "pool", "sp"):
            self.engs[n] = Eng(n, self.es.enter_context(nc.semaphore("s_" + n)))
        self.nd = 0
        self.nt = 0
        self.ninstr = 0
        self.spool = [self.dsem(serial=True) for _ in range(12)]
        self.spi = 0

    def dsem(self, serial=False):
        self.nd += 1
        return DSem(self.es.enter_context(self.nc.semaphore("d%d" % self.nd)), serial)

    def tile(self, shape, dt, name=None, stack=None):
        self.nt += 1
        es = stack if stack is not None else self.es
        t = es.enter_context(self.nc.sbuf_tensor("sb_" + (name or ("t%d" % self.nt)), list(shape), dt))
        return T(t, name or "")

    def ptile(self, shape, dt, name=None):
        self.nt += 1
        t = self.es.enter_context(self.nc.psum_tensor(name or ("p%d" % self.nt), list(shape), dt))
        return T(t, name or "")

    def _need(self, eng, deps):
        best = {}
        for so, v, raw in deps:
            if isinstance(so, DSem) and not so.serial:
                v = so.cnt
            k = id(so)
            if v > best.get(k, (None, 0))[1]:
                best[k] = (so, v)
        for k, (so, v) in best.items():
            if so is eng and eng.name == "pe":
                continue
            if eng.seen.get(k, 0) >= v:
                continue
            eng.seen[k] = v
            eng.q.append(lambda e, sem=so.sem, v=v: e.wait_ge(sem, v))

    @staticmethod
    def _deps(w, r):
        deps = []
        for t in r:
            if t.w is not None:
                deps.append((t.w[0], t.w[1], True))
        for t in w:
            if t.w is not None:
                deps.append((t.w[0], t.w[1], False))
            deps.extend((so, v, False) for so, v in t.r)
        return deps

    @staticmethod
    def _compact(t):
        m = {}
        for so, v in t.r:
            if isinstance(so, DSem) and not so.serial:
                v = so.cnt
            if v > m.get(id(so), (None, 0))[1]:
                m[id(so)] = (so, v)
        t.r = list(m.values())

    def _pool_pending(self, eng, w, r):
        if eng.pending and any(t.scoped for t in list(w) + list(r)):
            for so, v in eng.pending:
                k = id(so)
                if eng.seen.get(k, 0) >= v:
                    continue
                eng.seen[k] = v
                eng.q.append(lambda e, sem=so.sem, v=v: e.wait_ge(sem, v))
            eng.pending = []

    def op(self, en, fn, w=(), r=(), signal=True):
        eng = self.engs[en]
        self._pool_pending(eng, w, r)
        self._need(eng, self._deps(w, r))
        self.ninstr += 1
        if signal:
            eng.cnt += 1
            c = eng.cnt
            eng.q.append(lambda e, fn=fn, sem=eng.sem: fn(e).then_inc(sem, 1))
        else:
            c = eng.cnt + 1
            eng.q.append(lambda e, fn=fn: fn(e))
        for t in w:
            t.w = (eng, c)
            t.r = []
        for t in r:
            t.r.append((eng, c))
            if len(t.r) > 16:
                self._compact(t)

    def dma(self, en, ds, out, in_, w=(), r=()):
        eng = self.engs[en]
        self._pool_pending(eng, w, r)
        if ds is None:
            ds = self.spool[self.spi % len(self.spool)]
            self.spi += 1
            if ds.cnt and eng.seen.get(id(ds), 0) < ds.cnt:
                eng.seen[id(ds)] = ds.cnt
                eng.q.append(lambda e, sem=ds.sem, v=ds.cnt: e.wait_ge(sem, v))
        self._need(eng, self._deps(w, r))
        self.ninstr += 1
        ds.cnt += 16
        c = ds.cnt
        eng.q.append(lambda e, out=out, in_=in_, sem=ds.sem: e.dma_start(out=out, in_=in_).then_inc(sem, 16))
        for t in w:
            t.w = (ds, c)
            t.r = []
        for t in r:
            t.r.append((ds, c))
            if len(t.r) > 16:
                self._compact(t)

    def barrier(self, dsems=()):
        for eng in self.engs.values():
            waits = []
            for other in self.engs.values():
                if other is eng or other.cnt == 0:
                    continue
                waits.append((other, other.cnt))
            for ds in dsems:
                if ds.cnt:
                    waits.append((ds, ds.cnt))
            if eng.name == "pool":
                eng.pending = waits
                continue
            for so, v in waits:
                k = id(so)
                if eng.seen.get(k, 0) >= v:
                    continue
                eng.seen[k] = v
                eng.q.append(lambda e, sem=so.sem, v=v: e.wait_ge(sem, v))

    def flush(self, final=False, final_dsems=()):
        nc = self.nc
        if final:
            sp = self.engs["sp"]
            for ds in final_dsems:
                if ds.cnt:
                    sp.q.append(lambda e, sem=ds.sem, v=ds.cnt: e.wait_ge(sem, v))
            for n in ("pe", "act", "dve", "pool"):
                eg = self.engs[n]
                if eg.cnt:
                    sp.q.append(lambda e, sem=eg.sem, v=eg.cnt: e.wait_ge(sem, v))
        qs = {n: self.engs[n].q for n in self.engs}
        for n in self.engs:
            self.engs[n].q = []
        with nc.Block() as block:
            @block.tensor
            def _(e):
                for f in qs["pe"]:
                    f(e)

            @block.scalar
            def _(e):
                for f in qs["act"]:
                    f(e)

            @block.vector
            def _(e):
                for f in qs["dve"]:
                    f(e)

            @block.gpsimd
            def _(e):
                for f in qs["pool"]:
                    f(e)

            @block.sync
            def _(e):
                for f in qs["sp"]:
                    f(e)


def make_consts():
    s = np.arange(128)[:, None]
    t = np.arange(128)[None, :]
    same = (s // 64) == (t // 64)
    c = np.zeros((10, 128, 128), np.float32)
    c[0] = np.eye(128)
    k = -1.0 / 16.0
    c[1] = k * (same & (s <= t))
    c[2] = k * (same & (s >= t))
    c[3] = k * (same & (s > t))
    c[4] = k * (same & (s < t))
    c[5] = (same & (s <= t))
    c[6] = (same & (s >= t))
    c[7] = (s <= t)
    c[8] = (s >= t)
    pm = np.zeros((128, 128), np.float32)
    for p in range(128):
        q = p % 64
        b = (q % 32) // 16
        partner = p + 16 if b == 0 else p - 16
        pm[partner, p] = 1.0
    c[9] = pm
    L = 1024
    rows = np.repeat(np.arange(L // 64, dtype=np.float32), 64)
    cols = np.tile(np.arange(64, dtype=np.float32), L // 64)
    pos = [rows, cols]

    def tables(hd, nrows, base):
        nf = hd // 4
        inv = (10000.0 ** (-np.arange(nf, dtype=np.float32) / nf)).astype(np.float32)
        cs = np.zeros((2, nrows, L), np.float32)
        for p in range(nrows - base):
            q = p % hd
            a = q // (2 * nf)
            b = (q % (2 * nf)) // nf
            f = q % nf
            ang = (pos[a] * inv[f]).astype(np.float32)
            cs[0, base + p] = np.cos(ang)
            cs[1, base + p] = -np.sin(ang) if b == 0 else np.sin(ang)
        return cs

    cs64 = tables(64, 128, 0)
    cs32 = tables(32, 96, 64)
    pm32 = np.zeros((96, 96), np.float32)
    for q in range(32):
        b = (q % 16) // 8
        partner = q + 8 if b == 0 else q - 8
        pm32[64 + partner, 64 + q] = 1.0
    return c, cs64, cs32, pm32


def build_program(n_layers=4, do_mixer=True, do_mlp=True, groups_sel=(0, 1)):
    nc = bass.Bass("TRN2", target_bir_lowering=False)

    def din(name, shape):
        return nc.dram_tensor(name, list(shape), F32, kind="ExternalInput").ap()

    def dout(name, shape):
        return nc.dram_tensor(name, list(shape), F32, kind="ExternalOutput").ap()

    xin = din("xin", [1536, 1024])
    cvec = din("cvec", [2, 1024])
    st_f = din("st_f", [2, 4, 64, 128])
    st_b = din("st_b", [2, 4, 64, 128])
    cswa_k = din("cswa_k", [2, 512, 128])
    cswa_v = din("cswa_v", [2, 512, 128])
    cm_ckv = din("cm_ckv", [2, 512, 256])
    cm_kr = din("cm_kr", [2, 512, 32])
    w_mod = din("w_mod", [4, 1024, 6144])
    b_mod = din("b_mod", [4, 6144])
    g_norm = din("g_norm", [4, 4, 1024])
    w_ff1 = din("w_ff1", [4, 1024, 4096])
    w_ff2 = din("w_ff2", [4, 4096, 1024])
    w_in_ab = din("w_in_ab", [2, 1024, 2336])
    w_gk_f = din("w_gk_f", [2, 16, 256])
    b_gk_f = din("b_gk_f", [2, 256])
    w_gk_b = din("w_gk_b", [2, 16, 256])
    b_gk_b = din("b_gk_b", [2, 256])
    g_gla = din("g_gla", [2, 128])
    swa_sink = din("swa_sink", [2, 8])
    w_out_ab = din("w_out_ab", [2, 1024, 1024])
    w_mla_down = din("w_mla_down", [2, 1024, 672])
    g_mla_q = din("g_mla_q", [2, 384])
    g_mla_kv = din("g_mla_kv", [2, 256])
    w_mla_uq = din("w_mla_uq", [2, 384, 1536])
    w_mla_ukv = din("w_mla_ukv", [2, 256, 2048])
    w_mla_o = din("w_mla_o", [2, 1024, 1024])
    consts = din("consts", [10, 128, 128])
    cs64_d = din("cs64", [2, 128, 1024])
    cs32_d = din("cs32", [2, 96, 1024])
    pm32_d = din("pm32", [96, 96])

    y_out = dout("y", [1536, 1024])
    o_stf = dout("o_stf", [2, 2, 4, 64, 128])
    o_stb = dout("o_stb", [2, 2, 4, 64, 128])
    o_swk = dout("o_swk", [2, 2, 256, 128])
    o_swv = dout("o_swv", [2, 2, 256, 128])
    o_ckv = dout("o_ckv", [2, 2, 256, 256])
    o_kr = dout("o_kr", [2, 2, 256, 32])

    fw = FW(nc)
    d_in = None
    d_out = None
    uniq = {"i": 0}

    open_phases = []
    phase_stacks = []
    import os as _os
    DBG = int(_os.environ.get("KDBG", "0"))

    class Bail(Exception):
        pass

    def CK(k):
        if DBG == k:
            raise Bail()

    class Phase:
        def __init__(self):
            self.es = ExitStack()
            open_phases.append(self)

        def tile(self, shape, dt, name):
            uniq["i"] += 1
            t = fw.tile(shape, dt, "%s_%d" % (name, uniq["i"]), stack=self.es)
            t.scoped = True
            return t

        def close(self):
            fw.barrier(fw.spool)
            self.es.close()
            open_phases.remove(self)

    PB = [fw.ptile([128, 512], F32, "pb%d" % i) for i in range(8)]
    rot = {"i": 0, "a": 0, "wide": True}

    def PG():
        t = PB[rot["i"] % (8 if rot["wide"] else 4)]
        rot["i"] += 1
        return t

    def ACCPAIR():
        k = rot["a"] % 2
        rot["a"] += 1
        return PB[4 + 2 * k], PB[5 + 2 * k]

    def MM(out, lhsT, rhs, start, stop, w, r):
        fw.op("pe", lambda e: e.matmul(out, lhsT=lhsT, rhs=rhs, start=start, stop=stop), w=w, r=r, signal=stop)

    def TR(out, in_, idn, w, r):
        fw.op("pe", lambda e: e.transpose(out, in_, idn), w=w, r=r)

    def ACT(out, in_, func, w, r, scale=1.0, bias=0.0):
        fw.op("act", lambda e: e.activation(out=out, in_=in_, func=func, bias=bias, scale=scale), w=w, r=r)

    def TT(out, in0, in1, op, w, r, en="dve"):
        fw.op(en, lambda e: e.tensor_tensor(out=out, in0=in0, in1=in1, op=op), w=w, r=r)

    def STT(out, in0, scalar, in1, op0, op1, w, r, en="dve"):
        fw.op(en, lambda e: e.scalar_tensor_tensor(out=out, in0=in0, scalar=scalar, in1=in1, op0=op0, op1=op1), w=w, r=r)

    def TS(out, in0, s1, op0, w, r, en="dve"):
        fw.op(en, lambda e: e.tensor_scalar(out=out, in0=in0, scalar1=s1, scalar2=None, op0=op0), w=w, r=r)

    def CP(out, in_, w, r, en="dve"):
        fw.op(en, lambda e: e.tensor_copy(out=out, in_=in_), w=w, r=r)

    def RCP(out, in_, w, r):
        fw.op("dve", lambda e: e.reciprocal(out=out, in_=in_), w=w, r=r)

    def MSET(ap, val, w, en="dve"):
        fw.op(en, lambda e: e.memset(ap, val), w=w, r=())

    def SPLIT(src, hi, lo, hiT, loT_, srcT):
        CP(hi, src, [hiT], [srcT])
        TT(lo, src, hi, ALU.subtract, [loT_], [srcT, hiT])

    def bc(ap, shape, axis):
        return ap.unsqueeze(axis).to_broadcast(list(shape))

    evac_rr = {"i": 0}
    fin = {"f": None}

    def defer(fn):
        run_deferred()
        fin["f"] = fn

    def run_deferred():
        if fin["f"] is not None:
            f_ = fin["f"]
            fin["f"] = None
            f_()

    def EVAC(out, in_, w, r):
        bi = PB.index(r[0]) if r and r[0] in PB else evac_rr["i"]
        evac_rr["i"] += 1
        if bi % 2:
            ACT(out, in_, AF.Copy, w, r)
        else:
            CP(out, in_, w, r)

    xT = fw.tile([128, 8, 1536], F32, "xT")
    xTb = [T(xT.ap[:, :, b * 512:(b + 1) * 512], "xTb%d" % b) for b in range(3)]
    cst = fw.tile([128, 10, 128], F32, "cst")
    ident = cst.ap[:, 0, :]
    ones_bf = fw.tile([128, 3, 128], BF16, "ones_bf")
    ones_row = fw.tile([1, 128], F32, "ones_row")
    pm32 = fw.tile([96, 96], F32, "pm32")
    cstb = fw.tile([128, 10, 128], BF16, "cstb")
    pm32b = fw.tile([96, 96], BF16, "pm32b")
    ones_rb = fw.tile([1, 128], BF16, "ones_rb")
    wgkh = fw.tile([49, 2, 256], BF16, "wgkh")
    wgkl = fw.tile([49, 2, 256], BF16, "wgkl")
    sinkh = fw.tile([1, 16], BF16, "sinkh")
    sinkl = fw.tile([1, 16], BF16, "sinkl")
    gnT = fw.tile([128, 128], F32, "gnT")
    bmT = fw.tile([128, 192], F32, "bmT")
    smallT = fw.tile([128, 28], F32, "smallT")
    scT = fw.tile([128, 16], BF16, "scT")
    wgk = fw.tile([49, 2, 256], F32, "wgk")
    sink_row = fw.tile([1, 16], F32, "sink_row")
    esink = fw.tile([128, 16], F32, "esink")
    modv = [fw.tile([128, 2, 6, 8], F32, "modv%d" % i) for i in range(2)]
    mraw = fw.tile([128, 2, 48], F32, "mraw")
    rstd_t = [fw.tile([128, 512], F32, "rstd%d" % i) for i in range(2)]
    utmp = [fw.tile([128, 512], F32, "utmp%d" % i) for i in range(3)]
    rr = {"u": 0, "r": 0}

    def UT():
        rr["u"] += 1
        return utmp[rr["u"] % 3]

    NWB = 6
    WBE = 2048
    wbufs = [(fw.tile([128, WBE], BF16, "wb%d" % i), fw.dsem()) for i in range(NWB)]
    wrot = {"i": 0}
    for t_, _ in wbufs:
        MSET(t_[:], 0.0, [t_], en="pool")

    def wload(parts):
        t, ds = wbufs[wrot["i"] % NWB]
        wrot["i"] += 1
        for dstf, src in parts:
            fw.dma("pool", ds, dstf(t.ap), src, w=[t])
        return t, t.ap

    def v3(ap, P, a, b):
        assert a * b <= WBE
        return ap[0:P, 0:a * b].rearrange("p (a b) -> p a b", a=a)

    def wl3(src, P, a, b):
        t, ap = wload([(lambda q: v3(q, P, a, b), src)])
        return t, v3(ap, P, a, b)

    SU = Phase()
    stage = [SU.tile([128, 1024], F32, "stage") for _ in range(2)]
    fw.dma("sp", d_in, cst[:], consts.rearrange("k p n -> p k n"), w=[cst])
    fw.dma("sp", d_in, pm32[:], pm32_d, w=[pm32])
    MSET(ones_bf[:, 0, :], 1.0 / 1024.0, [ones_bf])
    MSET(ones_bf[:, 1, :], 1.0 / 128.0, [ones_bf])
    MSET(ones_bf[:, 2, :], 1.0, [ones_bf])
    MSET(ones_row[:], 1.0, [ones_row])
    MSET(ones_rb[:], 1.0, [ones_rb])
    CP(cstb[:], cst[:], [cstb], [cst])
    CP(pm32b[:], pm32[:], [pm32b], [pm32])
    st0 = stage[0]
    fw.dma("sp", d_in, st0[0:16, 0:128], cvec.rearrange("j (c p) -> (j c) p", p=128), w=[st0])
    fw.dma("sp", d_in, st0[16:18, 0:128], g_gla, w=[st0])
    fw.dma("sp", d_in, st0[18:24, 0:128], g_mla_q.rearrange("i (c p) -> (i c) p", p=128), w=[st0])
    fw.dma("sp", d_in, st0[24:28, 0:128], g_mla_kv.rearrange("i (c p) -> (i c) p", p=128), w=[st0])
    p = PG()
    TR(p[:, 0:28], st0[0:28, 0:128], ident[0:28, 0:28], [p], [st0, cst])
    CP(smallT[:], p[:, 0:28], [smallT], [p])
    ACT(scT[:], smallT[:, 0:16], AF.Silu, [scT], [smallT])
    st1 = stage[1]
    fw.dma("sp", d_in, st1[:, 0:128], g_norm.rearrange("l i (c p) -> (l i c) p", p=128), w=[st1])
    p = PG()
    TR(p[:, 0:128], st1[:, 0:128], ident, [p], [st1, cst])
    CP(gnT[:], p[:, 0:128], [gnT], [p])
    bmv = b_mod.rearrange("l (n p) -> (l n) p", p=128)
    fw.dma("sp", d_in, st0[:, 128:256], bmv[0:128, :], w=[st0])
    fw.dma("sp", d_in, st1[0:64, 128:256], bmv[128:192, :], w=[st1])
    p = PG()
    TR(p[:, 0:128], st0[:, 128:256], ident, [p], [st0, cst])
    TR(p[:, 128:192], st1[0:64, 128:256], ident[0:64, 0:64], [p], [st1, cst])
    CP(bmT[:], p[:, 0:192], [bmT], [p])
    fw.dma("sp", d_in, wgk[0:16, :, :], w_gk_f.rearrange("i k n -> k i n"), w=[wgk])
    fw.dma("sp", d_in, wgk[32:48, :, :], w_gk_b.rearrange("i k n -> k i n"), w=[wgk])
    fw.dma("sp", d_in, wgk[16:17, :, :], b_gk_f.rearrange("(o i) n -> o i n", o=1), w=[wgk])
    fw.dma("sp", d_in, wgk[48:49, :, :], b_gk_b.rearrange("(o i) n -> o i n", o=1), w=[wgk])
    fw.dma("sp", d_in, sink_row[:], swa_sink.rearrange("(o i) h -> o (i h)", o=1), w=[sink_row])
    SPLIT(sink_row[:], sinkh[:], sinkl[:], sinkh, sinkl, sink_row)
    for q0_, q1_ in ((0, 17), (32, 49)):
        SPLIT(wgk[q0_:q1_, :, :], wgkh[q0_:q1_, :, :], wgkl[q0_:q1_, :, :], wgkh, wgkl, wgk)
    p = PG()
    MM(p[:, 0:16], ones_rb[0:1, :], sinkh[0:1, :], True, False, [p], [ones_rb, sinkh])
    MM(p[:, 0:16], ones_rb[0:1, :], sinkl[0:1, :], False, True, [p], [ones_rb, sinkl])
    ACT(esink[:], p[:, 0:16], AF.Exp, [esink], [p])

    for t in range(12):
        stg = stage[t % 2]
        fw.dma("sp", d_in, stg[:], xin[t * 128:(t + 1) * 128, :], w=[stg])
        for half in range(2):
            p = PG()
            for cc in range(4):
                c = half * 4 + cc
                TR(p[:, cc * 128:(cc + 1) * 128], stg[:, c * 128:(c + 1) * 128], ident, [p], [stg, cst])
            EVAC(xT.ap[:, half * 4:half * 4 + 4, t * 128:(t + 1) * 128], p.ap[:, :].rearrange("p (c n) -> p c n", c=4),
                 [xTb[t // 4]], [p])
    SU.close()

    def modulation(l):
        mv = modv[l % 2]
        pm = PG()
        wv = w_mod[l].rearrange("(k p) n -> p k n", p=128)
        scv = scT.ap[:, :].rearrange("p (j k) -> p j k", j=2)
        for pc in range(24):
            wt, wvw = wl3(wv[:, :, pc * 256:(pc + 1) * 256], 128, 8, 256)
            for nn in range(2):
                n = pc * 2 + nn
                for k in range(8):
                    MM(pm[:, n * 2:(n + 1) * 2], wvw[:, k, nn * 128:(nn + 1) * 128], scv[:, :, k], k == 0, k == 7, [pm], [wt, scT])
        pmv = pm.ap[:, 0:96].rearrange("p (n j) -> p n j", j=2)
        for j in range(2):
            TT(mraw[:, j, :], pmv[:, :, j], bmT[:, l * 48:(l + 1) * 48], ALU.add, [mraw], [pm, bmT])
        g = lambda q: gnT[:, l * 32 + q * 8:l * 32 + q * 8 + 8]
        for j in range(2):
            STT(mv[:, j, 0, :], mraw[:, j, 8:16], 1.0, g(0), ALU.add, ALU.mult, [mv], [mraw, gnT])
            CP(mv[:, j, 1, :], mraw[:, j, 0:8], [mv], [mraw])
            TT(mv[:, j, 2, :], mraw[:, j, 16:24], g(1), ALU.mult, [mv], [mraw, gnT])
            STT(mv[:, j, 3, :], mraw[:, j, 32:40], 1.0, g(2), ALU.add, ALU.mult, [mv], [mraw, gnT])
            CP(mv[:, j, 4, :], mraw[:, j, 24:32], [mv], [mraw])
            TT(mv[:, j, 5, :], mraw[:, j, 40:48], g(3), ALU.mult, [mv], [mraw, gnT])
        return mv

    def rstd_of(sq_aps, sq_T, ones_ap, N):
        ss = PG()
        n = len(sq_aps)
        for c, a in enumerate(sq_aps):
            MM(ss[:, 0:N], ones_ap, a, c == 0, c == n - 1, [ss], [ones_bf] + sq_T)
        rr["r"] += 1
        rs = rstd_t[rr["r"] % 2]
        ACT(rs[:, 0:N], ss[:, 0:N], AF.Ln, [rs], [ss], bias=EPS)
        ACT(rs[:, 0:N], rs[:, 0:N], AF.Exp, [rs], [rs], scale=-0.5)
        return rs

    def pre(b, mv, which, dst, sq):
        j = 0 if b == 0 else 1
        A = mv.ap[:, j, 3 * which + 0, :]
        Sh = mv.ap[:, j, 3 * which + 1, :]
        xb = xTb[b]
        ACT(sq[:], xb[:], AF.Square, [sq], [xb])
        rs = rstd_of([sq[:, c, :] for c in range(8)], [sq], ones_bf[:, 0, :], 512)
        for c in range(8):
            u = UT()
            TT(u[:], xb[:, c, :], rs[:, :], ALU.mult, [u], [xb, rs])
            ACT(dst[:, c, :], u[:], AF.Identity, [dst], [u, mv], scale=A[:, c:c + 1], bias=Sh[:, c:c + 1])

    def post_evac(n, p, osb, sq):
        ACT(osb[:, n, :], p[:, :], AF.Copy, [osb], [p])
        TT(sq[:, n, :], osb[:, n, :], osb[:, n, :], ALU.mult, [sq], [osb])

    def post_finish(b, mv, which, osb, sq):
        j = 0 if b == 0 else 1
        G = mv.ap[:, j, 3 * which + 2, :]
        xb = xTb[b]
        rs = rstd_of([sq[:, c, :] for c in range(8)], [sq], ones_bf[:, 0, :], 512)
        for c in range(8):
            u = UT()
            STT(u[:], osb[:, c, :], G[:, c:c + 1], rs[:, :], ALU.mult, ALU.mult, [u], [osb, rs, mv])
            TT(xb[:, c, :], xb[:, c, :], u[:], ALU.add, [xb], [xb, u])

    def mlp(l, b, mv, hT, hid, osb, sq, next_pre=None):
        w1v = w_ff1[l].rearrange("(k p) n -> p k n", p=128)
        for jp in range(16):
            wt, wv = wl3(w1v[:, :, jp * 256:(jp + 1) * 256], 128, 8, 256)
            for jj in range(2):
                jn = jp * 2 + jj
                ph_ = PG()
                for k in range(8):
                    MM(ph_[:, :], wv[:, k, jj * 128:(jj + 1) * 128], hT[:, k, :], k == 0, k == 7, [ph_], [wt, hT])
                ACT(hid[:, jn, :], ph_[:, :], AF.Relu, [hid], [ph_])
                TT(hid[:, jn, :], hid[:, jn, :], hid[:, jn, :], ALU.mult, [hid], [hid])
        if next_pre is not None:
            next_pre()
        w2v = w_ff2[l].rearrange("(j p) n -> p j n", p=128)
        for n in range(8):
            po = PG()
            for jh in range(2):
                wt, wv = wl3(w2v[:, jh * 16:(jh + 1) * 16, n * 128:(n + 1) * 128], 128, 16, 128)
                for jj in range(16):
                    jn = jh * 16 + jj
                    MM(po[:, :], wv[:, jj, :], hid[:, jn, :], jn == 0, jn == 31, [po], [wt, hid])
            post_evac(n, po, osb, sq)
        post_finish(b, mv, 1, osb, sq)

    def even_mixer(l, i, mv, G):
        kind, nt, blocks = G["kind"], G["nt"], G["blocks"]
        ntok = nt * 128
        nb = len(blocks)
        tg0 = G["t0"]
        E = Phase()
        hT = [E.tile([128, 8, 512], BF16, "hT") for _ in range(nb)]
        sq = E.tile([128, 8, 512], BF16, "sq")
        loTh = E.tile([49, ntok], BF16, "loTh")
        loTl = E.tile([49, ntok], BF16, "loTl")
        lotmp = E.tile([48, 512], F32, "lotmp")
        for q0_ in (0, 32):
            MSET(loTh[q0_:q0_ + 17, :], 1.0, [loTh])
            MSET(loTl[q0_:q0_ + 17, :], 0.0, [loTl])
        mixT = E.tile([128, 4, ntok], BF16, "mixT")
        winv = w_in_ab[i].rearrange("(k p) n -> p k n", p=128)
        for bl, b in enumerate(blocks):
            pre(b, mv, 0, hT[bl], sq)
        wt, wa = wload([(lambda q: v3(q, 128, 8, 48)[:, :, 0:16], winv[:, :, 1536:1552]),
                        (lambda q: v3(q, 128, 8, 48)[:, :, 32:48], winv[:, :, 1552:1568])])
        wv = v3(wa, 128, 8, 48)
        for bl in range(nb):
            p = PG()
            for k in range(8):
                MM(p[0:48, :], wv[:, k, :], hT[bl][:, k, :], k == 0, k == 7, [p], [wt, hT[bl]])
            ACT(lotmp[0:48, :], p[0:48, :], AF.Copy, [lotmp], [p])
            for q0_ in (0, 32):
                SPLIT(lotmp[q0_:q0_ + 16, :], loTh[q0_:q0_ + 16, bl * 512:(bl + 1) * 512],
                      loTl[q0_:q0_ + 16, bl * 512:(bl + 1) * 512], loTh, loTl, lotmp)

        CK(1)
        S = Phase()
        qaT = S.tile([128, ntok], BF16, "qaT")
        kaT = S.tile([128, ntok], BF16, "kaT")
        sgT = S.tile([128, 2, ntok], BF16, "sgT")
        katok = S.tile([128, nt, 128], BF16, "katok")
        vatok = S.tile([128, nt, 256], BF16, "vatok")
        qeS = S.tile([128, nt, 2, 128], BF16, "qeS")
        attS = S.tile([128, nt, 2, 2, 128], BF16, "attS")
        Ust = S.tile([128, nt, 2, 2, 128], F32, "Ust")
        SbfF = S.tile([128, nt, 2, 128], BF16, "SbfF")
        SbfB = S.tile([128, nt, 2, 128], BF16, "SbfB")
        dsc = S.tile([128, nt, 2, 2], F32, "dsc")
        Sf = S.tile([128, 128], F32, "Sf")
        Sb = S.tile([128, 128], F32, "Sb")
        etmp = [S.tile([128, 256], F32, "etmp") for _ in range(2)]
        g2 = [S.tile([128, 2, 128], F32, "g2") for _ in range(2)]
        ggh = [S.tile([128, 2, 128], BF16, "ggh") for _ in range(2)]
        ggl = [S.tile([128, 2, 128], BF16, "ggl") for _ in range(2)]
        Epos = [S.tile([128, 2, 128], F32, "Epos") for _ in range(2)]
        Eneg = [S.tile([128, 2, 128], F32, "Eneg") for _ in range(2)]
        ED = [S.tile([128, 2, 128], F32, "ED") for _ in range(2)]
        ke = [S.tile([128, 2, 128], BF16, "ke") for _ in range(2)]
        kd = [S.tile([128, 2, 128], BF16, "kd") for _ in range(2)]
        osb2 = [S.tile([128, 256], F32, "osb2") for _ in range(2)]
        sq2 = [S.tile([128, 256], BF16, "sq2") for _ in range(2)]
        for ch in range(2):
            wt, wa = wload([(lambda q: v3(q, 128, 8, 256)[:, :, 0:128], winv[:, :, ch * 128:(ch + 1) * 128]),
                            (lambda q: v3(q, 128, 8, 256)[:, :, 128:256], winv[:, :, 256 + ch * 128:256 + (ch + 1) * 128])])
            wv = v3(wa, 128, 8, 256)
            for bl in range(nb):
                p = PG()
                for k in range(8):
                    MM(p[:, :], wv[:, k, 0:128], hT[bl][:, k, :], k == 0, k == 7, [p], [wt, hT[bl]])
                ACT(qaT[:, bl * 512:(bl + 1) * 512], p[:, :], AF.Copy, [qaT], [p], scale=0.125)
                p = PG()
                for k in range(8):
                    MM(p[:, :], wv[:, k, 128:256], hT[bl][:, k, :], k == 0, k == 7, [p], [wt, hT[bl]])
                CP(kaT[:, bl * 512:(bl + 1) * 512], p[:, :], [kaT], [p])
            CK(11)
            wt, wv = wl3(winv[:, :, 1024 + ch * 256:1024 + (ch + 1) * 256], 128, 8, 256)
            for hl in range(2):
                for bl in range(nb):
                    p = PG()
                    for k in range(8):
                        MM(p[:, :], wv[:, k, hl * 128:(hl + 1) * 128], hT[bl][:, k, :], k == 0, k == 7, [p], [wt, hT[bl]])
                    ACT(sgT[:, hl, bl * 512:(bl + 1) * 512], p[:, :], AF.Silu, [sgT], [p])
            CK(12)
            wt1, wv1 = wl3(winv[:, :, 256 + ch * 128:256 + (ch + 1) * 128], 128, 8, 128)
            wt2, wv2 = wl3(winv[:, :, 512 + ch * 256:512 + (ch + 1) * 256], 128, 8, 256)
            CK(13)
            for tl in range(nt):
                if tl == 1:
                    CK(14)
                bl, tt = tl // 4, tl % 4
                p = PG()
                for k in range(8):
                    MM(p[:, 0:128], hT[bl][:, k, tt * 128:(tt + 1) * 128], wv1[:, k, :], k == 0, k == 7, [p], [wt1, hT[bl]])
                CP(katok[:, tl, :], p[:, 0:128], [katok], [p])
                p = PG()
                for k in range(8):
                    MM(p[:, 0:256], hT[bl][:, k, tt * 128:(tt + 1) * 128], wv2[:, k, :], k == 0, k == 7, [p], [wt2, hT[bl]])
                ACT(vatok[:, tl, :], p[:, 0:256], AF.Copy, [vatok], [p])

            CK(2)
            for si, (s0, sn) in enumerate(G["seqs"]):
                tiles = list(range(s0, s0 + sn))
                if kind == "s":
                    fw.dma("sp", d_in, Sf[:], st_f[i, 2 * ch:2 * ch + 2].rearrange("h k v -> (h k) v"), w=[Sf])
                    fw.dma("sp", d_in, Sb[:], st_b[i, 2 * ch:2 * ch + 2].rearrange("h k v -> (h k) v"), w=[Sb])
                else:
                    MSET(Sf[:], 0.0, [Sf])
                    MSET(Sb[:], 0.0, [Sb])
                def stA1(tl):
                    tsl = slice(tl * 128, (tl + 1) * 128)
                    pb_ = tl % 2
                    gg = g2[pb_]
                    for d in range(2):
                        base = 0 if d == 0 else 32
                        bsl = slice(base, base + 17)
                        csl = slice(ch * 128, (ch + 1) * 128)
                        pz = PG()
                        zr = pz[:, 0:128]
                        MM(zr, loTh[bsl, tsl], wgkh[bsl, i, csl], True, False, [pz], [loTh, wgkh])
                        MM(zr, loTh[bsl, tsl], wgkl[bsl, i, csl], False, False, [pz], [loTh, wgkl])
                        MM(zr, loTl[bsl, tsl], wgkh[bsl, i, csl], False, True, [pz], [loTl, wgkh])
                        ACT(etmp[pb_][:, d * 128:(d + 1) * 128], pz[:, 0:128], AF.Exp, [etmp[pb_]], [pz], scale=-1.0)
                    ACT(gg.ap[:, :, :].rearrange("p a b -> p (a b)"), etmp[pb_][:], AF.Ln, [gg], [etmp[pb_]], bias=1.0)
                    SPLIT(gg[:, :, :], ggh[pb_][:, :, :], ggl[pb_][:, :, :], ggh[pb_], ggl[pb_], gg)

                def stA2(tl):
                    pb_ = tl % 2
                    for d in range(2):
                        pc = PG()
                        MM(pc[:, 0:128], ggh[pb_][:, d, :], cstb[:, 1 + d, :], True, False, [pc], [ggh[pb_], cstb])
                        MM(pc[:, 0:128], ggl[pb_][:, d, :], cstb[:, 1 + d, :], False, True, [pc], [ggl[pb_], cstb])
                        ACT(Epos[pb_][:, d, :], pc[:, 0:128], AF.Exp, [Epos[pb_]], [pc])
                        ACT(Eneg[pb_][:, d, :], pc[:, 0:128], AF.Exp, [Eneg[pb_]], [pc], scale=-1.0)
                    for d in range(2):
                        pc = PG()
                        MM(pc[:, 0:128], cstb[:, 3 + d, :], ggh[pb_][:, d, :], True, False, [pc], [ggh[pb_], cstb])
                        MM(pc[:, 0:128], cstb[:, 3 + d, :], ggl[pb_][:, d, :], False, True, [pc], [ggl[pb_], cstb])
                        ACT(ED[pb_][:, d, :], pc[:, 0:128], AF.Exp, [ED[pb_]], [pc])

                def stA3(tl):
                    tsl = slice(tl * 128, (tl + 1) * 128)
                    pb_ = tl % 2
                    TT(qeS[:, tl, :, :], bc(qaT[:, tsl], [128, 2, 128], 1), Epos[pb_][:], ALU.mult, [qeS], [qaT, Epos[pb_]])
                    TT(ke[pb_][:], bc(kaT[:, tsl], [128, 2, 128], 1), Eneg[pb_][:], ALU.mult, [ke[pb_]], [kaT, Eneg[pb_]])
                    TT(kd[pb_][:], bc(katok[:, tl, :], [128, 2, 128], 1), ED[pb_][:], ALU.mult, [kd[pb_]], [katok, ED[pb_]])
                    CP(dsc[:, tl, 0, :], Epos[pb_][:, 0, 63:128:64], [dsc], [Epos[pb_]])
                    CP(dsc[:, tl, 1, :], Epos[pb_][:, 1, 0:128:64], [dsc], [Epos[pb_]])

                def stA4(tl):
                    pb_ = tl % 2
                    for hl in range(2):
                        for d in range(2):
                            pa = PG()
                            MM(pa[:, 0:128], ke[pb_][hl * 64:(hl + 1) * 64, d, :],
                               qeS[hl * 64:(hl + 1) * 64, tl, d, :], True, True, [pa], [ke[pb_], qeS])
                            TT(attS[:, tl, hl, d, :], pa[:, 0:128], cst[:, 5 + d, :], ALU.mult, [attS], [pa, cst])

                def stA5(tl):
                    pb_ = tl % 2
                    for half in range(2):
                        for d in range(2):
                            pu = PG()
                            MM(pu[:, 0:256], kd[pb_][half * 64:(half + 1) * 64, d, :],
                               vatok[half * 64:(half + 1) * 64, tl, :], True, True, [pu], [kd[pb_], vatok])
                            EVAC(Ust[0:64, tl, d, half, :], pu[0:64, 0:128], [Ust], [pu])
                            EVAC(Ust[64:128, tl, d, half, :], pu[64:128, 128:256], [Ust], [pu])

                for t0_ in range(0, len(tiles), 2):
                    pair = tiles[t0_:t0_ + 2]
                    for st_ in (stA1, stA2, stA3, stA4, stA5):
                        for tl in pair:
                            st_(tl)
                CK(3)
                for tl in tiles:
                    for half in (0, 1):
                        CP(SbfF[:, tl, half, :], Sf[:], [SbfF], [Sf])
                        STT(Sf[:], Sf[:], dsc[:, tl, 0, half:half + 1], Ust[:, tl, 0, half, :], ALU.mult, ALU.add, [Sf], [Sf, dsc, Ust])
                for tl in reversed(tiles):
                    for half in (1, 0):
                        CP(SbfB[:, tl, half, :], Sb[:], [SbfB], [Sb])
                        STT(Sb[:], Sb[:], dsc[:, tl, 1, half:half + 1], Ust[:, tl, 1, half, :], ALU.mult, ALU.add, [Sb], [Sb, dsc, Ust])
                if kind == "p":
                    fw.dma("sp", d_out, o_stf[si, i, 2 * ch:2 * ch + 2].rearrange("h k v -> (h k) v"), Sf[:], r=[Sf])
                    fw.dma("sp", d_out, o_stb[si, i, 2 * ch:2 * ch + 2].rearrange("h k v -> (h k) v"), Sb[:], r=[Sb])
                CK(4)
                def stB1(tl):
                    pb_ = tl % 2
                    for hl in range(2):
                        hs = slice(hl * 64, (hl + 1) * 64)
                        po = PG()
                        reg = po[:, 0:128]
                        MM(reg, vatok[:, tl, hl * 128:(hl + 1) * 128], attS[:, tl, hl, 0, :], True, False, [po], [vatok, attS])
                        MM(reg, vatok[:, tl, hl * 128:(hl + 1) * 128], attS[:, tl, hl, 1, :], False, False, [po], [vatok, attS])
                        for half in range(2):
                            MM(po[:, half * 64:half * 64 + 64], SbfF[hs, tl, half, :],
                               qeS[hs, tl, 0, half * 64:(half + 1) * 64], False, False, [po], [SbfF, qeS])
                        for half in range(2):
                            MM(po[:, half * 64:half * 64 + 64], SbfB[hs, tl, half, :],
                               qeS[hs, tl, 1, half * 64:(half + 1) * 64], False, half == 1, [po], [SbfB, qeS])
                        EVAC(osb2[pb_][:, hl * 128:(hl + 1) * 128], po[:, 0:128], [osb2[pb_]], [po])
                    TT(sq2[pb_][:], osb2[pb_][:], osb2[pb_][:], ALU.mult, [sq2[pb_]], [osb2[pb_]])

                def stB2(tl):
                    tsl = slice(tl * 128, (tl + 1) * 128)
                    pb_ = tl % 2
                    rs = rstd_of([sq2[pb_][:, :]], [sq2[pb_]], ones_bf[:, 1, :], 256)
                    STT(osb2[pb_][:], osb2[pb_][:], smallT[:, 16 + i:17 + i], rs[:, 0:256], ALU.mult, ALU.mult,
                        [osb2[pb_]], [osb2[pb_], rs, smallT])
                    TT(mixT[:, 2 * ch:2 * ch + 2, tsl], osb2[pb_].ap[:, :].rearrange("p (a b) -> p a b", a=2), sgT[:, :, tsl],
                       ALU.mult, [mixT], [osb2[pb_], sgT])

                for t0_ in range(0, len(tiles), 2):
                    pair = tiles[t0_:t0_ + 2]
                    for st_ in (stB1, stB2):
                        for tl in pair:
                            st_(tl)
        S.close()

        CK(5)
        oswaT = E.tile([64, 8, ntok], BF16, "oswaT")
        W = Phase()
        qbT = W.tile([128, 4, ntok], BF16, "qbT")
        kbT = W.tile([128, ntok], BF16, "kbT")
        vbtok = W.tile([128, nt, 128], BF16, "vbtok")
        Et = [W.tile([128, 512], BF16, "Et") for _ in range(3)]
        dn = W.tile([64, 512], F32, "dn")
        if kind == "s":
            rt = W.tile([128, 2, 1024], F32, "rt")
            fw.dma("sp", d_in, rt[:], cs64_d.rearrange("k p n -> p k n"), w=[rt])
            xf = W.tile([128, 512], F32, "xf")
            xfh = W.tile([128, 512], BF16, "xfh")
            xfl = W.tile([128, 512], BF16, "xfl")
            t1 = W.tile([128, 512], F32, "t1")
            kc_st = W.tile([128, 4, 128], F32, "kc_st")
            vc_st = W.tile([128, 4, 128], F32, "vc_st")
            kcT = W.tile([128, 512], BF16, "kcT")
            vc = W.tile([128, 4, 128], BF16, "vc")
        else:
            kvst = [W.tile([128, 256], F32, "kvst") for _ in range(2)]

        def rope64(p, bl, dst, dstT):
            ACT(xf[:], p[:, :], AF.Copy, [xf], [p])
            SPLIT(xf[:], xfh[:], xfl[:], xfh, xfl, xf)
            pp = PG()
            MM(pp[:, :], cstb[:, 9, :], xfh[:], True, False, [pp], [cstb, xfh])
            MM(pp[:, :], cstb[:, 9, :], xfl[:], False, True, [pp], [cstb, xfl])
            TT(t1[:], xf[:], rt[:, 0, bl * 512:(bl + 1) * 512], ALU.mult, [t1], [xf, rt])
            TT(xf[:], pp[:, :], rt[:, 1, bl * 512:(bl + 1) * 512], ALU.mult, [xf], [pp, rt])
            TT(dst, t1[:], xf[:], ALU.add, [dstT], [t1, xf])

        for ap_ in range(2):
            def v5(q):
                return q[0:128, 0:2048].rearrange("p (k a g d) -> p k a g d", k=8, a=2, g=2)
            wt, wa = wload([(lambda q, g=g, al=al: v5(q)[:, :, al, g, :],
                             winv[:, :, 1568 + g * 256 + (ap_ * 2 + al) * 64:1568 + g * 256 + (ap_ * 2 + al + 1) * 64])
                            for g in range(2) for al in range(2)])
            wv = v5(wa)
            for al in range(2):
                a = ap_ * 2 + al
                for bl in range(nb):
                    p = PG()
                    for k in range(8):
                        MM(p[:, :], wv[:, k, al, :, :].rearrange("p g d -> p (g d)"), hT[bl][:, k, :], k == 0, k == 7, [p], [wt, hT[bl]])
                    if kind == "p":
                        EVAC(qbT[:, a, bl * 512:(bl + 1) * 512], p[:, :], [qbT], [p])
                    else:
                        rope64(p, bl, qbT[:, a, bl * 512:(bl + 1) * 512], qbT)
        wt, wv = wl3(winv[:, :, 2080:2208], 128, 8, 128)
        for bl in range(nb):
            p = PG()
            for k in range(8):
                MM(p[:, :], wv[:, k, :], hT[bl][:, k, :], k == 0, k == 7, [p], [wt, hT[bl]])
            if kind == "p":
                EVAC(kbT[:, bl * 512:(bl + 1) * 512], p[:, :], [kbT], [p])
            else:
                rope64(p, bl, kbT[:, bl * 512:(bl + 1) * 512], kbT)
        wt, wv = wl3(winv[:, :, 2080:2336], 128, 8, 256)
        for tl in range(nt):
            bl, tt = tl // 4, tl % 4
            p = PG()
            for k in range(8):
                MM(p[:, 0:256], hT[bl][:, k, tt * 128:(tt + 1) * 128], wv[:, k, :], k == 0, k == 7, [p], [wt, hT[bl]])
            if kind == "p":
                ks = kvst[tl % 2]
                ACT(ks[:], p[:, 0:256], AF.Copy, [ks], [p])
                CP(vbtok[:, tl, :], ks[:, 128:256], [vbtok], [ks])
            else:
                CP(vbtok[:, tl, :], p[:, 128:256], [vbtok], [p])
            if kind == "p":
                si, r0 = tl // 2, (tl % 2) * 128
                fw.dma("sp", d_out, o_swk[si, i, r0:r0 + 128, :], ks[:, 0:128], r=[ks])
                fw.dma("sp", d_out, o_swv[si, i, r0:r0 + 128, :], ks[:, 128:256], r=[ks])
        if kind == "s":
            fw.dma("sp", d_in, kc_st[:], cswa_k[i].rearrange("(t p) f -> p t f", p=128), w=[kc_st])
            fw.dma("sp", d_in, vc_st[:], cswa_v[i].rearrange("(t p) f -> p t f", p=128), w=[vc_st])
            p = PG()
            for t in range(4):
                TR(p[:, t * 128:(t + 1) * 128], kc_st[:, t, :], ident, [p], [kc_st, cst])
            CP(kcT[:], p[:, :], [kcT], [p])
            CP(vc[:], vc_st[:], [vc], [vc_st])
        CK(6)
        rot["wide"] = False
        SC = 0.125
        eti = 0
        if kind == "p":
            for si, (s0, sn) in enumerate(G["seqs"]):
                q0 = s0 * 128
                for kv in range(2):
                    ks_ = slice(kv * 64, (kv + 1) * 64)
                    for apair in range(2):
                        po, pd = ACCPAIR()

                        def score_p(kt):
                            tl = s0 + kt
                            ps = PG()
                            MM(ps[:, :], kbT[ks_, tl * 128:(tl + 1) * 128], qbT[ks_, 2 * apair:2 * apair + 2, q0:q0 + 256],
                               True, True, [ps], [kbT, qbT])
                            return ps
                        ps_next = score_p(0)
                        for kt in range(sn):
                            tl = s0 + kt
                            ps = ps_next
                            if kt + 1 < sn:
                                ps_next = score_p(kt + 1)
                            et = Et[eti % 3]
                            eti += 1
                            ACT(et[:], ps[:, :], AF.Exp, [et], [ps], scale=SC)
                            MM(po[0:64, :], vbtok[:, tl, kv * 64:(kv + 1) * 64], et[:], kt == 0, kt == sn - 1, [po], [vbtok, et])
                            MM(pd[0:64, :], ones_bf[:, 2, 0:64], et[:], kt == 0, kt == sn - 1, [pd], [ones_bf, et])
                            if kt == min(1, sn - 1):
                                run_deferred()

                        def fin_p(po=po, pd=pd, kv=kv, apair=apair, q0=q0):
                            for al in range(2):
                                col = i * 8 + kv * 4 + apair * 2 + al
                                TS(dn[0:64, al * 256:(al + 1) * 256], pd[0:64, al * 256:(al + 1) * 256], esink[0:64, col:col + 1],
                                   ALU.add, [dn], [pd, esink])
                            ACT(dn[:], dn[:], AF.Ln, [dn], [dn])
                            ACT(dn[:], dn[:], AF.Exp, [dn], [dn], scale=-1.0)
                            h0 = kv * 4 + apair * 2
                            TT(oswaT[0:64, h0:h0 + 2, q0:q0 + 256], po.ap[0:64, :].rearrange("p (a b) -> p a b", a=2),
                               dn.ap[0:64, :].rearrange("p (a b) -> p a b", a=2), ALU.mult, [oswaT], [po, dn])
                        defer(fin_p)
        else:
            for qt in range(nt):
                for kv in range(2):
                    ks_ = slice(kv * 64, (kv + 1) * 64)
                    keys = [("c", t) for t in range(4)]
                    if qt > 0:
                        keys.append(("l", qt - 1))
                    keys.append(("l", qt))
                    if qt < nt - 1:
                        keys.append(("l", qt + 1))
                    po, pd = ACCPAIR()

                    def score_s(ki):
                        kk, kt = keys[ki]
                        ps = PG()
                        if kk == "c":
                            MM(ps[:, :], kcT[ks_, kt * 128:(kt + 1) * 128], qbT[ks_, :, qt * 128:(qt + 1) * 128], True, True, [ps], [kcT, qbT])
                        else:
                            MM(ps[:, :], kbT[ks_, kt * 128:(kt + 1) * 128], qbT[ks_, :, qt * 128:(qt + 1) * 128], True, True, [ps], [kbT, qbT])
                        return ps
                    ps_next = score_s(0)
                    for ki, (kk, kt) in enumerate(keys):
                        ps = ps_next
                        if ki + 1 < len(keys):
                            ps_next = score_s(ki + 1)
                        et = Et[eti % 3]
                        eti += 1
                        ACT(et[:], ps[:, :], AF.Exp, [et], [ps], scale=SC)
                        if kk == "l" and kt != qt:
                            mk = cst[:, 8, :] if kt < qt else cst[:, 7, :]
                            etv = et.ap[:, :].rearrange("p (a b) -> p a b", a=4)
                            TT(etv, etv, bc(mk, [128, 4, 128], 1), ALU.mult, [et], [et, cst])
                        vsrc, vT = (vc[:, kt, kv * 64:(kv + 1) * 64], vc) if kk == "c" else (vbtok[:, kt, kv * 64:(kv + 1) * 64], vbtok)
                        last = ki == len(keys) - 1
                        MM(po[0:64, :], vsrc, et[:], ki == 0, last, [po], [vT, et])
                        MM(pd[0:64, :], ones_bf[:, 2, 0:64], et[:], ki == 0, last, [pd], [ones_bf, et])
                        if ki == 2:
                            run_deferred()

                    def fin_s(po=po, pd=pd, kv=kv, qt=qt):
                        c0 = i * 8 + kv * 4
                        dnv = dn.ap[0:64, :].rearrange("p (a b) -> p a b", a=4)
                        TT(dnv, pd.ap[0:64, :].rearrange("p (a b) -> p a b", a=4), bc(esink[0:64, c0:c0 + 4], [64, 4, 128], 2),
                           ALU.add, [dn], [pd, esink])
                        ACT(dn[:], dn[:], AF.Ln, [dn], [dn])
                        ACT(dn[:], dn[:], AF.Exp, [dn], [dn], scale=-1.0)
                        TT(oswaT[0:64, kv * 4:(kv + 1) * 4, qt * 128:(qt + 1) * 128],
                           po.ap[0:64, :].rearrange("p (a b) -> p a b", a=4), dnv, ALU.mult, [oswaT], [po, dn])
                    defer(fin_s)
        run_deferred()
        rot["wide"] = True
        CK(7)

        O = W
        osb = O.tile([128, 8, 512], F32, "osb")
        woa_v = w_out_ab[i][0:512, :].rearrange("(h p) n -> p h n", p=128)
        wob_v = w_out_ab[i][512:1024, :].rearrange("(h p) n -> p h n", p=64)
        for bl, b in enumerate(blocks):
            bs = slice(bl * 512, (bl + 1) * 512)
            for npc in range(4):
                wta, wva = wl3(woa_v[:, :, npc * 256:(npc + 1) * 256], 128, 4, 256)
                wtb, wvb = wl3(wob_v[:, :, npc * 256:(npc + 1) * 256], 64, 8, 256)
                for nn in range(2):
                    n = npc * 2 + nn
                    p = PG()
                    for h in range(4):
                        MM(p[:, :], wva[:, h, nn * 128:(nn + 1) * 128], mixT[:, h, bs], h == 0, False, [p], [wta, mixT])
                    for h in range(8):
                        MM(p[:, :], wvb[0:64, h, nn * 128:(nn + 1) * 128], oswaT[0:64, h, bs], False, h == 7, [p], [wtb, oswaT])
                    post_evac(n, p, osb, sq)
            post_finish(b, mv, 0, osb, sq)
        O.close()
        E.close()

    def odd_mixer(l, i, mv, G):
        kind, nt, blocks = G["kind"], G["nt"], G["blocks"]
        ntok = nt * 128
        nb = len(blocks)
        koff = 512 if kind == "s" else 0
        nkeys = ntok + koff
        E = Phase()
        cqn = E.tile([128, 3, ntok], BF16, "cqn")
        ckvT = E.tile([128, 2, nkeys], BF16, "ckvT")
        krT = E.tile([96, nkeys], BF16, "krT")
        oT = E.tile([64, 16, ntok], BF16, "oT")
        t1 = E.tile([96, 512], F32, "t1")
        kr32 = E.tile([96, 512], F32, "kr32")
        krh = E.tile([96, 512], BF16, "krh")
        krl = E.tile([96, 512], BF16, "krl")
        if kind == "s":
            rt32 = E.tile([96, 2, 1024], F32, "rt32")
            fw.dma("sp", d_in, rt32[:], cs32_d.rearrange("k p n -> p k n"), w=[rt32])
        wdv = w_mla_down[i].rearrange("(k p) n -> p k n", p=128)

        def rope32(p, bl, dst, dstT):
            ACT(kr32[64:96, :], p[64:96, :], AF.Copy, [kr32], [p])
            SPLIT(kr32[64:96, :], krh[64:96, :], krl[64:96, :], krh, krl, kr32)
            pp = PG()
            MM(pp[0:96, :], pm32b[64:96, 0:96], krh[64:96, :], True, False, [pp], [pm32b, krh])
            MM(pp[0:96, :], pm32b[64:96, 0:96], krl[64:96, :], False, True, [pp], [pm32b, krl])
            TT(t1[64:96, :], kr32[64:96, :], rt32[64:96, 0, bl * 512:(bl + 1) * 512], ALU.mult, [t1], [kr32, rt32])
            TT(kr32[64:96, :], pp[64:96, :], rt32[64:96, 1, bl * 512:(bl + 1) * 512], ALU.mult, [kr32], [pp, rt32])
            TT(dst, t1[64:96, :], kr32[64:96, :], ALU.add, [dstT], [t1, kr32])

        D = Phase()
        hT = [D.tile([128, 8, 512], BF16, "hT") for _ in range(nb)]
        sq = D.tile([128, 8, 512], BF16, "sq")
        for bl, b in enumerate(blocks):
            pre(b, mv, 0, hT[bl], sq)
        cq32 = D.tile([128, 3, 512], F32, "cq32")
        ckv32 = D.tile([128, 2, 512], F32, "ckv32")
        if kind == "p":
            ost = D.tile([128, 4, 256], F32, "ost")
            krst = D.tile([128, 4, 32], F32, "krst")
        else:
            ckv_st = D.tile([128, 4, 256], F32, "ckv_st")
            kr_st = D.tile([128, 4, 96], F32, "kr_st")
        for bl, b in enumerate(blocks):
            bs = slice(bl * 512, (bl + 1) * 512)
            for c in range(3):
                wt, wv = wl3(wdv[:, :, c * 128:(c + 1) * 128], 128, 8, 128)
                p = PG()
                for k in range(8):
                    MM(p[:, :], wv[:, k, :], hT[bl][:, k, :], k == 0, k == 7, [p], [wt, hT[bl]])
                ACT(cq32[:, c, :], p[:, :], AF.Copy, [cq32], [p])
                ACT(sq[:, c, :], p[:, :], AF.Square, [sq], [p], scale=float(1.0 / np.sqrt(3.0)))
            rs = rstd_of([sq[:, c, :] for c in range(3)], [sq], ones_bf[:, 1, :], 512)
            for c in range(3):
                STT(cqn[:, c, bs], cq32[:, c, :], smallT[:, 18 + i * 3 + c:19 + i * 3 + c], rs[:, :], ALU.mult, ALU.mult,
                    [cqn], [cq32, rs, smallT])
            for c in range(2):
                wt, wv = wl3(wdv[:, :, 384 + c * 128:384 + (c + 1) * 128], 128, 8, 128)
                p = PG()
                for k in range(8):
                    MM(p[:, :], wv[:, k, :], hT[bl][:, k, :], k == 0, k == 7, [p], [wt, hT[bl]])
                ACT(ckv32[:, c, :], p[:, :], AF.Copy, [ckv32], [p])
                ACT(sq[:, 4 + c, :], p[:, :], AF.Square, [sq], [p], scale=float(1.0 / np.sqrt(2.0)))
            rs = rstd_of([sq[:, 4 + c, :] for c in range(2)], [sq], ones_bf[:, 1, :], 512)
            for c in range(2):
                STT(ckv32[:, c, :], ckv32[:, c, :], smallT[:, 24 + i * 2 + c:25 + i * 2 + c], rs[:, :], ALU.mult, ALU.mult,
                    [ckv32], [ckv32, rs, smallT])
                CP(ckvT[:, c, koff + bl * 512:koff + (bl + 1) * 512], ckv32[:, c, :], [ckvT], [ckv32])
            if kind == "p":
                for c in range(2):
                    p = PG()
                    for tt in range(4):
                        TR(p[:, tt * 128:(tt + 1) * 128], ckv32[:, c, tt * 128:(tt + 1) * 128], ident, [p], [ckv32, cst])
                    EVAC(ost[:, :, c * 128:(c + 1) * 128], p.ap[:, :].rearrange("p (a b) -> p a b", a=4), [ost], [p])
                for tt in range(4):
                    si, r0 = tt // 2, (tt % 2) * 128
                    fw.dma("sp", d_out, o_ckv[si, i, r0:r0 + 128, :], ost[:, tt, :], r=[ost])
            wt, wa = wload([(lambda q: v3(q, 128, 8, 96)[:, :, 64:96], wdv[:, :, 640:672])])
            wv = v3(wa, 128, 8, 96)
            p = PG()
            for k in range(8):
                MM(p[0:96, :], wv[:, k, :], hT[bl][:, k, :], k == 0, k == 7, [p], [wt, hT[bl]])
            if kind == "p":
                ACT(kr32[64:96, :], p[64:96, :], AF.Copy, [kr32], [p])
                CP(krT[64:96, bs], kr32[64:96, :], [krT], [kr32])
                pp = PG()
                for tt in range(4):
                    TR(pp[:, tt * 32:(tt + 1) * 32], kr32[64:96, tt * 128:(tt + 1) * 128], ident[64:96, 64:96], [pp], [kr32, cst])
                CP(krst[:], pp.ap[:, 0:128].rearrange("p (a b) -> p a b", a=4), [krst], [pp])
                for tt in range(4):
                    si, r0 = tt // 2, (tt % 2) * 128
                    fw.dma("sp", d_out, o_kr[si, i, r0:r0 + 128, :], krst[:, tt, :], r=[krst])
            else:
                rope32(p, bl, krT[64:96, koff + bl * 512:koff + (bl + 1) * 512], krT)
        if kind == "s":
            fw.dma("sp", d_in, ckv_st[:], cm_ckv[i].rearrange("(t p) f -> p t f", p=128), w=[ckv_st])
            MSET(kr_st[:], 0.0, [kr_st])
            fw.dma("sp", d_in, kr_st[:, :, 64:96], cm_kr[i].rearrange("(t p) f -> p t f", p=128), w=[kr_st])
            for c in range(2):
                p = PG()
                for t in range(4):
                    TR(p[:, t * 128:(t + 1) * 128], ckv_st[:, t, c * 128:(c + 1) * 128], ident, [p], [ckv_st, cst])
                EVAC(ckvT[:, c, 0:512], p[:, :], [ckvT], [p])
            p = PG()
            for t in range(4):
                TR(p[0:96, t * 128:(t + 1) * 128], kr_st[:, t, :], ident, [p], [kr_st, cst])
            CP(krT[64:96, 0:512], p[64:96, :], [krT], [p])
        D.close()

        wuq_v = w_mla_uq[i].rearrange("(k p) n -> p k n", p=128)
        wukv_v = w_mla_ukv[i].rearrange("(k p) (h t d) -> p k h t d", p=128, t=2, d=64)
        SCL = float(96.0 ** -0.5)
        nkb = nkeys // 512
        nkt = nkeys // 128
        H = Phase()
        rot["wide"] = False
        QT = H.tile([96, 4, ntok], BF16, "QT")
        KT = H.tile([96, 4, nkeys], BF16, "KT")
        Vg = H.tile([128, nkt, 256], BF16, "Vg")
        Et = [H.tile([128, 512], BF16, "Et") for _ in range(4)]
        dn = H.tile([64, 512], F32, "dn")
        for hg in range(4):
            wt, wv = wl3(wuq_v[:, :, hg * 384:(hg + 1) * 384], 128, 3, 384)
            for hl in range(4):
                for bl in range(nb):
                    bs = slice(bl * 512, (bl + 1) * 512)
                    p = PG()
                    for k in range(3):
                        MM(p[0:96, :], wv[:, k, hl * 96:(hl + 1) * 96], cqn[:, k, bs], k == 0, k == 2, [p], [wt, cqn])
                    if kind == "p":
                        EVAC(QT[0:64, hl, bs], p[0:64, :], [QT], [p])
                        EVAC(QT[64:96, hl, bs], p[64:96, :], [QT], [p])
                    else:
                        ACT(QT[0:64, hl, bs], p[0:64, :], AF.Copy, [QT], [p])
                        rope32(p, bl, QT[64:96, hl, bs], QT)

            def vk(q):
                return q[0:128, 0:512].rearrange("p (k h d) -> p k h d", k=2, h=4)
            wtk, wak = wload([(lambda q, k=k: vk(q)[:, k, :, :], wukv_v[:, k, hg * 4:(hg + 1) * 4, 0, :]) for k in range(2)])
            wvk = vk(wak)
            wtv, wav = wload([(lambda q, k=k: vk(q)[:, k, :, :], wukv_v[:, k, hg * 4:(hg + 1) * 4, 1, :]) for k in range(2)])
            wvv = vk(wav)
            for hl in range(4):
                for kb in range(nkb):
                    p = PG()
                    for k in range(2):
                        MM(p[0:64, :], wvk[:, k, hl, :], ckvT[:, k, kb * 512:(kb + 1) * 512], k == 0, k == 1, [p], [wtk, ckvT])
                    EVAC(KT[0:64, hl, kb * 512:(kb + 1) * 512], p[0:64, :], [KT], [p])
                CP(KT[64:96, hl, :], krT[64:96, :], [KT], [krT])
            for kt in range(nkt):
                p = PG()
                for k in range(2):
                    MM(p[:, 0:256], ckvT[:, k, kt * 128:(kt + 1) * 128], wvv[:, k, :, :].rearrange("p h d -> p (h d)"),
                       k == 0, k == 1, [p], [wtv, ckvT])
                EVAC(Vg[:, kt, :], p[:, 0:256], [Vg], [p])
            eti = 0
            for hl in range(4):
                h = hg * 4 + hl
                if kind == "s":
                    jobs = [(qb * 512, 512, list(range(nkt))) for qb in range(nb)]
                else:
                    jobs = [(s0 * 128, sn * 128, list(range(s0, s0 + sn))) for (s0, sn) in G["seqs"]]
                for (q0, qn, kts) in jobs:
                    po, pd = ACCPAIR()

                    def score_m(ki):
                        kt = kts[ki]
                        ps = PG()
                        MM(ps[:, 0:qn], KT[0:96, hl, kt * 128:(kt + 1) * 128], QT[0:96, hl, q0:q0 + qn], True, True, [ps], [KT, QT])
                        return ps
                    psq = [score_m(0)]
                    if len(kts) > 1:
                        psq.append(score_m(1))
                    for ki, kt in enumerate(kts):
                        ps = psq.pop(0)
                        if ki + 2 < len(kts):
                            psq.append(score_m(ki + 2))
                        et = Et[eti % 4]
                        eti += 1
                        ACT(et[:, 0:qn], ps[:, 0:qn], AF.Exp, [et], [ps], scale=SCL)
                        last = ki == len(kts) - 1
                        MM(po[0:64, 0:qn], Vg[:, kt, hl * 64:(hl + 1) * 64], et[:, 0:qn], ki == 0, last, [po], [Vg, et])
                        MM(pd[0:64, 0:qn], ones_bf[:, 2, 0:64], et[:, 0:qn], ki == 0, last, [pd], [ones_bf, et])
                        if ki == min(2, len(kts) - 1):
                            run_deferred()

                    def fin_m(po=po, pd=pd, h=h, q0=q0, qn=qn):
                        ACT(dn[0:64, 0:qn], pd[0:64, 0:qn], AF.Ln, [dn], [pd])
                        ACT(dn[0:64, 0:qn], dn[0:64, 0:qn], AF.Exp, [dn], [dn], scale=-1.0)
                        TT(oT[0:64, h, q0:q0 + qn], po[0:64, 0:qn], dn[0:64, 0:qn], ALU.mult, [oT], [po, dn])
                    defer(fin_m)
        run_deferred()
        rot["wide"] = True
        H.close()

        O = Phase()
        osb = O.tile([128, 8, 512], F32, "osb")
        sq = O.tile([128, 8, 512], BF16, "sqo")
        wo_v = w_mla_o[i].rearrange("(h p) n -> p h n", p=64)
        for bl, b in enumerate(blocks):
            bs = slice(bl * 512, (bl + 1) * 512)
            for n in range(8):
                wt, wv = wl3(wo_v[:, :, n * 128:(n + 1) * 128], 64, 16, 128)
                p = PG()
                for h in range(16):
                    MM(p[:, :], wv[0:64, h, :], oT[0:64, h, bs], h == 0, h == 15, [p], [wt, oT])
                post_evac(n, p, osb, sq)
            post_finish(b, mv, 0, osb, sq)
        O.close()
        E.close()

    GROUPS = [dict(t0=0, nt=4, blocks=[0], seqs=[(0, 2), (2, 2)], kind="p"),
              dict(t0=4, nt=8, blocks=[1, 2], seqs=[(0, 8)], kind="s")]

    for l in range(n_layers):
        mv = modulation(l)
        i = l // 2
        if do_mixer:
            for gi in groups_sel:
                try:
                    if l % 2 == 0:
                        even_mixer(l, i, mv, GROUPS[gi])
                    else:
                        odd_mixer(l, i, mv, GROUPS[gi])
                except Bail:
                    for ph_ in reversed(list(open_phases)):
                        ph_.close()
        if do_mlp:
            M = Phase()
            hid = M.tile([128, 32, 512], BF16, "hid")
            hTm = [M.tile([128, 8, 512], BF16, "hTm") for _ in range(2)]
            osb = M.tile([128, 8, 512], F32, "osbm")
            sqm = M.tile([128, 8, 512], BF16, "sqm")
            sqp = M.tile([128, 8, 512], BF16, "sqp")
            pre(0, mv, 1, hTm[0], sqp)
            for b in range(3):
                nxt = (lambda b=b: pre(b + 1, mv, 1, hTm[(b + 1) % 2], sqp)) if b < 2 else None
                mlp(l, b, mv, hTm[b % 2], hid, osb, sqm, nxt)
            M.close()

    Y = Phase()
    stage = [Y.tile([128, 1024], F32, "ystage") for _ in range(2)]
    for t in range(12):
        stg = stage[t % 2]
        for half in range(2):
            p = PG()
            for cc in range(4):
                c = half * 4 + cc
                TR(p[:, cc * 128:(cc + 1) * 128], xT.ap[:, c, t * 128:(t + 1) * 128], ident, [p], [xTb[t // 4], cst])
            EVAC(stg[:, half * 512:(half + 1) * 512], p[:, :], [stg], [p])
        fw.dma("sp", d_out, y_out[t * 128:(t + 1) * 128, :], stg[:], r=[stg])
    fw.barrier(fw.spool)
    fw.flush(final=True, final_dsems=fw.spool)
    Y.es.close()
    fw.es.close()
    print("program: %d instrs; engine counts:" % fw.ninstr, {n: e.cnt for n, e in fw.engs.items()})
    return nc


_CACHE = {}


def _run(inputs, n_layers=4, do_mixer=True, do_mlp=True, trace=False, groups_sel=(0, 1)):
    key = (n_layers, do_mixer, do_mlp, tuple(groups_sel))
    if key not in _CACHE:
        _CACHE[key] = build_program(n_layers, do_mixer, do_mlp, tuple(groups_sel))
    nc = _CACHE[key]
    f = lambda a: np.ascontiguousarray(np.asarray(a, dtype=np.float32))
    consts, cs64, cs32, pm32 = make_consts()
    xp = f(inputs["x_prompt"])
    xs = f(inputs["x_sample"])
    shared = {k: f(inputs[k]) for k in ("w_mod", "b_mod", "g_norm", "w_ff1", "w_ff2", "w_in_ab", "w_gk_f", "b_gk_f",
                                        "w_gk_b", "b_gk_b", "g_gla", "swa_sink", "w_out_ab", "w_mla_down", "g_mla_q",
                                        "g_mla_kv", "w_mla_uq", "w_mla_ukv", "w_mla_o")}
    shared.update(consts=consts, cs64=cs64, cs32=cs32, pm32=pm32)
    c = f(inputs["c"])
    c_ctx = f(inputs["c_ctx"])
    in_maps = []
    for core in range(NCORES):
        sb = core % 2
        m = dict(shared)
        m["xin"] = np.ascontiguousarray(np.concatenate([xp[2 * core], xp[2 * core + 1], xs[sb]], axis=0))
        m["cvec"] = np.ascontiguousarray(np.stack([c_ctx, c[sb]], axis=0))
        m["st_f"] = f(inputs["state_gla_fwd"][sb])
        m["st_b"] = f(inputs["state_gla_bwd"][sb])
        m["cswa_k"] = f(inputs["cache_swa_k"][sb]).reshape(2, 512, 128)
        m["cswa_v"] = f(inputs["cache_swa_v"][sb]).reshape(2, 512, 128)
        m["cm_ckv"] = f(inputs["cache_mla_ckv"][sb])
        m["cm_kr"] = f(inputs["cache_mla_kr"][sb])
        in_maps.append(m)
    res = run_bass_kernel_spmd(nc, in_maps, core_ids=list(range(NCORES)), trace=trace)
    R = res.results
    y_p = np.stack([R[cidx]["y"][s * 256:(s + 1) * 256] for cidx in range(NCORES) for s in range(2)], axis=0)
    y_s = np.stack([R[b]["y"][512:1536] for b in range(2)], axis=0)

    def gather(name, shape):
        return np.concatenate([R[cidx][name] for cidx in range(NCORES)], axis=0).reshape(shape)
    outs = (y_p.astype(np.float32), y_s.astype(np.float32),
            gather("o_stf", (16, 2, 4, 64, 128)), gather("o_stb", (16, 2, 4, 64, 128)),
            gather("o_swk", (16, 2, 256, 2, 64)), gather("o_swv", (16, 2, 256, 2, 64)),
            gather("o_ckv", (16, 2, 256, 256)), gather("o_kr", (16, 2, 256, 32)))
    return outs, res


def kernel(**inputs):
    outs, _ = _run(inputs)
    return outs
```

```python
from contextlib import ExitStack
import numpy as np
import concourse.bass as bass
import concourse.mybir as mybir
from concourse.bass_utils import run_bass_kernel_spmd

F32 = mybir.dt.float32
BF16 = mybir.dt.bfloat16
AF = mybir.ActivationFunctionType
ALU = mybir.AluOpType

EPS = 1e-6
NCORES = 8


class Eng:
    def __init__(self, name, sem):
        self.name = name
        self.sem = sem
        self.cnt = 0
        self.seen = {}
        self.q = []
        self.pending = []


class DSem:
    def __init__(self, sem, serial=False):
        self.sem = sem
        self.cnt = 0
        self.serial = serial


class T:
    __slots__ = ("ap", "w", "r", "name", "scoped")

    def __init__(self, ap, name=""):
        self.ap = ap
        self.w = None
        self.r = []
        self.name = name
        self.scoped = False

    def __getitem__(self, k):
        return self.ap[k]


class FW:
    def __init__(self, nc):
        self.nc = nc
        self.es = ExitStack()
        self.engs = {}
        for n in ("pe", "act", "dve", "pool", "sp"):
            self.engs[n] = Eng(n, self.es.enter_context(nc.semaphore("s_" + n)))
        self.nd = 0
        self.nt = 0
        self.ninstr = 0
        self.spool = [self.dsem(serial=True) for _ in range(12)]
        self.spi = 0

    def dsem(self, serial=False):
        self.nd += 1
        return DSem(self.es.enter_context(self.nc.semaphore("d%d" % self.nd)), serial)

    def tile(self, shape, dt, name=None, stack=None):
        self.nt += 1
        es = stack if stack is not None else self.es
        t = es.enter_context(self.nc.sbuf_tensor("sb_" + (name or ("t%d" % self.nt)), list(shape), dt))
        return T(t, name or "")

    def ptile(self, shape, dt, name=None):
        self.nt += 1
        t = self.es.enter_context(self.nc.psum_tensor(name or ("p%d" % self.nt), list(shape), dt))
        return T(t, name or "")

    def _need(self, eng, deps):
        best = {}
        for so, v, raw in deps:
            if isinstance(so, DSem) and not so.serial:
                v = so.cnt
            k = id(so)
            if v > best.get(k, (None, 0))[1]:
                best[k] = (so, v)
        for k, (so, v) in best.items():
            if so is eng and eng.name == "pe":
                continue
            if eng.seen.get(k, 0) >= v:
                continue
            eng.seen[k] = v
            eng.q.append(lambda e, sem=so.sem, v=v: e.wait_ge(sem, v))

    @staticmethod
    def _deps(w, r):
        deps = []
        for t in r:
            if t.w is not None:
                deps.append((t.w[0], t.w[1], True))
        for t in w:
            if t.w is not None:
                deps.append((t.w[0], t.w[1], False))
            deps.extend((so, v, False) for so, v in t.r)
        return deps

    @staticmethod
    def _compact(t):
        m = {}
        for so, v in t.r:
            if isinstance(so, DSem) and not so.serial:
                v = so.cnt
            if v > m.get(id(so), (None, 0))[1]:
                m[id(so)] = (so, v)
        t.r = list(m.values())

    def _pool_pending(self, eng, w, r):
        if eng.pending and any(t.scoped for t in list(w) + list(r)):
            for so, v in eng.pending:
                k = id(so)
                if eng.seen.get(k, 0) >= v:
                    continue
                eng.seen[k] = v
                eng.q.append(lambda e, sem=so.sem, v=v: e.wait_ge(sem, v))
            eng.pending = []

    def op(self, en, fn, w=(), r=(), signal=True):
        eng = self.engs[en]
        self._pool_pending(eng, w, r)
        self._need(eng, self._deps(w, r))
        self.ninstr += 1
        if signal:
            eng.cnt += 1
            c = eng.cnt
            eng.q.append(lambda e, fn=fn, sem=eng.sem: fn(e).then_inc(sem, 1))
        else:
            c = eng.cnt + 1
            eng.q.append(lambda e, fn=fn: fn(e))
        for t in w:
            t.w = (eng, c)
            t.r = []
        for t in r:
            t.r.append((eng, c))
            if len(t.r) > 16:
                self._compact(t)

    def dma(self, en, ds, out, in_, w=(), r=()):
        eng = self.engs[en]
        self._pool_pending(eng, w, r)
        if ds is None:
            ds = self.spool[self.spi % len(self.spool)]
            self.spi += 1
            if ds.cnt and eng.seen.get(id(ds), 0) < ds.cnt:
                eng.seen[id(ds)] = ds.cnt
                eng.q.append(lambda e, sem=ds.sem, v=ds.cnt: e.wait_ge(sem, v))
        self._need(eng, self._deps(w, r))
        self.ninstr += 1
        ds.cnt += 16
        c = ds.cnt
        eng.q.append(lambda e, out=out, in_=in_, sem=ds.sem: e.dma_start(out=out, in_=in_).then_inc(sem, 16))
        for t in w:
            t.w = (ds, c)
            t.r = []
        for t in r:
            t.r.append((ds, c))
            if len(t.r) > 16:
                self._compact(t)

    def barrier(self, dsems=()):
        for eng in self.engs.values():
            waits = []
            for other in self.engs.values():
                if other is eng or other.cnt == 0:
                    continue
                waits.append((other, other.cnt))
            for ds in dsems:
                if ds.cnt:
                    waits.append((ds, ds.cnt))
            if eng.name == "pool":
                eng.pending = waits
                continue
            for so, v in waits:
                k = id(so)
                if eng.seen.get(k, 0) >= v:
                    continue
                eng.seen[k] = v
                eng.q.append(lambda e, sem=so.sem, v=v: e.wait_ge(sem, v))

    def flush(self, final=False, final_dsems=()):
        nc = self.nc
        if final:
            sp = self.engs["sp"]
            for ds in final_dsems:
                if ds.cnt:
                    sp.q.append(lambda e, sem=ds.sem, v=ds.cnt: e.wait_ge(sem, v))
            for n in ("pe", "act", "dve", "pool"):
                eg = self.engs[n]
                if eg.cnt:
                    sp.q.append(lambda e, sem=eg.sem, v=eg.cnt: e.wait_ge(sem, v))
        qs = {n: self.engs[n].q for n in self.engs}
        for n in self.engs:
            self.engs[n].q = []
        with nc.Block() as block:
            @block.tensor
            def _(e):
                for f in qs["pe"]:
                    f(e)

            @block.scalar
            def _(e):
                for f in qs["act"]:
                    f(e)

            @block.vector
            def _(e):
                for f in qs["dve"]:
                    f(e)

            @block.gpsimd
            def _(e):
                for f in qs["pool"]:
                    f(e)

            @block.sync
            def _(e):
                for f in qs["sp"]:
                    f(e)


def make_consts():
    s = np.arange(128)[:, None]
    t = np.arange(128)[None, :]
    same = (s // 64) == (t // 64)
    c = np.zeros((10, 128, 128), np.float32)
    c[0] = np.eye(128)
    k = -1.0 / 16.0
    c[1] = k * (same & (s <= t))
    c[2] = k * (same & (s >= t))
    c[3] = k * (same & (s > t))
    c[4] = k * (same & (s < t))
    c[5] = (same & (s <= t))
    c[6] = (same & (s >= t))
    c[7] = (s <= t)
    c[8] = (s >= t)
    pm = np.zeros((128, 128), np.float32)
    for p in range(128):
        q = p % 64
        b = (q % 32) // 16
        partner = p + 16 if b == 0 else p - 16
        pm[partner, p] = 1.0
    c[9] = pm
    L = 1024
    rows = np.repeat(np.arange(L // 64, dtype=np.float32), 64)
    cols = np.tile(np.arange(64, dtype=np.float32), L // 64)
    pos = [rows, cols]

    def tables(hd, nrows, base):
        nf = hd // 4
        inv = (10000.0 ** (-np.arange(nf, dtype=np.float32) / nf)).astype(np.float32)
        cs = np.zeros((2, nrows, L), np.float32)
        for p in range(nrows - base):
            q = p % hd
            a = q // (2 * nf)
            b = (q % (2 * nf)) // nf
            f = q % nf
            ang = (pos[a] * inv[f]).astype(np.float32)
            cs[0, base + p] = np.cos(ang)
            cs[1, base + p] = -np.sin(ang) if b == 0 else np.sin(ang)
        return cs

    cs64 = tables(64, 128, 0)
    cs32 = tables(32, 96, 64)
    pm32 = np.zeros((96, 96), np.float32)
    for q in range(32):
        b = (q % 16) // 8
        partner = q + 8 if b == 0 else q - 8
        pm32[64 + partner, 64 + q] = 1.0
    return c, cs64, cs32, pm32


def build_program(n_layers=4, do_mixer=True, do_mlp=True, groups_sel=(0, 1)):
    nc = bass.Bass("TRN2", target_bir_lowering=False)

    def din(name, shape):
        return nc.dram_tensor(name, list(shape), F32, kind="ExternalInput").ap()

    def dout(name, shape):
        return nc.dram_tensor(name, list(shape), F32, kind="ExternalOutput").ap()

    xin = din("xin", [1536, 1024])
    cvec = din("cvec", [2, 1024])
    st_f = din("st_f", [2, 4, 64, 128])
    st_b = din("st_b", [2, 4, 64, 128])
    cswa_k = din("cswa_k", [2, 512, 128])
    cswa_v = din("cswa_v", [2, 512, 128])
    cm_ckv = din("cm_ckv", [2, 512, 256])
    cm_kr = din("cm_kr", [2, 512, 32])
    w_mod = din("w_mod", [4, 1024, 6144])
    b_mod = din("b_mod", [4, 6144])
    g_norm = din("g_norm", [4, 4, 1024])
    w_ff1 = din("w_ff1", [4, 1024, 4096])
    w_ff2 = din("w_ff2", [4, 4096, 1024])
    w_in_ab = din("w_in_ab", [2, 1024, 2336])
    w_gk_f = din("w_gk_f", [2, 16, 256])
    b_gk_f = din("b_gk_f", [2, 256])
    w_gk_b = din("w_gk_b", [2, 16, 256])
    b_gk_b = din("b_gk_b", [2, 256])
    g_gla = din("g_gla", [2, 128])
    swa_sink = din("swa_sink", [2, 8])
    w_out_ab = din("w_out_ab", [2, 1024, 1024])
    w_mla_down = din("w_mla_down", [2, 1024, 672])
    g_mla_q = din("g_mla_q", [2, 384])
    g_mla_kv = din("g_mla_kv", [2, 256])
    w_mla_uq = din("w_mla_uq", [2, 384, 1536])
    w_mla_ukv = din("w_mla_ukv", [2, 256, 2048])
    w_mla_o = din("w_mla_o", [2, 1024, 1024])
    consts = din("consts", [10, 128, 128])
    cs64_d = din("cs64", [2, 128, 1024])
    cs32_d = din("cs32", [2, 96, 1024])
    pm32_d = din("pm32", [96, 96])

    y_out = dout("y", [1536, 1024])
    o_stf = dout("o_stf", [2, 2, 4, 64, 128])
    o_stb = dout("o_stb", [2, 2, 4, 64, 128])
    o_swk = dout("o_swk", [2, 2, 256, 128])
    o_swv = dout("o_swv", [2, 2, 256, 128])
    o_ckv = dout("o_ckv", [2, 2, 256, 256])
    o_kr = dout("o_kr", [2, 2, 256, 32])

    fw = FW(nc)
    d_in = None
    d_out = None
    uniq = {"i": 0}

    open_phases = []
    phase_stacks = []
    import os as _os
    DBG = int(_os.environ.get("KDBG", "0"))

    class Bail(Exception):
        pass

    def CK(k):
        if DBG == k:
            raise Bail()

    class Phase:
        def __init__(self):
            self.es = ExitStack()
            open_phases.append(self)

        def tile(self, shape, dt, name):
            uniq["i"] += 1
            t = fw.tile(shape, dt, "%s_%d" % (name, uniq["i"]), stack=self.es)
            t.scoped = True
            return t

        def close(self):
            fw.barrier(fw.spool)
            self.es.close()
            open_phases.remove(self)

    PB = [fw.ptile([128, 512], F32, "pb%d" % i) for i in range(8)]
    rot = {"i": 0, "a": 0, "wide": True}

    def PG():
        t = PB[rot["i"] % (8 if rot["wide"] else 4)]
        rot["i"] += 1
        return t

    def ACCPAIR():
        k = rot["a"] % 2
        rot["a"] += 1
        return PB[4 + 2 * k], PB[5 + 2 * k]

    def MM(out, lhsT, rhs, start, stop, w, r):
        fw.op("pe", lambda e: e.matmul(out, lhsT=lhsT, rhs=rhs, start=start, stop=stop), w=w, r=r, signal=stop)

    def TR(out, in_, idn, w, r):
        fw.op("pe", lambda e: e.transpose(out, in_, idn), w=w, r=r)

    def ACT(out, in_, func, w, r, scale=1.0, bias=0.0):
        fw.op("act", lambda e: e.activation(out=out, in_=in_, func=func, bias=bias, scale=scale), w=w, r=r)

    def TT(out, in0, in1, op, w, r, en="dve"):
        fw.op(en, lambda e: e.tensor_tensor(out=out, in0=in0, in1=in1, op=op), w=w, r=r)

    def STT(out, in0, scalar, in1, op0, op1, w, r, en="dve"):
        fw.op(en, lambda e: e.scalar_tensor_tensor(out=out, in0=in0, scalar=scalar, in1=in1, op0=op0, op1=op1), w=w, r=r)

    def TS(out, in0, s1, op0, w, r, en="dve"):
        fw.op(en, lambda e: e.tensor_scalar(out=out, in0=in0, scalar1=s1, scalar2=None, op0=op0), w=w, r=r)

    def CP(out, in_, w, r, en="dve"):
        fw.op(en, lambda e: e.tensor_copy(out=out, in_=in_), w=w, r=r)

    def RCP(out, in_, w, r):
        fw.op("dve", lambda e: e.reciprocal(out=out, in_=in_), w=w, r=r)

    def MSET(ap, val, w, en="dve"):
        fw.op(en, lambda e: e.memset(ap, val), w=w, r=())

    def SPLIT(src, hi, lo, hiT, loT_, srcT):
        CP(hi, src, [hiT], [srcT])
        TT(lo, src, hi, ALU.subtract, [loT_], [srcT, hiT])

    def bc(ap, shape, axis):
        return ap.unsqueeze(axis).to_broadcast(list(shape))

    evac_rr = {"i": 0}
    fin = {"f": None}

    def defer(fn):
        run_deferred()
        fin["f"] = fn

    def run_deferred():
        if fin["f"] is not None:
            f_ = fin["f"]
            fin["f"] = None
            f_()

    def EVAC(out, in_, w, r):
        bi = PB.index(r[0]) if r and r[0] in PB else evac_rr["i"]
        evac_rr["i"] += 1
        if bi % 2:
            ACT(out, in_, AF.Copy, w, r)
        else:
            CP(out, in_, w, r)

    xT = fw.tile([128, 8, 1536], F32, "xT")
    xTb = [T(xT.ap[:, :, b * 512:(b + 1) * 512], "xTb%d" % b) for b in range(3)]
    cst = fw.tile([128, 10, 128], F32, "cst")
    ident = cst.ap[:, 0, :]
    ones_bf = fw.tile([128, 3, 128], BF16, "ones_bf")
    ones_row = fw.tile([1, 128], F32, "ones_row")
    pm32 = fw.tile([96, 96], F32, "pm32")
    cstb = fw.tile([128, 10, 128], BF16, "cstb")
    pm32b = fw.tile([96, 96], BF16, "pm32b")
    ones_rb = fw.tile([1, 128], BF16, "ones_rb")
    wgkh = fw.tile([49, 2, 256], BF16, "wgkh")
    wgkl = fw.tile([49, 2, 256], BF16, "wgkl")
    sinkh = fw.tile([1, 16], BF16, "sinkh")
    sinkl = fw.tile([1, 16], BF16, "sinkl")
    gnT = fw.tile([128, 128], F32, "gnT")
    bmT = fw.tile([128, 192], F32, "bmT")
    smallT = fw.tile([128, 28], F32, "smallT")
    scT = fw.tile([128, 16], BF16, "scT")
    wgk = fw.tile([49, 2, 256], F32, "wgk")
    sink_row = fw.tile([1, 16], F32, "sink_row")
    esink = fw.tile([128, 16], F32, "esink")
    modv = [fw.tile([128, 2, 6, 8], F32, "modv%d" % i) for i in range(2)]
    mraw = fw.tile([128, 2, 48], F32, "mraw")
    rstd_t = [fw.tile([128, 512], F32, "rstd%d" % i) for i in range(2)]
    utmp = [fw.tile([128, 512], F32, "utmp%d" % i) for i in range(3)]
    rr = {"u": 0, "r": 0}

    def UT():
        rr["u"] += 1
        return utmp[rr["u"] % 3]

    NWB = 6
    WBE = 2048
    wbufs = [(fw.tile([128, WBE], BF16, "wb%d" % i), fw.dsem()) for i in range(NWB)]
    wrot = {"i": 0}
    for t_, _ in wbufs:
        MSET(t_[:], 0.0, [t_], en="pool")

    def wload(parts):
        t, ds = wbufs[wrot["i"] % NWB]
        wrot["i"] += 1
        for dstf, src in parts:
            fw.dma("pool", ds, dstf(t.ap), src, w=[t])
        return t, t.ap

    def v3(ap, P, a, b):
        assert a * b <= WBE
        return ap[0:P, 0:a * b].rearrange("p (a b) -> p a b", a=a)

    def wl3(src, P, a, b):
        t, ap = wload([(lambda q: v3(q, P, a, b), src)])
        return t, v3(ap, P, a, b)

    SU = Phase()
    stage = [SU.tile([128, 1024], F32, "stage") for _ in range(2)]
    fw.dma("sp", d_in, cst[:], consts.rearrange("k p n -> p k n"), w=[cst])
    fw.dma("sp", d_in, pm32[:], pm32_d, w=[pm32])
    MSET(ones_bf[:, 0, :], 1.0 / 1024.0, [ones_bf])
    MSET(ones_bf[:, 1, :], 1.0 / 128.0, [ones_bf])
    MSET(ones_bf[:, 2, :], 1.0, [ones_bf])
    MSET(ones_row[:], 1.0, [ones_row])
    MSET(ones_rb[:], 1.0, [ones_rb])
    CP(cstb[:], cst[:], [cstb], [cst])
    CP(pm32b[:], pm32[:], [pm32b], [pm32])
    st0 = stage[0]
    fw.dma("sp", d_in, st0[0:16, 0:128], cvec.rearrange("j (c p) -> (j c) p", p=128), w=[st0])
    fw.dma("sp", d_in, st0[16:18, 0:128], g_gla, w=[st0])
    fw.dma("sp", d_in, st0[18:24, 0:128], g_mla_q.rearrange("i (c p) -> (i c) p", p=128), w=[st0])
    fw.dma("sp", d_in, st0[24:28, 0:128], g_mla_kv.rearrange("i (c p) -> (i c) p", p=128), w=[st0])
    p = PG()
    TR(p[:, 0:28], st0[0:28, 0:128], ident[0:28, 0:28], [p], [st0, cst])
    CP(smallT[:], p[:, 0:28], [smallT], [p])
    ACT(scT[:], smallT[:, 0:16], AF.Silu, [scT], [smallT])
    st1 = stage[1]
    fw.dma("sp", d_in, st1[:, 0:128], g_norm.rearrange("l i (c p) -> (l i c) p", p=128), w=[st1])
    p = PG()
    TR(p[:, 0:128], st1[:, 0:128], ident, [p], [st1, cst])
    CP(gnT[:], p[:, 0:128], [gnT], [p])
    bmv = b_mod.rearrange("l (n p) -> (l n) p", p=128)
    fw.dma("sp", d_in, st0[:, 128:256], bmv[0:128, :], w=[st0])
    fw.dma("sp", d_in, st1[0:64, 128:256], bmv[128:192, :], w=[st1])
    p = PG()
    TR(p[:, 0:128], st0[:, 128:256], ident, [p], [st0, cst])
    TR(p[:, 128:192], st1[0:64, 128:256], ident[0:64, 0:64], [p], [st1, cst])
    CP(bmT[:], p[:, 0:192], [bmT], [p])
    fw.dma("sp", d_in, wgk[0:16, :, :], w_gk_f.rearrange("i k n -> k i n"), w=[wgk])
    fw.dma("sp", d_in, wgk[32:48, :, :], w_gk_b.rearrange("i k n -> k i n"), w=[wgk])
    fw.dma("sp", d_in, wgk[16:17, :, :], b_gk_f.rearrange("(o i) n -> o i n", o=1), w=[wgk])
    fw.dma("sp", d_in, wgk[48:49, :, :], b_gk_b.rearrange("(o i) n -> o i n", o=1), w=[wgk])
    fw.dma("sp", d_in, sink_row[:], swa_sink.rearrange("(o i) h -> o (i h)", o=1), w=[sink_row])
    SPLIT(sink_row[:], sinkh[:], sinkl[:], sinkh, sinkl, sink_row)
    for q0_, q1_ in ((0, 17), (32, 49)):
        SPLIT(wgk[q0_:q1_, :, :], wgkh[q0_:q1_, :, :], wgkl[q0_:q1_, :, :], wgkh, wgkl, wgk)
    p = PG()
    MM(p[:, 0:16], ones_rb[0:1, :], sinkh[0:1, :], True, False, [p], [ones_rb, sinkh])
    MM(p[:, 0:16], ones_rb[0:1, :], sinkl[0:1, :], False, True, [p], [ones_rb, sinkl])
    ACT(esink[:], p[:, 0:16], AF.Exp, [esink], [p])

    for t in range(12):
        stg = stage[t % 2]
        fw.dma("sp", d_in, stg[:], xin[t * 128:(t + 1) * 128, :], w=[stg])
        for half in range(2):
            p = PG()
            for cc in range(4):
                c = half * 4 + cc
                TR(p[:, cc * 128:(cc + 1) * 128], stg[:, c * 128:(c + 1) * 128], ident, [p], [stg, cst])
            EVAC(xT.ap[:, half * 4:half * 4 + 4, t * 128:(t + 1) * 128], p.ap[:, :].rearrange("p (c n) -> p c n", c=4),
                 [xTb[t // 4]], [p])
    SU.close()

    def modulation(l):
        mv = modv[l % 2]
        pm = PG()
        wv = w_mod[l].rearrange("(k p) n -> p k n", p=128)
        scv = scT.ap[:, :].rearrange("p (j k) -> p j k", j=2)
        for pc in range(24):
            wt, wvw = wl3(wv[:, :, pc * 256:(pc + 1) * 256], 128, 8, 256)
            for nn in range(2):
                n = pc * 2 + nn
                for k in range(8):
                    MM(pm[:, n * 2:(n + 1) * 2], wvw[:, k, nn * 128:(nn + 1) * 128], scv[:, :, k], k == 0, k == 7, [pm], [wt, scT])
        pmv = pm.ap[:, 0:96].rearrange("p (n j) -> p n j", j=2)
        for j in range(2):
            TT(mraw[:, j, :], pmv[:, :, j], bmT[:, l * 48:(l + 1) * 48], ALU.add, [mraw], [pm, bmT])
        g = lambda q: gnT[:, l * 32 + q * 8:l * 32 + q * 8 + 8]
        for j in range(2):
            STT(mv[:, j, 0, :], mraw[:, j, 8:16], 1.0, g(0), ALU.add, ALU.mult, [mv], [mraw, gnT])
            CP(mv[:, j, 1, :], mraw[:, j, 0:8], [mv], [mraw])
            TT(mv[:, j, 2, :], mraw[:, j, 16:24], g(1), ALU.mult, [mv], [mraw, gnT])
            STT(mv[:, j, 3, :], mraw[:, j, 32:40], 1.0, g(2), ALU.add, ALU.mult, [mv], [mraw, gnT])
            CP(mv[:, j, 4, :], mraw[:, j, 24:32], [mv], [mraw])
            TT(mv[:, j, 5, :], mraw[:, j, 40:48], g(3), ALU.mult, [mv], [mraw, gnT])
        return mv

    def rstd_of(sq_aps, sq_T, ones_ap, N):
        ss = PG()
        n = len(sq_aps)
        for c, a in enumerate(sq_aps):
            MM(ss[:, 0:N], ones_ap, a, c == 0, c == n - 1, [ss], [ones_bf] + sq_T)
        rr["r"] += 1
        rs = rstd_t[rr["r"] % 2]
        ACT(rs[:, 0:N], ss[:, 0:N], AF.Ln, [rs], [ss], bias=EPS)
        ACT(rs[:, 0:N], rs[:, 0:N], AF.Exp, [rs], [rs], scale=-0.5)
        return rs

    def pre(b, mv, which, dst, sq):
        j = 0 if b == 0 else 1
        A = mv.ap[:, j, 3 * which + 0, :]
        Sh = mv.ap[:, j, 3 * which + 1, :]
        xb = xTb[b]
        ACT(sq[:], xb[:], AF.Square, [sq], [xb])
        rs = rstd_of([sq[:, c, :] for c in range(8)], [sq], ones_bf[:, 0, :], 512)
        for c in range(8):
            u = UT()
            TT(u[:], xb[:, c, :], rs[:, :], ALU.mult, [u], [xb, rs])
            ACT(dst[:, c, :], u[:], AF.Identity, [dst], [u, mv], scale=A[:, c:c + 1], bias=Sh[:, c:c + 1])

    def post_evac(n, p, osb, sq):
        ACT(osb[:, n, :], p[:, :], AF.Copy, [osb], [p])
        TT(sq[:, n, :], osb[:, n, :], osb[:, n, :], ALU.mult, [sq], [osb])

    def post_finish(b, mv, which, osb, sq):
        j = 0 if b == 0 else 1
        G = mv.ap[:, j, 3 * which + 2, :]
        xb = xTb[b]
        rs = rstd_of([sq[:, c, :] for c in range(8)], [sq], ones_bf[:, 0, :], 512)
        for c in range(8):
            u = UT()
            STT(u[:], osb[:, c, :], G[:, c:c + 1], rs[:, :], ALU.mult, ALU.mult, [u], [osb, rs, mv])
            TT(xb[:, c, :], xb[:, c, :], u[:], ALU.add, [xb], [xb, u])

    def mlp(l, b, mv, hT, hid, osb, sq, next_pre=None):
        w1v = w_ff1[l].rearrange("(k p) n -> p k n", p=128)
        for jp in range(16):
            wt, wv = wl3(w1v[:, :, jp * 256:(jp + 1) * 256], 128, 8, 256)
            for jj in range(2):
                jn = jp * 2 + jj
                ph_ = PG()
                for k in range(8):
                    MM(ph_[:, :], wv[:, k, jj * 128:(jj + 1) * 128], hT[:, k, :], k == 0, k == 7, [ph_], [wt, hT])
                ACT(hid[:, jn, :], ph_[:, :], AF.Relu, [hid], [ph_])
                TT(hid[:, jn, :], hid[:, jn, :], hid[:, jn, :], ALU.mult, [hid], [hid])
        if next_pre is not None:
            next_pre()
        w2v = w_ff2[l].rearrange("(j p) n -> p j n", p=128)
        for n in range(8):
            po = PG()
            for jh in range(2):
                wt, wv = wl3(w2v[:, jh * 16:(jh + 1) * 16, n * 128:(n + 1) * 128], 128, 16, 128)
                for jj in range(16):
                    jn = jh * 16 + jj
                    MM(po[:, :], wv[:, jj, :], hid[:, jn, :], jn == 0, jn == 31, [po], [wt, hid])
            post_evac(n, po, osb, sq)
        post_finish(b, mv, 1, osb, sq)

    def even_mixer(l, i, mv, G):
        kind, nt, blocks = G["kind"], G["nt"], G["blocks"]
        ntok = nt * 128
        nb = len(blocks)
        tg0 = G["t0"]
        E = Phase()
        hT = [E.tile([128, 8, 512], BF16, "hT") for _ in range(nb)]
        sq = E.tile([128, 8, 512], BF16, "sq")
        loTh = E.tile([49, ntok], BF16, "loTh")
        loTl = E.tile([49, ntok], BF16, "loTl")
        lotmp = E.tile([48, 512], F32, "lotmp")
        for q0_ in (0, 32):
            MSET(loTh[q0_:q0_ + 17, :], 1.0, [loTh])
            MSET(loTl[q0_:q0_ + 17, :], 0.0, [loTl])
        mixT = E.tile([128, 4, ntok], BF16, "mixT")
        winv = w_in_ab[i].rearrange("(k p) n -> p k n", p=128)
        for bl, b in enumerate(blocks):
            pre(b, mv, 0, hT[bl], sq)
        wt, wa = wload([(lambda q: v3(q, 128, 8, 48)[:, :, 0:16], winv[:, :, 1536:1552]),
                        (lambda q: v3(q, 128, 8, 48)[:, :, 32:48], winv[:, :, 1552:1568])])
        wv = v3(wa, 128, 8, 48)
        for bl in range(nb):
            p = PG()
            for k in range(8):
                MM(p[0:48, :], wv[:, k, :], hT[bl][:, k, :], k == 0, k == 7, [p], [wt, hT[bl]])
            ACT(lotmp[0:48, :], p[0:48, :], AF.Copy, [lotmp], [p])
            for q0_ in (0, 32):
                SPLIT(lotmp[q0_:q0_ + 16, :], loTh[q0_:q0_ + 16, bl * 512:(bl + 1) * 512],
                      loTl[q0_:q0_ + 16, bl * 512:(bl + 1) * 512], loTh, loTl, lotmp)

        CK(1)
        S = Phase()
        qaT = S.tile([128, ntok], BF16, "qaT")
        kaT = S.tile([128, ntok], BF16, "kaT")
        sgT = S.tile([128, 2, ntok], BF16, "sgT")
        katok = S.tile([128, nt, 128], BF16, "katok")
        vatok = S.tile([128, nt, 256], BF16, "vatok")
        qeS = S.tile([128, nt, 2, 128], BF16, "qeS")
        attS = S.tile([128, nt, 2, 2, 128], BF16, "attS")
        Ust = S.tile([128, nt, 2, 2, 128], F32, "Ust")
        SbfF = S.tile([128, nt, 2, 128], BF16, "SbfF")
        SbfB = S.tile([128, nt, 2, 128], BF16, "SbfB")
        dsc = S.tile([128, nt, 2, 2], F32, "dsc")
        Sf = S.tile([128, 128], F32, "Sf")
        Sb = S.tile([128, 128], F32, "Sb")
        etmp = [S.tile([128, 256], F32, "etmp") for _ in range(2)]
        g2 = [S.tile([128, 2, 128], F32, "g2") for _ in range(2)]
        ggh = [S.tile([128, 2, 128], BF16, "ggh") for _ in range(2)]
        ggl = [S.tile([128, 2, 128], BF16, "ggl") for _ in range(2)]
        Epos = [S.tile([128, 2, 128], F32, "Epos") for _ in range(2)]
        Eneg = [S.tile([128, 2, 128], F32, "Eneg") for _ in range(2)]
        ED = [S.tile([128, 2, 128], F32, "ED") for _ in range(2)]
        ke = [S.tile([128, 2, 128], BF16, "ke") for _ in range(2)]
        kd = [S.tile([128, 2, 128], BF16, "kd") for _ in range(2)]
        osb2 = [S.tile([128, 256], F32, "osb2") for _ in range(2)]
        sq2 = [S.tile([128, 256], BF16, "sq2") for _ in range(2)]
        for ch in range(2):
            wt, wa = wload([(lambda q: v3(q, 128, 8, 256)[:, :, 0:128], winv[:, :, ch * 128:(ch + 1) * 128]),
                            (lambda q: v3(q, 128, 8, 256)[:, :, 128:256], winv[:, :, 256 + ch * 128:256 + (ch + 1) * 128])])
            wv = v3(wa, 128, 8, 256)
            for bl in range(nb):
                p = PG()
                for k in range(8):
                    MM(p[:, :], wv[:, k, 0:128], hT[bl][:, k, :], k == 0, k == 7, [p], [wt, hT[bl]])
                ACT(qaT[:, bl * 512:(bl + 1) * 512], p[:, :], AF.Copy, [qaT], [p], scale=0.125)
                p = PG()
                for k in range(8):
                    MM(p[:, :], wv[:, k, 128:256], hT[bl][:, k, :], k == 0, k == 7, [p], [wt, hT[bl]])
                CP(kaT[:, bl * 512:(bl + 1) * 512], p[:, :], [kaT], [p])
            CK(11)
            wt, wv = wl3(winv[:, :, 1024 + ch * 256:1024 + (ch + 1) * 256], 128, 8, 256)
            for hl in range(2):
                for bl in range(nb):
                    p = PG()
                    for k in range(8):
                        MM(p[:, :], wv[:, k, hl * 128:(hl + 1) * 128], hT[bl][:, k, :], k == 0, k == 7, [p], [wt, hT[bl]])
                    ACT(sgT[:, hl, bl * 512:(bl + 1) * 512], p[:, :], AF.Silu, [sgT], [p])
            CK(12)
            wt1, wv1 = wl3(winv[:, :, 256 + ch * 128:256 + (ch + 1) * 128], 128, 8, 128)
            wt2, wv2 = wl3(winv[:, :, 512 + ch * 256:512 + (ch + 1) * 256], 128, 8, 256)
            CK(13)
            for tl in range(nt):
                if tl == 1:
                    CK(14)
                bl, tt = tl // 4, tl % 4
                p = PG()
                for k in range(8):
                    MM(p[:, 0:128], hT[bl][:, k, tt * 128:(tt + 1) * 128], wv1[:, k, :], k == 0, k == 7, [p], [wt1, hT[bl]])
                CP(katok[:, tl, :], p[:, 0:128], [katok], [p])
                p = PG()
                for k in range(8):
                    MM(p[:, 0:256], hT[bl][:, k, tt * 128:(tt + 1) * 128], wv2[:, k, :], k == 0, k == 7, [p], [wt2, hT[bl]])
                ACT(vatok[:, tl, :], p[:, 0:256], AF.Copy, [vatok], [p])

            CK(2)
            for si, (s0, sn) in enumerate(G["seqs"]):
                tiles = list(range(s0, s0 + sn))
                if kind == "s":
                    fw.dma("sp", d_in, Sf[:], st_f[i, 2 * ch:2 * ch + 2].rearrange("h k v -> (h k) v"), w=[Sf])
                    fw.dma("sp", d_in, Sb[:], st_b[i, 2 * ch:2 * ch + 2].rearrange("h k v -> (h k) v"), w=[Sb])
                else:
                    MSET(Sf[:], 0.0, [Sf])
                    MSET(Sb[:], 0.0, [Sb])
                def stA1(tl):
                    tsl = slice(tl * 128, (tl + 1) * 128)
                    pb_ = tl % 2
                    gg = g2[pb_]
                    for d in range(2):
                        base = 0 if d == 0 else 32
                        bsl = slice(base, base + 17)
                        csl = slice(ch * 128, (ch + 1) * 128)
                        pz = PG()
                        zr = pz[:, 0:128]
                        MM(zr, loTh[bsl, tsl], wgkh[bsl, i, csl], True, False, [pz], [loTh, wgkh])
                        MM(zr, loTh[bsl, tsl], wgkl[bsl, i, csl], False, False, [pz], [loTh, wgkl])
                        MM(zr, loTl[bsl, tsl], wgkh[bsl, i, csl], False, True, [pz], [loTl, wgkh])
                        ACT(etmp[pb_][:, d * 128:(d + 1) * 128], pz[:, 0:128], AF.Exp, [etmp[pb_]], [pz], scale=-1.0)
                    ACT(gg.ap[:, :, :].rearrange("p a b -> p (a b)"), etmp[pb_][:], AF.Ln, [gg], [etmp[pb_]], bias=1.0)
                    SPLIT(gg[:, :, :], ggh[pb_][:, :, :], ggl[pb_][:, :, :], ggh[pb_], ggl[pb_], gg)

                def stA2(tl):
                    pb_ = tl % 2
                    for d in range(2):
                        pc = PG()
                        MM(pc[:, 0:128], ggh[pb_][:, d, :], cstb[:, 1 + d, :], True, False, [pc], [ggh[pb_], cstb])
                        MM(pc[:, 0:128], ggl[pb_][:, d, :], cstb[:, 1 + d, :], False, True, [pc], [ggl[pb_], cstb])
                        ACT(Epos[pb_][:, d, :], pc[:, 0:128], AF.Exp, [Epos[pb_]], [pc])
                        ACT(Eneg[pb_][:, d, :], pc[:, 0:128], AF.Exp, [Eneg[pb_]], [pc], scale=-1.0)
                    for d in range(2):
                        pc = PG()
                        MM(pc[:, 0:128], cstb[:, 3 + d, :], ggh[pb_][:, d, :], True, False, [pc], [ggh[pb_], cstb])
                        MM(pc[:, 0:128], cstb[:, 3 + d, :], ggl[pb_][:, d, :], False, True, [pc], [ggl[pb_], cstb])
                        ACT(ED[pb_][:, d, :], pc[:, 0:128], AF.Exp, [ED[pb_]], [pc])

                def stA3(tl):
                    tsl = slice(tl * 128, (tl + 1) * 128)
                    pb_ = tl % 2
                    TT(qeS[:, tl, :, :], bc(qaT[:, tsl], [128, 2, 128], 1), Epos[pb_][:], ALU.mult, [qeS], [qaT, Epos[pb_]])
                    TT(ke[pb_][:], bc(kaT[:, tsl], [128, 2, 128], 1), Eneg[pb_][:], ALU.mult, [ke[pb_]], [kaT, Eneg[pb_]])
                    TT(kd[pb_][:], bc(katok[:, tl, :], [128, 2, 128], 1), ED[pb_][:], ALU.mult, [kd[pb_]], [katok, ED[pb_]])
                    CP(dsc[:, tl, 0, :], Epos[pb_][:, 0, 63:128:64], [dsc], [Epos[pb_]])
                    CP(dsc[:, tl, 1, :], Epos[pb_][:, 1, 0:128:64], [dsc], [Epos[pb_]])

                def stA4(tl):
                    pb_ = tl % 2
                    for hl in range(2):
                        for d in range(2):
                            pa = PG()
                            MM(pa[:, 0:128], ke[pb_][hl * 64:(hl + 1) * 64, d, :],
                               qeS[hl * 64:(hl + 1) * 64, tl, d, :], True, True, [pa], [ke[pb_], qeS])
                            TT(attS[:, tl, hl, d, :], pa[:, 0:128], cst[:, 5 + d, :], ALU.mult, [attS], [pa, cst])

                def stA5(tl):
                    pb_ = tl % 2
                    for half in range(2):
                        for d in range(2):
                            pu = PG()
                            MM(pu[:, 0:256], kd[pb_][half * 64:(half + 1) * 64, d, :],
                               vatok[half * 64:(half + 1) * 64, tl, :], True, True, [pu], [kd[pb_], vatok])
                            EVAC(Ust[0:64, tl, d, half, :], pu[0:64, 0:128], [Ust], [pu])
                            EVAC(Ust[64:128, tl, d, half, :], pu[64:128, 128:256], [Ust], [pu])

                for t0_ in range(0, len(tiles), 2):
                    pair = tiles[t0_:t0_ + 2]
                    for st_ in (stA1, stA2, stA3, stA4, stA5):
                        for tl in pair:
                            st_(tl)
                CK(3)
                for tl in tiles:
                    for half in (0, 1):
                        CP(SbfF[:, tl, half, :], Sf[:], [SbfF], [Sf])
                        STT(Sf[:], Sf[:], dsc[:, tl, 0, half:half + 1], Ust[:, tl, 0, half, :], ALU.mult, ALU.add, [Sf], [Sf, dsc, Ust])
                for tl in reversed(tiles):
                    for half in (1, 0):
                        CP(SbfB[:, tl, half, :], Sb[:], [SbfB], [Sb])
                        STT(Sb[:], Sb[:], dsc[:, tl, 1, half:half + 1], Ust[:, tl, 1, half, :], ALU.mult, ALU.add, [Sb], [Sb, dsc, Ust])
                if kind == "p":
                    fw.dma("sp", d_out, o_stf[si, i, 2 * ch:2 * ch + 2].rearrange("h k v -> (h k) v"), Sf[:], r=[Sf])
                    fw.dma("sp", d_out, o_stb[si, i, 2 * ch:2 * ch + 2].rearrange("h k v -> (h k) v"), Sb[:], r=[Sb])
                CK(4)
                def stB1(tl):
                    pb_ = tl % 2
                    for hl in range(2):
                        hs = slice(hl * 64, (hl + 1) * 64)
                        po = PG()
                        reg = po[:, 0:128]
                        MM(reg, vatok[:, tl, hl * 128:(hl + 1) * 128], attS[:, tl, hl, 0, :], True, False, [po], [vatok, attS])
                        MM(reg, vatok[:, tl, hl * 128:(hl + 1) * 128], attS[:, tl, hl, 1, :], False, False, [po], [vatok, attS])
                        for half in range(2):
                            MM(po[:, half * 64:half * 64 + 64], SbfF[hs, tl, half, :],
                               qeS[hs, tl, 0, half * 64:(half + 1) * 64], False, False, [po], [SbfF, qeS])
                        for half in range(2):
                            MM(po[:, half * 64:half * 64 + 64], SbfB[hs, tl, half, :],
                               qeS[hs, tl, 1, half * 64:(half + 1) * 64], False, half == 1, [po], [SbfB, qeS])
                        EVAC(osb2[pb_][:, hl * 128:(hl + 1) * 128], po[:, 0:128], [osb2[pb_]], [po])
                    TT(sq2[pb_][:], osb2[pb_][:], osb2[pb_][:], ALU.mult, [sq2[pb_]], [osb2[pb_]])

                def stB2(tl):
                    tsl = slice(tl * 128, (tl + 1) * 128)
                    pb_ = tl % 2
                    rs = rstd_of([sq2[pb_][:, :]], [sq2[pb_]], ones_bf[:, 1, :], 256)
                    STT(osb2[pb_][:], osb2[pb_][:], smallT[:, 16 + i:17 + i], rs[:, 0:256], ALU.mult, ALU.mult,
                        [osb2[pb_]], [osb2[pb_], rs, smallT])
                    TT(mixT[:, 2 * ch:2 * ch + 2, tsl], osb2[pb_].ap[:, :].rearrange("p (a b) -> p a b", a=2), sgT[:, :, tsl],
                       ALU.mult, [mixT], [osb2[pb_], sgT])

                for t0_ in range(0, len(tiles), 2):
                    pair = tiles[t0_:t0_ + 2]
                    for st_ in (stB1, stB2):
                        for tl in pair:
                            st_(tl)
        S.close()

        CK(5)
        oswaT = E.tile([64, 8, ntok], BF16, "oswaT")
        W = Phase()
        qbT = W.tile([128, 4, ntok], BF16, "qbT")
        kbT = W.tile([128, ntok], BF16, "kbT")
        vbtok = W.tile([128, nt, 128], BF16, "vbtok")
        Et = [W.tile([128, 512], BF16, "Et") for _ in range(4)]
        dn = W.tile([64, 512], F32, "dn")
        if kind == "s":
            rt = W.tile([128, 2, 1024], F32, "rt")
            fw.dma("sp", d_in, rt[:], cs64_d.rearrange("k p n -> p k n"), w=[rt])
            xf = W.tile([128, 512], F32, "xf")
            xfh = W.tile([128, 512], BF16, "xfh")
            xfl = W.tile([128, 512], BF16, "xfl")
            t1 = W.tile([128, 512], F32, "t1")
            kc_st = W.tile([128, 4, 128], F32, "kc_st")
            vc_st = W.tile([128, 4, 128], F32, "vc_st")
            kcT = W.tile([128, 512], BF16, "kcT")
            vc = W.tile([128, 4, 128], BF16, "vc")
        else:
            kvst = [W.tile([128, 256], F32, "kvst") for _ in range(2)]

        def rope64(p, bl, dst, dstT):
            ACT(xf[:], p[:, :], AF.Copy, [xf], [p])
            SPLIT(xf[:], xfh[:], xfl[:], xfh, xfl, xf)
            pp = PG()
            MM(pp[:, :], cstb[:, 9, :], xfh[:], True, False, [pp], [cstb, xfh])
            MM(pp[:, :], cstb[:, 9, :], xfl[:], False, True, [pp], [cstb, xfl])
            TT(t1[:], xf[:], rt[:, 0, bl * 512:(bl + 1) * 512], ALU.mult, [t1], [xf, rt])
            TT(xf[:], pp[:, :], rt[:, 1, bl * 512:(bl + 1) * 512], ALU.mult, [xf], [pp, rt])
            TT(dst, t1[:], xf[:], ALU.add, [dstT], [t1, xf])

        for ap_ in range(2):
            def v5(q):
                return q[0:128, 0:2048].rearrange("p (k a g d) -> p k a g d", k=8, a=2, g=2)
            wt, wa = wload([(lambda q, g=g, al=al: v5(q)[:, :, al, g, :],
                             winv[:, :, 1568 + g * 256 + (ap_ * 2 + al) * 64:1568 + g * 256 + (ap_ * 2 + al + 1) * 64])
                            for g in range(2) for al in range(2)])
            wv = v5(wa)
            for al in range(2):
                a = ap_ * 2 + al
                for bl in range(nb):
                    p = PG()
                    for k in range(8):
                        MM(p[:, :], wv[:, k, al, :, :].rearrange("p g d -> p (g d)"), hT[bl][:, k, :], k == 0, k == 7, [p], [wt, hT[bl]])
                    if kind == "p":
                        EVAC(qbT[:, a, bl * 512:(bl + 1) * 512], p[:, :], [qbT], [p])
                    else:
                        rope64(p, bl, qbT[:, a, bl * 512:(bl + 1) * 512], qbT)
        wt, wv = wl3(winv[:, :, 2080:2208], 128, 8, 128)
        for bl in range(nb):
            p = PG()
            for k in range(8):
                MM(p[:, :], wv[:, k, :], hT[bl][:, k, :], k == 0, k == 7, [p], [wt, hT[bl]])
            if kind == "p":
                EVAC(kbT[:, bl * 512:(bl + 1) * 512], p[:, :], [kbT], [p])
            else:
                rope64(p, bl, kbT[:, bl * 512:(bl + 1) * 512], kbT)
        wt, wv = wl3(winv[:, :, 2080:2336], 128, 8, 256)
        for tl in range(nt):
            bl, tt = tl // 4, tl % 4
            p = PG()
            for k in range(8):
                MM(p[:, 0:256], hT[bl][:, k, tt * 128:(tt + 1) * 128], wv[:, k, :], k == 0, k == 7, [p], [wt, hT[bl]])
            if kind == "p":
                ks = kvst[tl % 2]
                ACT(ks[:], p[:, 0:256], AF.Copy, [ks], [p])
                CP(vbtok[:, tl, :], ks[:, 128:256], [vbtok], [ks])
            else:
                CP(vbtok[:, tl, :], p[:, 128:256], [vbtok], [p])
            if kind == "p":
                si, r0 = tl // 2, (tl % 2) * 128
                fw.dma("sp", d_out, o_swk[si, i, r0:r0 + 128, :], ks[:, 0:128], r=[ks])
                fw.dma("sp", d_out, o_swv[si, i, r0:r0 + 128, :], ks[:, 128:256], r=[ks])
        if kind == "s":
            fw.dma("sp", d_in, kc_st[:], cswa_k[i].rearrange("(t p) f -> p t f", p=128), w=[kc_st])
            fw.dma("sp", d_in, vc_st[:], cswa_v[i].rearrange("(t p) f -> p t f", p=128), w=[vc_st])
            p = PG()
            for t in range(4):
                TR(p[:, t * 128:(t + 1) * 128], kc_st[:, t, :], ident, [p], [kc_st, cst])
            CP(kcT[:], p[:, :], [kcT], [p])
            CP(vc[:], vc_st[:], [vc], [vc_st])
        CK(6)
        rot["wide"] = False
        SC = 0.125
        eti = 0
        if kind == "p":
            for si, (s0, sn) in enumerate(G["seqs"]):
                q0 = s0 * 128
                for kv in range(2):
                    ks_ = slice(kv * 64, (kv + 1) * 64)
                    for apair in range(2):
                        po, pd = ACCPAIR()

                        def score_p(kt):
                            tl = s0 + kt
                            ps = PG()
                            MM(ps[:, :], kbT[ks_, tl * 128:(tl + 1) * 128], qbT[ks_, 2 * apair:2 * apair + 2, q0:q0 + 256],
                               True, True, [ps], [kbT, qbT])
                            return ps
                        ps_next = score_p(0)
                        for kt in range(sn):
                            tl = s0 + kt
                            ps = ps_next
                            if kt + 1 < sn:
                                ps_next = score_p(kt + 1)
                            et = Et[eti % 4]
                            eti += 1
                            ACT(et[:], ps[:, :], AF.Exp, [et], [ps], scale=SC)
                            MM(po[0:64, :], vbtok[:, tl, kv * 64:(kv + 1) * 64], et[:], kt == 0, kt == sn - 1, [po], [vbtok, et])
                            MM(pd[0:64, :], ones_bf[:, 2, 0:64], et[:], kt == 0, kt == sn - 1, [pd], [ones_bf, et])
                            if kt == min(1, sn - 1):
                                run_deferred()

                        def fin_p(po=po, pd=pd, kv=kv, apair=apair, q0=q0):
                            for al in range(2):
                                col = i * 8 + kv * 4 + apair * 2 + al
                                TS(dn[0:64, al * 256:(al + 1) * 256], pd[0:64, al * 256:(al + 1) * 256], esink[0:64, col:col + 1],
                                   ALU.add, [dn], [pd, esink])
                            ACT(dn[:], dn[:], AF.Ln, [dn], [dn])
                            ACT(dn[:], dn[:], AF.Exp, [dn], [dn], scale=-1.0)
                            h0 = kv * 4 + apair * 2
                            TT(oswaT[0:64, h0:h0 + 2, q0:q0 + 256], po.ap[0:64, :].rearrange("p (a b) -> p a b", a=2),
                               dn.ap[0:64, :].rearrange("p (a b) -> p a b", a=2), ALU.mult, [oswaT], [po, dn])
                        defer(fin_p)
        else:
            for qt in range(nt):
                for kv in range(2):
                    ks_ = slice(kv * 64, (kv + 1) * 64)
                    keys = [("c", t) for t in range(4)]
                    if qt > 0:
                        keys.append(("l", qt - 1))
                    keys.append(("l", qt))
                    if qt < nt - 1:
                        keys.append(("l", qt + 1))
                    po, pd = ACCPAIR()

                    def score_s(ki):
                        kk, kt = keys[ki]
                        ps = PG()
                        if kk == "c":
                            MM(ps[:, :], kcT[ks_, kt * 128:(kt + 1) * 128], qbT[ks_, :, qt * 128:(qt + 1) * 128], True, True, [ps], [kcT, qbT])
                        else:
                            MM(ps[:, :], kbT[ks_, kt * 128:(kt + 1) * 128], qbT[ks_, :, qt * 128:(qt + 1) * 128], True, True, [ps], [kbT, qbT])
                        return ps
                    psq = [score_s(0), score_s(1)]
                    for ki, (kk, kt) in enumerate(keys):
                        ps = psq.pop(0)
                        if ki + 2 < len(keys):
                            psq.append(score_s(ki + 2))
                        et = Et[eti % 4]
                        eti += 1
                        ACT(et[:], ps[:, :], AF.Exp, [et], [ps], scale=SC)
                        if kk == "l" and kt != qt:
                            mk = cst[:, 8, :] if kt < qt else cst[:, 7, :]
                            etv = et.ap[:, :].rearrange("p (a b) -> p a b", a=4)
                            TT(etv, etv, bc(mk, [128, 4, 128], 1), ALU.mult, [et], [et, cst])
                        vsrc, vT = (vc[:, kt, kv * 64:(kv + 1) * 64], vc) if kk == "c" else (vbtok[:, kt, kv * 64:(kv + 1) * 64], vbtok)
                        last = ki == len(keys) - 1
                        MM(po[0:64, :], vsrc, et[:], ki == 0, last, [po], [vT, et])
                        MM(pd[0:64, :], ones_bf[:, 2, 0:64], et[:], ki == 0, last, [pd], [ones_bf, et])
                        if ki == 2:
                            run_deferred()

                    def fin_s(po=po, pd=pd, kv=kv, qt=qt):
                        c0 = i * 8 + kv * 4
                        dnv = dn.ap[0:64, :].rearrange("p (a b) -> p a b", a=4)
                        TT(dnv, pd.ap[0:64, :].rearrange("p (a b) -> p a b", a=4), bc(esink[0:64, c0:c0 + 4], [64, 4, 128], 2),
                           ALU.add, [dn], [pd, esink])
                        ACT(dn[:], dn[:], AF.Ln, [dn], [dn])
                        ACT(dn[:], dn[:], AF.Exp, [dn], [dn], scale=-1.0)
                        TT(oswaT[0:64, kv * 4:(kv + 1) * 4, qt * 128:(qt + 1) * 128],
                           po.ap[0:64, :].rearrange("p (a b) -> p a b", a=4), dnv, ALU.mult, [oswaT], [po, dn])
                    defer(fin_s)
        run_deferred()
        rot["wide"] = True
        CK(7)

        O = W
        osb = O.tile([128, 8, 512], F32, "osb")
        woa_v = w_out_ab[i][0:512, :].rearrange("(h p) n -> p h n", p=128)
        wob_v = w_out_ab[i][512:1024, :].rearrange("(h p) n -> p h n", p=64)
        for bl, b in enumerate(blocks):
            bs = slice(bl * 512, (bl + 1) * 512)
            for npc in range(4):
                wta, wva = wl3(woa_v[:, :, npc * 256:(npc + 1) * 256], 128, 4, 256)
                wtb, wvb = wl3(wob_v[:, :, npc * 256:(npc + 1) * 256], 64, 8, 256)
                for nn in range(2):
                    n = npc * 2 + nn
                    p = PG()
                    for h in range(4):
                        MM(p[:, :], wva[:, h, nn * 128:(nn + 1) * 128], mixT[:, h, bs], h == 0, False, [p], [wta, mixT])
                    for h in range(8):
                        MM(p[:, :], wvb[0:64, h, nn * 128:(nn + 1) * 128], oswaT[0:64, h, bs], False, h == 7, [p], [wtb, oswaT])
                    post_evac(n, p, osb, sq)
            post_finish(b, mv, 0, osb, sq)
        O.close()
        E.close()

    def odd_mixer(l, i, mv, G):
        kind, nt, blocks = G["kind"], G["nt"], G["blocks"]
        ntok = nt * 128
        nb = len(blocks)
        koff = 512 if kind == "s" else 0
        nkeys = ntok + koff
        E = Phase()
        cqn = E.tile([128, 3, ntok], BF16, "cqn")
        ckvT = E.tile([128, 2, nkeys], BF16, "ckvT")
        krT = E.tile([96, nkeys], BF16, "krT")
        oT = E.tile([64, 16, ntok], BF16, "oT")
        t1 = E.tile([96, 512], F32, "t1")
        kr32 = E.tile([96, 512], F32, "kr32")
        krh = E.tile([96, 512], BF16, "krh")
        krl = E.tile([96, 512], BF16, "krl")
        if kind == "s":
            rt32 = E.tile([96, 2, 1024], F32, "rt32")
            fw.dma("sp", d_in, rt32[:], cs32_d.rearrange("k p n -> p k n"), w=[rt32])
        wdv = w_mla_down[i].rearrange("(k p) n -> p k n", p=128)

        def rope32(p, bl, dst, dstT):
            ACT(kr32[64:96, :], p[64:96, :], AF.Copy, [kr32], [p])
            SPLIT(kr32[64:96, :], krh[64:96, :], krl[64:96, :], krh, krl, kr32)
            pp = PG()
            MM(pp[0:96, :], pm32b[64:96, 0:96], krh[64:96, :], True, False, [pp], [pm32b, krh])
            MM(pp[0:96, :], pm32b[64:96, 0:96], krl[64:96, :], False, True, [pp], [pm32b, krl])
            TT(t1[64:96, :], kr32[64:96, :], rt32[64:96, 0, bl * 512:(bl + 1) * 512], ALU.mult, [t1], [kr32, rt32])
            TT(kr32[64:96, :], pp[64:96, :], rt32[64:96, 1, bl * 512:(bl + 1) * 512], ALU.mult, [kr32], [pp, rt32])
            TT(dst, t1[64:96, :], kr32[64:96, :], ALU.add, [dstT], [t1, kr32])

        D = Phase()
        hT = [D.tile([128, 8, 512], BF16, "hT") for _ in range(nb)]
        sq = D.tile([128, 8, 512], BF16, "sq")
        for bl, b in enumerate(blocks):
            pre(b, mv, 0, hT[bl], sq)
        cq32 = D.tile([128, 3, 512], F32, "cq32")
        ckv32 = D.tile([128, 2, 512], F32, "ckv32")
        if kind == "p":
            ost = D.tile([128, 4, 256], F32, "ost")
            krst = D.tile([128, 4, 32], F32, "krst")
        else:
            ckv_st = D.tile([128, 4, 256], F32, "ckv_st")
            kr_st = D.tile([128, 4, 96], F32, "kr_st")
        for bl, b in enumerate(blocks):
            bs = slice(bl * 512, (bl + 1) * 512)
            for c in range(3):
                wt, wv = wl3(wdv[:, :, c * 128:(c + 1) * 128], 128, 8, 128)
                p = PG()
                for k in range(8):
                    MM(p[:, :], wv[:, k, :], hT[bl][:, k, :], k == 0, k == 7, [p], [wt, hT[bl]])
                ACT(cq32[:, c, :], p[:, :], AF.Copy, [cq32], [p])
                ACT(sq[:, c, :], p[:, :], AF.Square, [sq], [p], scale=float(1.0 / np.sqrt(3.0)))
            rs = rstd_of([sq[:, c, :] for c in range(3)], [sq], ones_bf[:, 1, :], 512)
            for c in range(3):
                STT(cqn[:, c, bs], cq32[:, c, :], smallT[:, 18 + i * 3 + c:19 + i * 3 + c], rs[:, :], ALU.mult, ALU.mult,
                    [cqn], [cq32, rs, smallT])
            for c in range(2):
                wt, wv = wl3(wdv[:, :, 384 + c * 128:384 + (c + 1) * 128], 128, 8, 128)
                p = PG()
                for k in range(8):
                    MM(p[:, :], wv[:, k, :], hT[bl][:, k, :], k == 0, k == 7, [p], [wt, hT[bl]])
                ACT(ckv32[:, c, :], p[:, :], AF.Copy, [ckv32], [p])
                ACT(sq[:, 4 + c, :], p[:, :], AF.Square, [sq], [p], scale=float(1.0 / np.sqrt(2.0)))
            rs = rstd_of([sq[:, 4 + c, :] for c in range(2)], [sq], ones_bf[:, 1, :], 512)
            for c in range(2):
                STT(ckv32[:, c, :], ckv32[:, c, :], smallT[:, 24 + i * 2 + c:25 + i * 2 + c], rs[:, :], ALU.mult, ALU.mult,
                    [ckv32], [ckv32, rs, smallT])
                CP(ckvT[:, c, koff + bl * 512:koff + (bl + 1) * 512], ckv32[:, c, :], [ckvT], [ckv32])
            if kind == "p":
                for c in range(2):
                    p = PG()
                    for tt in range(4):
                        TR(p[:, tt * 128:(tt + 1) * 128], ckv32[:, c, tt * 128:(tt + 1) * 128], ident, [p], [ckv32, cst])
                    EVAC(ost[:, :, c * 128:(c + 1) * 128], p.ap[:, :].rearrange("p (a b) -> p a b", a=4), [ost], [p])
                for tt in range(4):
                    si, r0 = tt // 2, (tt % 2) * 128
                    fw.dma("sp", d_out, o_ckv[si, i, r0:r0 + 128, :], ost[:, tt, :], r=[ost])
            wt, wa = wload([(lambda q: v3(q, 128, 8, 96)[:, :, 64:96], wdv[:, :, 640:672])])
            wv = v3(wa, 128, 8, 96)
            p = PG()
            for k in range(8):
                MM(p[0:96, :], wv[:, k, :], hT[bl][:, k, :], k == 0, k == 7, [p], [wt, hT[bl]])
            if kind == "p":
                ACT(kr32[64:96, :], p[64:96, :], AF.Copy, [kr32], [p])
                CP(krT[64:96, bs], kr32[64:96, :], [krT], [kr32])
                pp = PG()
                for tt in range(4):
                    TR(pp[:, tt * 32:(tt + 1) * 32], kr32[64:96, tt * 128:(tt + 1) * 128], ident[64:96, 64:96], [pp], [kr32, cst])
                CP(krst[:], pp.ap[:, 0:128].rearrange("p (a b) -> p a b", a=4), [krst], [pp])
                for tt in range(4):
                    si, r0 = tt // 2, (tt % 2) * 128
                    fw.dma("sp", d_out, o_kr[si, i, r0:r0 + 128, :], krst[:, tt, :], r=[krst])
            else:
                rope32(p, bl, krT[64:96, koff + bl * 512:koff + (bl + 1) * 512], krT)
        if kind == "s":
            fw.dma("sp", d_in, ckv_st[:], cm_ckv[i].rearrange("(t p) f -> p t f", p=128), w=[ckv_st])
            MSET(kr_st[:], 0.0, [kr_st])
            fw.dma("sp", d_in, kr_st[:, :, 64:96], cm_kr[i].rearrange("(t p) f -> p t f", p=128), w=[kr_st])
            for c in range(2):
                p = PG()
                for t in range(4):
                    TR(p[:, t * 128:(t + 1) * 128], ckv_st[:, t, c * 128:(c + 1) * 128], ident, [p], [ckv_st, cst])
                EVAC(ckvT[:, c, 0:512], p[:, :], [ckvT], [p])
            p = PG()
            for t in range(4):
                TR(p[0:96, t * 128:(t + 1) * 128], kr_st[:, t, :], ident, [p], [kr_st, cst])
            CP(krT[64:96, 0:512], p[64:96, :], [krT], [p])
        D.close()

        wuq_v = w_mla_uq[i].rearrange("(k p) n -> p k n", p=128)
        wukv_v = w_mla_ukv[i].rearrange("(k p) (h t d) -> p k h t d", p=128, t=2, d=64)
        SCL = float(96.0 ** -0.5)
        nkb = nkeys // 512
        nkt = nkeys // 128
        H = Phase()
        rot["wide"] = False
        QT = H.tile([96, 4, ntok], BF16, "QT")
        KT = H.tile([96, 4, nkeys], BF16, "KT")
        Vg = H.tile([128, nkt, 256], BF16, "Vg")
        Et = [H.tile([128, 512], BF16, "Et") for _ in range(4)]
        dn = H.tile([64, 512], F32, "dn")
        for hg in range(4):
            wt, wv = wl3(wuq_v[:, :, hg * 384:(hg + 1) * 384], 128, 3, 384)
            for hl in range(4):
                for bl in range(nb):
                    bs = slice(bl * 512, (bl + 1) * 512)
                    p = PG()
                    for k in range(3):
                        MM(p[0:96, :], wv[:, k, hl * 96:(hl + 1) * 96], cqn[:, k, bs], k == 0, k == 2, [p], [wt, cqn])
                    if kind == "p":
                        EVAC(QT[0:64, hl, bs], p[0:64, :], [QT], [p])
                        EVAC(QT[64:96, hl, bs], p[64:96, :], [QT], [p])
                    else:
                        ACT(QT[0:64, hl, bs], p[0:64, :], AF.Copy, [QT], [p])
                        rope32(p, bl, QT[64:96, hl, bs], QT)

            def vk(q):
                return q[0:128, 0:512].rearrange("p (k h d) -> p k h d", k=2, h=4)
            wtk, wak = wload([(lambda q, k=k: vk(q)[:, k, :, :], wukv_v[:, k, hg * 4:(hg + 1) * 4, 0, :]) for k in range(2)])
            wvk = vk(wak)
            wtv, wav = wload([(lambda q, k=k: vk(q)[:, k, :, :], wukv_v[:, k, hg * 4:(hg + 1) * 4, 1, :]) for k in range(2)])
            wvv = vk(wav)
            for hl in range(4):
                for kb in range(nkb):
                    p = PG()
                    for k in range(2):
                        MM(p[0:64, :], wvk[:, k, hl, :], ckvT[:, k, kb * 512:(kb + 1) * 512], k == 0, k == 1, [p], [wtk, ckvT])
                    EVAC(KT[0:64, hl, kb * 512:(kb + 1) * 512], p[0:64, :], [KT], [p])
                CP(KT[64:96, hl, :], krT[64:96, :], [KT], [krT])
            for kt in range(nkt):
                p = PG()
                for k in range(2):
                    MM(p[:, 0:256], ckvT[:, k, kt * 128:(kt + 1) * 128], wvv[:, k, :, :].rearrange("p h d -> p (h d)"),
                       k == 0, k == 1, [p], [wtv, ckvT])
                EVAC(Vg[:, kt, :], p[:, 0:256], [Vg], [p])
            eti = 0
            for hl in range(4):
                h = hg * 4 + hl
                if kind == "s":
                    jobs = [(qb * 512, 512, list(range(nkt))) for qb in range(nb)]
                else:
                    jobs = [(s0 * 128, sn * 128, list(range(s0, s0 + sn))) for (s0, sn) in G["seqs"]]
                for (q0, qn, kts) in jobs:
                    po, pd = ACCPAIR()

                    def score_m(ki):
                        kt = kts[ki]
                        ps = PG()
                        MM(ps[:, 0:qn], KT[0:96, hl, kt * 128:(kt + 1) * 128], QT[0:96, hl, q0:q0 + qn], True, True, [ps], [KT, QT])
                        return ps
                    psq = [score_m(0)]
                    if len(kts) > 1:
                        psq.append(score_m(1))
                    for ki, kt in enumerate(kts):
                        ps = psq.pop(0)
                        if ki + 2 < len(kts):
                            psq.append(score_m(ki + 2))
                        et = Et[eti % 4]
                        eti += 1
                        ACT(et[:, 0:qn], ps[:, 0:qn], AF.Exp, [et], [ps], scale=SCL)
                        last = ki == len(kts) - 1
                        MM(po[0:64, 0:qn], Vg[:, kt, hl * 64:(hl + 1) * 64], et[:, 0:qn], ki == 0, last, [po], [Vg, et])
                        MM(pd[0:64, 0:qn], ones_bf[:, 2, 0:64], et[:, 0:qn], ki == 0, last, [pd], [ones_bf, et])
                        if ki == min(2, len(kts) - 1):
                            run_deferred()

                    def fin_m(po=po, pd=pd, h=h, q0=q0, qn=qn):
                        ACT(dn[0:64, 0:qn], pd[0:64, 0:qn], AF.Ln, [dn], [pd])
                        ACT(dn[0:64, 0:qn], dn[0:64, 0:qn], AF.Exp, [dn], [dn], scale=-1.0)
                        TT(oT[0:64, h, q0:q0 + qn], po[0:64, 0:qn], dn[0:64, 0:qn], ALU.mult, [oT], [po, dn])
                    defer(fin_m)
        run_deferred()
        rot["wide"] = True
        H.close()

        O = Phase()
        osb = O.tile([128, 8, 512], F32, "osb")
        sq = O.tile([128, 8, 512], BF16, "sqo")
        wo_v = w_mla_o[i].rearrange("(h p) n -> p h n", p=64)
        for bl, b in enumerate(blocks):
            bs = slice(bl * 512, (bl + 1) * 512)
            for n in range(8):
                wt, wv = wl3(wo_v[:, :, n * 128:(n + 1) * 128], 64, 16, 128)
                p = PG()
                for h in range(16):
                    MM(p[:, :], wv[0:64, h, :], oT[0:64, h, bs], h == 0, h == 15, [p], [wt, oT])
                post_evac(n, p, osb, sq)
            post_finish(b, mv, 0, osb, sq)
        O.close()
        E.close()

    GROUPS = [dict(t0=0, nt=4, blocks=[0], seqs=[(0, 2), (2, 2)], kind="p"),
              dict(t0=4, nt=8, blocks=[1, 2], seqs=[(0, 8)], kind="s")]

    for l in range(n_layers):
        mv = modulation(l)
        i = l // 2
        if do_mixer:
            for gi in groups_sel:
                try:
                    if l % 2 == 0:
                        even_mixer(l, i, mv, GROUPS[gi])
                    else:
                        odd_mixer(l, i, mv, GROUPS[gi])
                except Bail:
                    for ph_ in reversed(list(open_phases)):
                        ph_.close()
        if do_mlp:
            M = Phase()
            hid = M.tile([128, 32, 512], BF16, "hid")
            hTm = [M.tile([128, 8, 512], BF16, "hTm") for _ in range(2)]
            osb = M.tile([128, 8, 512], F32, "osbm")
            sqm = M.tile([128, 8, 512], BF16, "sqm")
            sqp = M.tile([128, 8, 512], BF16, "sqp")
            pre(0, mv, 1, hTm[0], sqp)
            for b in range(3):
                nxt = (lambda b=b: pre(b + 1, mv, 1, hTm[(b + 1) % 2], sqp)) if b < 2 else None
                mlp(l, b, mv, hTm[b % 2], hid, osb, sqm, nxt)
            M.close()

    Y = Phase()
    stage = [Y.tile([128, 1024], F32, "ystage") for _ in range(2)]
    for t in range(12):
        stg = stage[t % 2]
        for half in range(2):
            p = PG()
            for cc in range(4):
                c = half * 4 + cc
                TR(p[:, cc * 128:(cc + 1) * 128], xT.ap[:, c, t * 128:(t + 1) * 128], ident, [p], [xTb[t // 4], cst])
            EVAC(stg[:, half * 512:(half + 1) * 512], p[:, :], [stg], [p])
        fw.dma("sp", d_out, y_out[t * 128:(t + 1) * 128, :], stg[:], r=[stg])
    fw.barrier(fw.spool)
    fw.flush(final=True, final_dsems=fw.spool)
    Y.es.close()
    fw.es.close()
    print("program: %d instrs; engine counts:" % fw.ninstr, {n: e.cnt for n, e in fw.engs.items()})
    return nc


_CACHE = {}


def _run(inputs, n_layers=4, do_mixer=True, do_mlp=True, trace=False, groups_sel=(0, 1)):
    key = (n_layers, do_mixer, do_mlp, tuple(groups_sel))
    if key not in _CACHE:
        _CACHE[key] = build_program(n_layers, do_mixer, do_mlp, tuple(groups_sel))
    nc = _CACHE[key]
    f = lambda a: np.ascontiguousarray(np.asarray(a, dtype=np.float32))
    consts, cs64, cs32, pm32 = make_consts()
    xp = f(inputs["x_prompt"])
    xs = f(inputs["x_sample"])
    shared = {k: f(inputs[k]) for k in ("w_mod", "b_mod", "g_norm", "w_ff1", "w_ff2", "w_in_ab", "w_gk_f", "b_gk_f",
                                        "w_gk_b", "b_gk_b", "g_gla", "swa_sink", "w_out_ab", "w_mla_down", "g_mla_q",
                                        "g_mla_kv", "w_mla_uq", "w_mla_ukv", "w_mla_o")}
    shared.update(consts=consts, cs64=cs64, cs32=cs32, pm32=pm32)
    c = f(inputs["c"])
    c_ctx = f(inputs["c_ctx"])
    in_maps = []
    for core in range(NCORES):
        sb = core % 2
        m = dict(shared)
        m["xin"] = np.ascontiguousarray(np.concatenate([xp[2 * core], xp[2 * core + 1], xs[sb]], axis=0))
        m["cvec"] = np.ascontiguousarray(np.stack([c_ctx, c[sb]], axis=0))
        m["st_f"] = f(inputs["state_gla_fwd"][sb])
        m["st_b"] = f(inputs["state_gla_bwd"][sb])
        m["cswa_k"] = f(inputs["cache_swa_k"][sb]).reshape(2, 512, 128)
        m["cswa_v"] = f(inputs["cache_swa_v"][sb]).reshape(2, 512, 128)
        m["cm_ckv"] = f(inputs["cache_mla_ckv"][sb])
        m["cm_kr"] = f(inputs["cache_mla_kr"][sb])
        in_maps.append(m)
    res = run_bass_kernel_spmd(nc, in_maps, core_ids=list(range(NCORES)), trace=trace)
    R = res.results
    y_p = np.stack([R[cidx]["y"][s * 256:(s + 1) * 256] for cidx in range(NCORES) for s in range(2)], axis=0)
    y_s = np.stack([R[b]["y"][512:1536] for b in range(2)], axis=0)

    def gather(name, shape):
        return np.concatenate([R[cidx][name] for cidx in range(NCORES)], axis=0).reshape(shape)
    outs = (y_p.astype(np.float32), y_s.astype(np.float32),
            gather("o_stf", (16, 2, 4, 64, 128)), gather("o_stb", (16, 2, 4, 64, 128)),
            gather("o_swk", (16, 2, 256, 2, 64)), gather("o_swv", (16, 2, 256, 2, 64)),
            gather("o_ckv", (16, 2, 256, 256)), gather("o_kr", (16, 2, 256, 32)))
    return outs, res


def kernel(**inputs):
    outs, _ = _run(inputs)
    return outs
```
